# Optimizing a Trainium2 kernel written in Bass

```python
import math
import jax
import jax.numpy as jnp
from jax import lax
import numpy as np

D_MODEL = 1024
BATCH = 16
SEQ = 2048
DEPTH = 4
DEC_BATCH = 8
DEC_SEQ = 64
PAST_LEN = 1024

CHUNK = 64
N_A_LAYERS = DEPTH // 2
N_B_LAYERS = DEPTH - N_A_LAYERS
SSM_WIDTH = D_MODEL
SSM_GROUP_CH = 16
SSM_GROUPS = SSM_WIDTH // SSM_GROUP_CH
SSM_STATE = 64
DT_MIN = 1e-3
DT_MAX = 1e-1
HEAD_DIM = 64
N_HEADS = D_MODEL // HEAD_DIM
N_KV_HEADS = 4
Q_PER_KV = N_HEADS // N_KV_HEADS
ATTN_WIDTH = N_HEADS * HEAD_DIM
KV_WIDTH = N_KV_HEADS * HEAD_DIM
WINDOW = 128
WINDOW_CHUNKS = WINDOW // CHUNK
ROPE_THETA = 10000.0
NEG_INF = -1e30
DEEPNORM_ALPHA = (2.0 * DEPTH) ** 0.25
DEEPNORM_BETA = (8.0 * DEPTH) ** -0.25
LN_EPS = 1e-5

kernel_name = 'yoco_s5_swa_sink_streaming_step'


def layer_norm(x, g, b):
    xf = x.astype(jnp.float32)
    mu = jnp.mean(xf, axis=-1, keepdims=True)
    var = jnp.mean(jnp.square(xf - mu), axis=-1, keepdims=True)
    y = (xf - mu) * lax.rsqrt(var + LN_EPS) * g.astype(jnp.float32) + b.astype(jnp.float32)
    return y.astype(x.dtype)


def ada_params(c, w, b):
    cond = jax.nn.silu(c) @ w + b
    shift, scale, gate = jnp.split(cond, 3, axis=-1)
    return shift[:, None], scale[:, None], gate[:, None]


def rope(x, pos):
    half = HEAD_DIM // 2
    inv_freq = jnp.power(ROPE_THETA, -jnp.arange(half, dtype=jnp.float32) / half)
    ang = pos.astype(jnp.float32)[:, None] * inv_freq[None, :]
    cos = jnp.cos(ang)[None, :, None, :]
    sin = jnp.sin(ang)[None, :, None, :]
    xf = x.astype(jnp.float32)
    x1, x2 = xf[..., :half], xf[..., half:]
    out = jnp.concatenate([x1 * cos - x2 * sin, x2 * cos + x1 * sin], axis=-1)
    return out.astype(x.dtype)


def s5_scan(u, h0, a_re, a_im, b_re, b_im, c_re, c_im, d, log_dt):
    f32 = jnp.float32
    bsz, seq_len, _ = u.shape
    lam = lax.complex(a_re.astype(f32), a_im.astype(f32))
    dt = jnp.exp(log_dt.astype(f32))[:, None]
    a_bar = jnp.exp(lam * dt)
    b_bar = ((a_bar - 1.0) / lam)[..., None] * lax.complex(b_re.astype(f32), b_im.astype(f32))
    uf = u.astype(f32)
    ug = uf.reshape(bsz, seq_len, SSM_GROUPS, SSM_GROUP_CH).astype(jnp.complex64)
    bu = jnp.einsum('gpc,blgc->blgp', b_bar, ug)
    bu = bu.at[:, 0].add(a_bar[None] * h0)
    a_seq = jnp.broadcast_to(a_bar, (1, seq_len) + a_bar.shape)

    def combine(e1, e2):
        a1, b1 = e1
        a2, b2 = e2
        return a1 * a2, a2 * b1 + b2

    _, h = lax.associative_scan(combine, (a_seq, bu), axis=1)
    c_mat = lax.complex(c_re.astype(f32), c_im.astype(f32))
    y = jnp.real(jnp.einsum('gcp,blgp->blgc', c_mat, h)).reshape(bsz, seq_len, SSM_WIDTH)
    y = y + d.astype(f32) * uf
    return y.astype(u.dtype), h[:, -1]


def ssm_branch(hmod, la, h0, p):
    uz = hmod @ p['w_in_a'][la]
    u, z = jnp.split(uz, 2, axis=-1)
    y, h_last = s5_scan(u, h0, p['ssm_a_re'][la], p['ssm_a_im'][la], p['ssm_b_re'][la], p['ssm_b_im'][la],
                        p['ssm_c_re'][la], p['ssm_c_im'][la], p['ssm_d'][la], p['ssm_log_dt'][la])
    g = jax.nn.gelu(y)
    y = g * jax.nn.sigmoid(g @ p['w_glu'][la] + p['b_glu'][la])
    y = y * jax.nn.silu(z)
    return y @ p['w_out_a'][la], h_last


def shared_kv(x, w_kv, pos):
    bsz, seq_len, _ = x.shape
    kv = x @ w_kv
    k, v = jnp.split(kv, 2, axis=-1)
    k = rope(k.reshape(bsz, seq_len, N_KV_HEADS, HEAD_DIM), pos)
    v = v.reshape(bsz, seq_len, N_KV_HEADS, HEAD_DIM)
    return k, v


def sink_attention(q, k, v, sinks, mask):
    s = jnp.einsum('ncqkgd,ncjkd->nckgqj', q, k).astype(jnp.float32) * (HEAD_DIM ** -0.5)
    if mask is not None:
        s = jnp.where(mask, s, NEG_INF)
    sink = sinks.astype(jnp.float32).reshape(1, 1, N_KV_HEADS, Q_PER_KV, 1, 1)
    m = jnp.maximum(jnp.max(s, axis=-1, keepdims=True), sink)
    e = jnp.exp(s - m)
    denom = jnp.sum(e, axis=-1, keepdims=True) + jnp.exp(sink - m)
    probs = (e / denom).astype(v.dtype)
    return jnp.einsum('nckgqj,ncjkd->ncqkgd', probs, v)


def banded_window_attention(q, k, v, sinks):
    bsz, seq_len = q.shape[:2]
    n_chunks = seq_len // CHUNK
    qb = q.reshape(bsz, n_chunks, CHUNK, N_KV_HEADS, Q_PER_KV, HEAD_DIM)
    pad = ((0, 0), (WINDOW, 0), (0, 0), (0, 0))
    kp = jnp.pad(k, pad).reshape(bsz, n_chunks + WINDOW_CHUNKS, CHUNK, N_KV_HEADS, HEAD_DIM)
    vp = jnp.pad(v, pad).reshape(bsz, n_chunks + WINDOW_CHUNKS, CHUNK, N_KV_HEADS, HEAD_DIM)
    kb = jnp.concatenate([kp[:, i:i + n_chunks] for i in range(WINDOW_CHUNKS + 1)], axis=2)
    vb = jnp.concatenate([vp[:, i:i + n_chunks] for i in range(WINDOW_CHUNKS + 1)], axis=2)
    key_pos = (jnp.arange(n_chunks)[:, None] * CHUNK
               + jnp.arange((WINDOW_CHUNKS + 1) * CHUNK)[None, :] - WINDOW)
    mask = (key_pos >= 0)[None, :, None, None, None, :]
    o = sink_attention(qb, kb, vb, sinks, mask)
    return o.reshape(bsz, seq_len, ATTN_WIDTH)


def cached_window_attention(q, k_new, v_new, cache_k, cache_v, sinks):
    bsz, t_new = q.shape[:2]
    qb = q.reshape(bsz, 1, t_new, N_KV_HEADS, Q_PER_KV, HEAD_DIM)
    kb = jnp.concatenate([cache_k, k_new], axis=1)[:, None]
    vb = jnp.concatenate([cache_v, v_new], axis=1)[:, None]
    o = sink_attention(qb, kb, vb, sinks, None)
    return o.reshape(bsz, t_new, ATTN_WIDTH)


def attn_branch(hmod, lb, pos, k_sh, v_sh, cache_k, cache_v, p):
    bsz, seq_len, _ = hmod.shape
    qz = hmod @ p['w_in_b'][lb]
    q, z = jnp.split(qz, 2, axis=-1)
    q = rope(q.reshape(bsz, seq_len, N_HEADS, HEAD_DIM), pos)
    if cache_k is None:
        o = banded_window_attention(q, k_sh, v_sh, p['attn_sinks'][lb])
    else:
        o = cached_window_attention(q, k_sh, v_sh, cache_k, cache_v, p['attn_sinks'][lb])
    o = o * jax.nn.silu(z)
    return o @ p['w_out_b'][lb]


def run_trunk(x, c, pos, h0, cache_k, cache_v, p):
    new_h = []
    k_sh = None
    v_sh = None
    for layer in range(DEPTH):
        shift, scale, gate = ada_params(c, p['w_ada'][layer], p['b_ada'][layer])
        if layer == N_A_LAYERS:
            k_sh, v_sh = shared_kv(x, p['w_kv'], pos)
        hmod = x * (1.0 + scale) + shift
        if layer < N_A_LAYERS:
            out, h_last = ssm_branch(hmod, layer, h0[layer], p)
            new_h.append(h_last)
        else:
            out = attn_branch(hmod, layer - N_A_LAYERS, pos, k_sh, v_sh, cache_k, cache_v, p)
        x = layer_norm(DEEPNORM_ALPHA * x + gate * out, p['ln_g'][layer], p['ln_b'][layer])
    h_stack = jnp.stack(new_h)
    state = jnp.stack([jnp.real(h_stack), jnp.imag(h_stack)], axis=-1)
    return x, state, k_sh, v_sh


def setup_inputs(seed: int = 0) -> dict:
    key = jax.random.key(seed)
    ks = jax.random.split(key, 32)
    f32 = jnp.float32

    def nrm(k, shape, s):
        return jax.random.normal(k, shape, f32) * s

    kv_rows = min(WINDOW, PAST_LEN)
    n_idx = jnp.arange(SSM_STATE, dtype=f32)
    ga = (N_A_LAYERS, SSM_GROUPS, SSM_STATE)
    return {
        'x_prompt': nrm(ks[0], (BATCH, SEQ, D_MODEL), 1.0),
        'x_sample': nrm(ks[1], (DEC_BATCH, DEC_SEQ, D_MODEL), 1.0),
        'state_ssm': nrm(ks[2], (N_A_LAYERS, DEC_BATCH, SSM_GROUPS, SSM_STATE, 2), 0.1),
        'cache_k': nrm(ks[3], (DEC_BATCH, kv_rows, N_KV_HEADS, HEAD_DIM), 1.0),
        'cache_v': nrm(ks[4], (DEC_BATCH, kv_rows, N_KV_HEADS, HEAD_DIM), 1.0),
        'c_prompt': nrm(ks[5], (BATCH, D_MODEL), 1.0),
        'c_sample': nrm(ks[6], (DEC_BATCH, D_MODEL), 1.0),
        'w_ada': nrm(ks[7], (DEPTH, D_MODEL, 3 * D_MODEL), 0.5 * D_MODEL ** -0.5),
        'b_ada': nrm(ks[8], (DEPTH, 3 * D_MODEL), 0.01),
        'ln_g': 1.0 + nrm(ks[9], (DEPTH, D_MODEL), 0.02),
        'ln_b': nrm(ks[10], (DEPTH, D_MODEL), 0.02),
        'w_in_a': nrm(ks[11], (N_A_LAYERS, D_MODEL, 2 * SSM_WIDTH), D_MODEL ** -0.5),
        'ssm_a_re': -0.5 + nrm(ks[12], ga, 0.01),
        'ssm_a_im': math.pi * n_idx + nrm(ks[13], ga, 0.01),
        'ssm_b_re': nrm(ks[14], ga + (SSM_GROUP_CH,), (2.0 * SSM_GROUP_CH) ** -0.5),
        'ssm_b_im': nrm(ks[15], ga + (SSM_GROUP_CH,), (2.0 * SSM_GROUP_CH) ** -0.5),
        'ssm_c_re': nrm(ks[16], (N_A_LAYERS, SSM_GROUPS, SSM_GROUP_CH, SSM_STATE), SSM_STATE ** -0.5),
        'ssm_c_im': nrm(ks[17], (N_A_LAYERS, SSM_GROUPS, SSM_GROUP_CH, SSM_STATE), SSM_STATE ** -0.5),
        'ssm_d': nrm(ks[18], (N_A_LAYERS, SSM_WIDTH), 1.0),
        'ssm_log_dt': jax.random.uniform(ks[19], (N_A_LAYERS, SSM_GROUPS), f32,
                                         math.log(DT_MIN), math.log(DT_MAX)),
        'w_glu': nrm(ks[20], (N_A_LAYERS, SSM_WIDTH, SSM_WIDTH), SSM_WIDTH ** -0.5),
        'b_glu': nrm(ks[21], (N_A_LAYERS, SSM_WIDTH), 0.01),
        'w_out_a': nrm(ks[22], (N_A_LAYERS, SSM_WIDTH, D_MODEL), SSM_WIDTH ** -0.5 * DEEPNORM_BETA),
        'w_kv': nrm(ks[23], (D_MODEL, 2 * KV_WIDTH), D_MODEL ** -0.5),
        'w_in_b': nrm(ks[24], (N_B_LAYERS, D_MODEL, 2 * ATTN_WIDTH), D_MODEL ** -0.5),
        'attn_sinks': nrm(ks[25], (N_B_LAYERS, N_HEADS), 0.5),
        'w_out_b': nrm(ks[26], (N_B_LAYERS, ATTN_WIDTH, D_MODEL), ATTN_WIDTH ** -0.5 * DEEPNORM_BETA),
    }


def reference(x_prompt, x_sample, state_ssm, cache_k, cache_v, c_prompt, c_sample,
              w_ada, b_ada, ln_g, ln_b, w_in_a, ssm_a_re, ssm_a_im, ssm_b_re, ssm_b_im,
              ssm_c_re, ssm_c_im, ssm_d, ssm_log_dt, w_glu, b_glu, w_out_a,
              w_kv, w_in_b, attn_sinks, w_out_b):
    p = dict(w_ada=w_ada, b_ada=b_ada, ln_g=ln_g, ln_b=ln_b, w_in_a=w_in_a,
             ssm_a_re=ssm_a_re, ssm_a_im=ssm_a_im, ssm_b_re=ssm_b_re, ssm_b_im=ssm_b_im,
             ssm_c_re=ssm_c_re, ssm_c_im=ssm_c_im, ssm_d=ssm_d, ssm_log_dt=ssm_log_dt,
             w_glu=w_glu, b_glu=b_glu, w_out_a=w_out_a, w_kv=w_kv, w_in_b=w_in_b,
             attn_sinks=attn_sinks, w_out_b=w_out_b)
    pos_prompt = jnp.arange(x_prompt.shape[1], dtype=jnp.int32)
    pos_sample = PAST_LEN + jnp.arange(x_sample.shape[1], dtype=jnp.int32)
    h0_prompt = jnp.zeros((N_A_LAYERS, x_prompt.shape[0], SSM_GROUPS, SSM_STATE), jnp.complex64)
    h0_sample = lax.complex(state_ssm[..., 0].astype(jnp.float32), state_ssm[..., 1].astype(jnp.float32))
    y_prompt, ssm_p, k_p, v_p = run_trunk(x_prompt, c_prompt, pos_prompt, h0_prompt, None, None, p)
    y_sample, ssm_s, k_s, v_s = run_trunk(x_sample, c_sample, pos_sample, h0_sample, cache_k, cache_v, p)
    rows = min(WINDOW, x_prompt.shape[1])
    return (y_prompt, y_sample, ssm_p, k_p[:, -rows:], v_p[:, -rows:], ssm_s, k_s, v_s)
```

```python
import math
import numpy as np
import concourse.bass as bass
import concourse.mybir as mybir
from concourse.bass_utils import run_bass_kernel_spmd

F32 = mybir.dt.float32
BF16 = mybir.dt.bfloat16
AF = mybir.ActivationFunctionType
ALU = mybir.AluOpType
AX = mybir.AxisListType

D = 1024
SEQ = 2048
DEC = 64
NCORES = 8
ALPHA = (2.0 * 4) ** 0.25
EPS = 1e-5
MAGIC = 12582912.0
TWO_PI = 2.0 * math.pi


class Prog:
    ENGS = ("pe", "act", "dve", "pool", "sp")

    def __init__(self, nc):
        self.nc = nc
        self.ops = {e: [] for e in self.ENGS}
        self.cur = {}
        self.waited = {e: {} for e in self.ENGS}
        self.lastw = {}
        self.reads = {}
        self.pending = {e: ([], []) for e in self.ENGS}
        self.dma_pool = []
        self.dma_rr = 0
        self.swdge_pool = []
        self.swdge_rr = 0
        self.nsem = 0
        self.stack = []

    def new_sem(self):
        cm = self.nc.semaphore("s%d" % self.nsem)
        self.nsem += 1
        s = cm.__enter__()
        self.stack.append(cm)
        return s

    def setup(self, n_dma=32):
        for e in self.ENGS:
            self.cur[e] = [self.new_sem(), 0]
        for _ in range(n_dma):
            self.dma_pool.append([self.new_sem(), 0])
        for _ in range(24):
            self.swdge_pool.append([self.new_sem(), 0])

    def _need(self, eng, ev, waits):
        if ev is None:
            return
        sem, val, src = ev
        if src == eng and eng == "pe":
            return
        k = id(sem)
        if self.waited[eng].get(k, (None, 0))[1] >= val:
            return
        self.waited[eng][k] = (sem, val)
        waits.append((sem, val))

    @staticmethod
    def _best(waits):
        best = {}
        for sem, val in waits:
            k = id(sem)
            if k not in best or best[k][1] < val:
                best[k] = (sem, val)
        return list(best.values())

    def op(self, eng, fn, reads=(), writes=(), mark=True, dma=False):
        al = {"psA_lo": ("psA_q0", "psA_q1"), "psA_hi": ("psA_q2", "psA_q3"), "psB": ("psB0", "psB1")}
        reads = [x for r in reads for x in al.get(r, (r,))]
        writes = [x for r in writes for x in al.get(r, (r,))]
        writes = list(writes) + [r for r in reads if r.startswith("ps")]
        reads = [r for r in reads if not r.startswith("ps")]
        waits = []
        for r in reads:
            self._need(eng, self.lastw.get(r), waits)
        for w in writes:
            self._need(eng, self.lastw.get(w), waits)
            for ev in self.reads.get(w, ()):
                self._need(eng, ev, waits)
        pr, pw = self.pending[eng]
        if dma:
            if eng == "pool":
                slot = self.swdge_pool[self.swdge_rr % len(self.swdge_pool)]
                self.swdge_rr += 1
            else:
                slot = self.dma_pool[self.dma_rr % len(self.dma_pool)]
                self.dma_rr += 1
            if slot[1] > 0:
                self._need(eng, (slot[0], slot[1], "dma"), waits)
            slot[1] += 16
            ev = (slot[0], slot[1], "dma")
            self.ops[eng].append((fn, self._best(waits), (slot[0], 16)))
            rr, ww = list(reads), list(writes)
        elif mark:
            c = self.cur[eng]
            if c[1] >= 2000:
                c = self.cur[eng] = [self.new_sem(), 0]
            c[1] += 1
            ev = (c[0], c[1], eng)
            self.ops[eng].append((fn, self._best(waits), (c[0], 1)))
            rr, ww = list(reads) + pr, list(writes) + pw
            self.pending[eng] = ([], [])
        else:
            self.ops[eng].append((fn, self._best(waits), None))
            pr.extend(reads)
            pw.extend(writes)
            return None
        for r in rr:
            self.reads.setdefault(r, []).append(ev)
        for w in ww:
            self.lastw[w] = ev
            self.reads[w] = []
        return ev

    def barrier(self, engs=None):
        evs = list(self.lastw.values())
        for l in self.reads.values():
            evs.extend(l)
        for eng in (engs or self.ENGS):
            waits = []
            for ev in evs:
                if ev is not None:
                    sem, val, src = ev
                    k = id(sem)
                    if self.waited[eng].get(k, (None, 0))[1] >= val:
                        continue
                    self.waited[eng][k] = (sem, val)
                    waits.append((sem, val))
            self.ops[eng].append((None, self._best(waits), None))
        if engs is None:
            self.lastw = {}
            self.reads = {}

    def replay(self):
        nc = self.nc
        engmap = {"pe": "tensor", "act": "scalar", "dve": "vector", "pool": "gpsimd", "sp": "sync"}
        with nc.Block() as block:
            for e in self.ENGS:
                ops = self.ops[e]

                def body(engine, ops=ops):
                    for fn, waits, inc in ops:
                        for sem, val in waits:
                            engine.wait_ge(sem, val)
                        if fn is None:
                            continue
                        inst = fn(engine)
                        if inc is not None:
                            inst.then_inc(inc[0], inc[1])
                getattr(block, engmap[e])(body)

    def close(self):
        for cm in reversed(self.stack):
            cm.__exit__(None, None, None)


DEBUG = {"stop": None}


class _Stop(Exception):
    pass


def build_nc():
    nc = bass.Bass("TRN2", target_bir_lowering=False)

    def din(name, shape):
        return nc.dram_tensor(name, list(shape), F32, kind="ExternalInput").ap()

    def dout(name, shape):
        return nc.dram_tensor(name, list(shape), F32, kind="ExternalOutput").ap()

    def dscr(name, shape, dt=F32):
        return nc.dram_tensor(name, list(shape), dt).ap()

    x_p = din("x_p", [2, SEQ, D]); x_s = din("x_s", [DEC, D])
    st_in = din("st_in", [2, 64, 64, 2]); ck_in = din("ck_in", [128, 256]); cv_in = din("cv_in", [128, 256])
    c_all = din("c_all", [3, D])
    w_ada = din("w_ada", [4, D, 3 * D]); b_ada = din("b_ada", [4, 3 * D])
    ln_g = din("ln_g", [4, D]); ln_b = din("ln_b", [4, D])
    w_in_a = din("w_in_a", [2, D, 2 * D])
    a_re = din("ssm_a_re", [2, 64, 64]); a_im = din("ssm_a_im", [2, 64, 64])
    b_re = din("ssm_b_re", [2, 64, 64, 16]); b_im = din("ssm_b_im", [2, 64, 64, 16])
    c_re = din("ssm_c_re", [2, 64, 16, 64]); c_im = din("ssm_c_im", [2, 64, 16, 64])
    ssm_d = din("ssm_d", [2, D]); log_dt = din("ssm_log_dt", [2, 64])
    w_glu = din("w_glu", [2, D, D]); b_glu = din("b_glu", [2, D]); w_out_a = din("w_out_a", [2, D, D])
    w_kv = din("w_kv", [D, 512]); w_in_b = din("w_in_b", [2, D, 2 * D])
    sinks = din("attn_sinks", [2, 16]); w_out_b = din("w_out_b", [2, D, D])

    y_p = dout("y_p", [2, SEQ, D]); y_s = dout("y_s", [DEC, D])
    ssm_p = dout("ssm_p", [2, 2, 64, 64, 2]); ck_p = dout("ck_p", [2, 128, 256]); cv_p = dout("cv_p", [2, 128, 256])
    ssm_s = dout("ssm_s", [2, 64, 64, 2]); ck_s = dout("ck_s", [DEC, 256]); cv_s = dout("cv_s", [DEC, 256])

    xa_p = dscr("xa_p", [2, SEQ, D]); xa_s = dscr("xa_s", [DEC, D])
    xb_p = dscr("xb_p", [2, SEQ, D]); xb_s = dscr("xb_s", [DEC, D])
    zscr = dscr("zscr", [16, 128, D], BF16)
    ktscr = dscr("ktscr", [3, 64, 4, SEQ], BF16); vscr = dscr("vscr", [3, SEQ, 256], BF16)
    TXd2 = dscr("TXd", [2, 64, 128, 512], BF16); Gd2 = dscr("Gd", [2, 64, 64, 512], BF16)
    a16d = dscr("a16d", [2, 64, 64], F32)
    ucs_d = dscr("ucs_d", [DEC, D], BF16); gcs_d = dscr("gcs_d", [DEC, D], BF16)

    P = Prog(nc)
    P.setup()
    _cnt = [0]

    scopes = []

    def T(shape, dt=F32, name=None):
        _cnt[0] += 1
        nm = "t%d" % _cnt[0]
        if scopes:
            cm = nc.sbuf_tensor(nm, list(shape), dt)
            t = cm.__enter__()
            scopes[-1].append(cm)
            return t
        return nc.alloc_sbuf_tensor(nm, list(shape), dt)

    def push_scope():
        P.barrier()
        scopes.append([])

    def pop_scope():
        P.barrier()
        for cm in reversed(scopes.pop()):
            cm.__exit__(None, None, None)

    def dve(fn, r=(), w=(), mark=True): return P.op("dve", fn, r, w, mark)
    def act(fn, r=(), w=(), mark=True): return P.op("act", fn, r, w, mark)
    def pool(fn, r=(), w=(), mark=True): return P.op("pool", fn, r, w, mark)
    def pe(fn, r=(), w=(), mark=True): return P.op("pe", fn, r, w, mark)
    def dma(out, in_, r=(), w=(), eng="sp", slow=False):
        if slow:
            return P.op(eng, lambda e: e.dma_start(out=out, in_=in_, allow_slow_non_contiguous=True), r, w, dma=True)
        return P.op(eng, lambda e: e.dma_start(out=out, in_=in_), r, w, dma=True)

    psA = nc.alloc_psum_tensor("psA", [128, 2048], F32)
    psB = nc.alloc_psum_tensor("psB", [128, 1024], F32)
    psT = nc.alloc_psum_tensor("psT", [128, 1024], BF16)
    psQ = nc.alloc_psum_tensor("psQ", [128, 1024], BF16)

    ident = T([128, 128]); identb = T([128, 128], BF16)
    pool(lambda e: e.memset(ident[:], 0.0), w=["ident"])
    pool(lambda e: e.affine_select(out=ident[:], in_=ident[:], pattern=[[-1, 128]], compare_op=ALU.not_equal,
                                   fill=1.0, base=0, channel_multiplier=1), r=["ident"], w=["ident"])
    dve(lambda e: e.tensor_copy(out=identb[:], in_=ident[:]), r=["ident"], w=["identb"])
    dsel = T([128, 256])
    pool(lambda e: e.memset(dsel[:], 0.0), w=["dsel"])
    dve(lambda e: e.tensor_copy(out=dsel[:, 0:128], in_=ident[:]), r=["ident", "dsel"], w=["dsel"])
    tmask = T([128, 16, 16])
    pool(lambda e: e.memset(tmask[:], 1.0), w=["tmask"])
    pool(lambda e: e.affine_select(out=tmask[:], in_=tmask[:], pattern=[[16, 16], [0, 16]], compare_op=ALU.is_ge,
                                   fill=0.0, base=15, channel_multiplier=-1), r=["tmask"], w=["tmask"])
    mLa = T([1, 128], BF16); mLb = T([1, 128], BF16); mL1 = T([1, 128], BF16)
    mRa = T([1, 512], BF16); mRb = T([1, 512], BF16); mR0 = T([1, 512], BF16)
    pool(lambda e: e.memset(mLa[:], 0.0), w=["mk"]); pool(lambda e: e.memset(mLa[0:1, 0:64], 1.0), r=["mk"], w=["mk"])
    pool(lambda e: e.memset(mLb[:], 0.0), r=["mk"], w=["mk"]); pool(lambda e: e.memset(mLb[0:1, 64:128], 1.0), r=["mk"], w=["mk"])
    pool(lambda e: e.memset(mL1[:], 1.0), r=["mk"], w=["mk"])
    for hh2 in range(2):
        pool(lambda e, hh2=hh2: e.memset(mRa[0:1, hh2 * 256:hh2 * 256 + 192], 0.0), r=["mk"], w=["mk"])
        pool(lambda e, hh2=hh2: e.memset(mRa[0:1, hh2 * 256 + 192:hh2 * 256 + 256], -1e30), r=["mk"], w=["mk"])
        pool(lambda e, hh2=hh2: e.memset(mRb[0:1, hh2 * 256 + 64:hh2 * 256 + 256], 0.0), r=["mk"], w=["mk"])
        pool(lambda e, hh2=hh2: e.memset(mRb[0:1, hh2 * 256:hh2 * 256 + 64], -1e30), r=["mk"], w=["mk"])
        pool(lambda e, hh2=hh2: e.memset(mR0[0:1, hh2 * 256 + 128:hh2 * 256 + 256], 0.0), r=["mk"], w=["mk"])
        pool(lambda e, hh2=hh2: e.memset(mR0[0:1, hh2 * 256:hh2 * 256 + 128], -1e30), r=["mk"], w=["mk"])
    maskG = "maskG"; mask0 = "mask0"
    epsb = T([128, 1])
    pool(lambda e: e.memset(epsb[:], EPS), w=["epsb"])

    def range_sin(out, ang_over_2pi, shape_tmp, r, w, nparts, offs=0.0):
        t1 = shape_tmp[0]; t2 = shape_tmp[1]
        dve(lambda e: e.tensor_scalar(out=t1, in0=ang_over_2pi, scalar1=offs, scalar2=MAGIC, op0=ALU.add, op1=ALU.add),
            r=r, w=["rs_t1"])
        dve(lambda e: e.tensor_scalar(out=t1, in0=t1, scalar1=MAGIC, scalar2=None, op0=ALU.subtract), r=["rs_t1"], w=["rs_t1"])
        dve(lambda e: e.scalar_tensor_tensor(out=t2, in0=ang_over_2pi, scalar=offs, in1=t1, op0=ALU.add, op1=ALU.subtract),
            r=r + ["rs_t1"], w=["rs_t2"])
        act(lambda e: e.activation(out=out, in_=t2, func=AF.Sin, scale=TWO_PI), r=["rs_t2"], w=w)

    cosT = T([128, 17, 32]); sinT = T([128, 17, 32])
    push_scope()
    posf = T([128, 17]); invf = T([128, 32]); rang = T([128, 17, 32]); rt1 = T([128, 17, 32]); rt2 = T([128, 17, 32])
    pool(lambda e: e.iota(posf[:, 0:16], pattern=[[128, 16]], base=0, channel_multiplier=1,
                          allow_small_or_imprecise_dtypes=True), w=["posf"])
    pool(lambda e: e.iota(posf[:, 16:17], pattern=[[0, 1]], base=1024, channel_multiplier=1,
                          allow_small_or_imprecise_dtypes=True), r=["posf"], w=["posf"])
    for jj in range(32):
        pool(lambda e, jj=jj: e.memset(invf[:, jj:jj + 1], float(np.float32(np.power(np.float32(10000.0), np.float32(-jj / 32.0))))),
             r=["invf"], w=["invf"])
    dve(lambda e: e.tensor_tensor(out=rang[:], in0=posf[:, :].unsqueeze(2).broadcast_to([128, 17, 32]),
                                  in1=invf[:, :].unsqueeze(1).broadcast_to([128, 17, 32]), op=ALU.mult),
        r=["posf", "invf"], w=["rang"])
    dve(lambda e: e.tensor_scalar(out=rang[:], in0=rang[:], scalar1=1.0 / TWO_PI, scalar2=None, op0=ALU.mult), r=["rang"], w=["rang"])
    range_sin(sinT[:], rang[:], (rt1[:], rt2[:]), ["rang"], ["sinT"], 128, 0.0)
    range_sin(cosT[:], rang[:], (rt1[:], rt2[:]), ["rang"], ["cosT"], 128, 0.25)
    pop_scope()

    lngb = T([128, D]); lnbb = T([128, D])
    condPs = [T([128, 24, 4]) for _ in range(4)]; gateb = T([128, D])
    gate_d4 = dscr("gate_d", [4, 3, D])
    CUR = {}

    def load_gate(s):
        dma(gateb[:], gate_d4[CUR["l"], s].partition_broadcast(128), r=["gate_d"], w=["gateb"])

    ADA = {}

    def ada_alloc():
        ADA["cT"] = T([128, 3, 8]); ADA["cTf"] = T([128, 8, 4])
        ADA["wts"] = [T([128, 1536]) for _ in range(3)]
        ADA["bbT"] = T([128, 24]); ADA["zt"] = T([1, 128])

    def ada_all():
        cT = ADA["cT"]; cTf = ADA["cTf"]; wts = ADA["wts"]; bbT = ADA["bbT"]; zt = ADA["zt"]
        pso = psA[:, 1024:1120]
        pool(lambda e: e.memset(zt[:], 0.0), w=["zt"])
        for s in range(3):
            dma(cT[:, s, :], c_all[s].rearrange("(c p) -> p c", p=128), w=["cT"], slow=True)
        act(lambda e: e.activation(out=cT[:], in_=cT[:], func=AF.Silu), r=["cT"], w=["cT"])
        dve(lambda e: e.tensor_copy(out=cTf[:, :, 0:3], in_=cT[:, :, :].rearrange("p s c -> p c s")), r=["cT"], w=["cTf"])
        chunks = [(l, c, hf) for l in range(4) for c in range(8) for hf in range(2)]

        def load(i):
            l, c, hf = chunks[i]
            dma(wts[i % 3][:, :], w_ada[l, c * 128:(c + 1) * 128, hf * 1536:(hf + 1) * 1536], w=["wada%d" % (i % 3)],
                eng=("sp" if i % 2 == 0 else "pool"))
        load(0); load(1)
        for i, (l, c, hf) in enumerate(chunks):
            condP = condPs[l]
            if i + 2 < len(chunks):
                load(i + 2)
            yield
            wt = wts[i % 3]; wn = "wada%d" % (i % 3)
            if c == 0 and hf == 0:
                dma(bbT[:], b_ada[l].rearrange("(c p) -> p c", p=128), w=["bbT"], slow=True)
                pe(lambda e: e.matmul(pso, lhsT=zt[0:1, 0:128], rhs=zt[0:1, 0:96], start=True, stop=False, skip_group_check=True),
                   r=["zt"], w=["psA_hi"], mark=False)
            for cc in range(12):
                ch = hf * 12 + cc
                pe(lambda e, c=c, ch=ch, cc=cc, wt=wt: e.matmul(pso[:, ch * 4:ch * 4 + 3], lhsT=wt[:, cc * 128:(cc + 1) * 128],
                                                                rhs=cTf[:, c, 0:3], start=False, stop=(c == 7), skip_group_check=True),
                   r=[wn, "cTf"], w=["psA_hi"], mark=(cc == 11))
            if c == 7 and hf == 1:
                dve(lambda e, condP=condP: e.tensor_tensor(out=condP[:, :, 0:3], in0=pso.rearrange("p (a b) -> p a b", b=4)[:, :, 0:3],
                                                           in1=bbT[:, :].unsqueeze(2).broadcast_to([128, 24, 3]), op=ALU.add),
                    r=["psA_hi", "bbT"], w=["adaP%d" % l])
                dve(lambda e, condP=condP: e.tensor_scalar(out=condP[:, 8:16, 0:3], in0=condP[:, 8:16, 0:3], scalar1=1.0, scalar2=None, op0=ALU.add),
                    r=["adaP%d" % l], w=["adaP%d" % l])
                for s in range(3):
                    dma(gate_d4[l, s].rearrange("(c p) -> p c", p=128), condP[:, 16:24, s], r=["adaP%d" % l], w=["gate_d"], slow=True, eng="pool")
        yield

    def layer_ln(l):
        CUR["l"] = l; CUR["condP"] = condPs[l]
        dma(lngb[:], ln_g[l].partition_broadcast(128), w=["lngb"])
        dma(lnbb[:], ln_b[l].partition_broadcast(128), w=["lnbb"])

    Win = T([128, 8, 2048], BF16); Wg = T([128, 8, D], BF16); Wo = T([128, 8, D], BF16)

    def load_w(dst, src, name, ncols):
        v = src.rearrange("(c p) n -> p c n", p=128)
        for c in range(8):
            dma(dst[:, c, :], v[:, c, :], w=[name], eng="pool")

    xt1 = T([128, D]); xt = [xt1, xt1]
    xbt = T([128, D], BF16); hmT = T([128, 8, 128], BF16)
    t2 = T([128, D]); t1 = T([128, D]); junk = t1
    s1 = T([128, 16])
    szt = T([128, D]); sg = szt; szb = T([128, D], BF16)
    ybt = T([128, D], BF16); yT = T([128, 8, 128], BF16)

    def run_streams(tasks, ns, extra=(), stagger=0):
        free = list(range(ns))
        active = [(g, None) for g in extra]
        it = iter(tasks)
        rnd = 0
        started = 0
        exhausted = False
        while True:
            while free and not exhausted:
                if started < ns and rnd < started * stagger:
                    break
                try:
                    f = next(it)
                except StopIteration:
                    exhausted = True
                    break
                sl = free.pop(0)
                active.append((f(sl), sl))
                started += 1
            if not active:
                if exhausted or not free:
                    break
                rnd += 1
                continue
            for item in list(active):
                try:
                    next(item[0])
                except StopIteration:
                    active.remove(item)
                    if item[1] is not None:
                        free.append(item[1])
            rnd += 1

    def transpose_to(dstT, src_b, nt, rname, wname):
        for c in range(8):
            pe(lambda e, c=c: e.transpose(out=psT[:, c * 128:c * 128 + nt], in_=src_b[:nt, c * 128:(c + 1) * 128],
                                          identity=identb[:nt, :nt]), r=[rname, "identb"], w=["psT"], mark=(c == 7))
        dve(lambda e: e.tensor_copy(out=dstT[:, :, :nt], in_=psT[:, :].rearrange("p (c t) -> p c t", t=128)[:, :, :nt]),
            r=["psT"], w=[wname])

    def load_mod(src_ap, nt, slot, s, want_plain, X=None, xn="xt0"):
        if X is None:
            X = xt[slot]
            dma(X[:nt, :], src_ap, w=[xn])
        act(lambda e: e.copy(out=xbt[:nt, :], in_=X[:nt, :]), r=[xn], w=["xbt"])
        for c in range(8):
            pe(lambda e, c=c: e.transpose(out=psT[:, c * 128:c * 128 + nt], in_=xbt[:nt, c * 128:(c + 1) * 128],
                                          identity=identb[:nt, :nt]), r=["xbt", "identb"], w=["psT"], mark=(c == 7))
        for c in range(8):
            act(lambda e, c=c, cp=CUR["condP"]: e.activation(out=hmT[:, c, :nt], in_=psT[:, c * 128:c * 128 + nt], func=AF.Identity,
                                                             bias=cp[:, c, s:s + 1], scale=cp[:, 8 + c, s:s + 1]),
                r=["psT"], w=["hmT"], mark=(c == 7))
        if want_plain:
            dve(lambda e: e.tensor_copy(out=AT["xTp"][:, :, :nt], in_=psT[:, :].rearrange("p (c t) -> p c t", t=128)[:, :, :nt]),
                r=["psT"], w=["xTp"])

    def proj(ps_out, lhsT_t, rname, W, wname, nt, ncol, psname, bias_row=None):
        for n in range(ncol // 512):
            if bias_row is not None:
                pe(lambda e, n=n: e.matmul(ps_out[:nt, n * 512:(n + 1) * 512], lhsT=onesb[0:1, :nt], rhs=bias_row[0:1, n * 512:(n + 1) * 512],
                                           start=True, stop=False), r=["onesb", "bglur"],
                   w=(list(psname) if isinstance(psname, (list, tuple)) else [psname]), mark=False)
            for c in range(8):
                pe(lambda e, n=n, c=c: e.matmul(ps_out[:nt, n * 512:(n + 1) * 512], lhsT=lhsT_t[:, c, :nt],
                                                rhs=W[:, c, n * 512:(n + 1) * 512], start=(c == 0 and bias_row is None), stop=(c == 7)),
                   r=[rname, wname], w=(list(psname) if isinstance(psname, (list, tuple)) else [psname]),
                   mark=(c == 7 and n == ncol // 512 - 1))

    def resid_ln(ps_o, psname, slot, s, nt, dst_ap, wregion, X=None, xn="xt0"):
        if X is None:
            X = xt[slot]
        XO = t1; xon = "t1"
        dve(lambda e: e.tensor_tensor(out=t1[:nt, :], in0=ps_o, in1=gateb[:nt, :], op=ALU.mult), r=[psname, "gateb"], w=["t1"])
        dve(lambda e: e.scalar_tensor_tensor(out=t2[:nt, :], in0=X[:nt, :], scalar=ALPHA, in1=t1[:nt, :], op0=ALU.mult, op1=ALU.add),
            r=[xn, "t1"], w=["t2"])
        dve(lambda e: e.bn_stats(out=s1[:nt, 0:6], in_=t2[:nt, 0:512]), r=["t2"], w=["s1a"])
        dve(lambda e: e.bn_stats(out=s1[:nt, 6:12], in_=t2[:nt, 512:1024]), r=["t2", "s1a"], w=["s1a"])
        dve(lambda e: e.bn_aggr(out=s1[:nt, 12:14], in_=s1[:nt, 0:12]), r=["s1a"], w=["s1b"])
        act(lambda e: e.activation(out=s1[:nt, 14:15], in_=s1[:nt, 13:14], func=AF.Sqrt, bias=epsb[:nt, 0:1]), r=["s1b", "epsb"], w=["s1c"])
        dve(lambda e: e.reciprocal(out=s1[:nt, 15:16], in_=s1[:nt, 14:15]), r=["s1c"], w=["s1d"])
        dve(lambda e: e.scalar_tensor_tensor(out=XO[:nt, :], in0=t2[:nt, :], scalar=s1[:nt, 12:13], in1=lngb[:nt, :], op0=ALU.subtract, op1=ALU.mult),
            r=["t2", "s1b", "lngb"], w=[xon])
        dve(lambda e: e.scalar_tensor_tensor(out=XO[:nt, :], in0=XO[:nt, :], scalar=s1[:nt, 15:16], in1=lnbb[:nt, :], op0=ALU.mult, op1=ALU.add),
            r=[xon, "s1d", "lnbb"], w=[xon])
        dma(dst_ap, XO[:nt, :], r=[xon], w=[wregion], eng="pool")

    A1s = [[T([128, 32, 2]) for _ in range(3)] for _ in range(2)]; A2s = [[T([128, 32, 2]) for _ in range(3)] for _ in range(2)]
    apw = T([128, 2, 32]); apt = T([128, 2, 32])
    GEN = {}
    bglur = T([1, D], BF16); onesb = T([1, 128], BF16)
    pool(lambda e: e.memset(onesb[:], 1.0), w=["onesb"])
    S5 = {}
    DBG = {}

    def alloc_s5_seq():
        S5["Uc"] = T([128, 64, 16, 16], BF16)
        S5["UTs"] = [T([128, 2, 128], BF16) for _ in range(4)]
        S5["TXs"] = [T([128, 512], BF16) for _ in range(4)]
        S5["Gs"] = [T([128, 512], BF16) for _ in range(4)]
        S5["SH"] = T([128, 32, 2, 130])
        S5["Hbs"] = [T([128, 2, 128], BF16) for _ in range(4)]
        S5["sc1"] = T([128, 32, 2]); S5["sc2"] = T([128, 32, 2])
        S5["gt"] = T([128, D], BF16)
        S5["h3b"] = T([128, D], BF16); S5["h4b"] = T([128, D], BF16)
        S5["s1"] = [T([128, 16]) for _ in range(4)]

    GB = 4

    def gen_alloc():
        GEN['are'] = T([64, 64])
        GEN['aim'] = T([64, 64])
        GEN['dtt'] = T([64, 64])
        GEN['rho'] = T([64, 64])
        GEN['th'] = T([64, 64])
        GEN['bre_t'] = T([64, 64, 16])
        GEN['bim_t'] = T([64, 64, 16])
        GEN['craw'] = T([128, 2, 8, 64])
        GEN['cre_t'] = T([64, 64, 16])
        GEN['cim_t'] = T([64, 64, 16])
        GEN['dcol'] = T([128, 64])
        GEN['ev'] = T([64, 40])
        GEN['a16'] = T([64, 2, 64])
        GEN['a16p'] = T([128, 2, 32])
        GEN['Pr'] = T([64, GB, 40])
        GEN['Pi'] = T([64, GB, 40])
        GEN['pa'] = T([64, GB, 40])
        GEN['pt1'] = T([64, GB, 40])
        GEN['pt2'] = T([64, GB, 40])
        GEN['pm'] = T([64, GB, 40])
        GEN['q1'] = T([64, GB])
        GEN['q2'] = T([64, GB])
        GEN['q3'] = T([64, GB])
        GEN['wr'] = T([64, GB])
        GEN['wi'] = T([64, GB])
        GEN['bbr'] = T([64, GB, 16])
        GEN['bbi'] = T([64, GB, 16])
        GEN['tb1'] = T([64, GB, 16])
        GEN['Fr'] = T([64, GB, 23, 16])
        GEN['Fi'] = T([64, GB, 23, 16])
        GEN['Ft'] = T([64, GB, 23, 16])
        GEN['Gr'] = T([64, GB, 17, 16])
        GEN['Gi'] = T([64, GB, 17, 16])
        GEN['Gt'] = T([64, GB, 17, 16])
        GEN['txs'] = [T([128, 512], BF16) for i in range(2)]
        GEN['gsb'] = [T([64, 2, 256], BF16) for i in range(2)]
        GEN['tbf'] = T([128, 256])

        for nm in ('en', 'er', 'ep', 'e2', 'et'):
            GEN[nm] = T([64, 64])

    def gen_run(la):
        are = GEN['are']
        aim = GEN['aim']
        dtt = GEN['dtt']
        rho = GEN['rho']
        th = GEN['th']
        bre_t = GEN['bre_t']
        bim_t = GEN['bim_t']
        craw = GEN['craw']
        cre_t = GEN['cre_t']
        cim_t = GEN['cim_t']
        dcol = GEN['dcol']
        ev = GEN['ev']
        a16 = GEN['a16']
        a16p = GEN['a16p']
        Pr = GEN['Pr']
        Pi = GEN['Pi']
        pa = GEN['pa']
        pt1 = GEN['pt1']
        pt2 = GEN['pt2']
        pm = GEN['pm']
        q1 = GEN['q1']
        q2 = GEN['q2']
        q3 = GEN['q3']
        wr = GEN['wr']
        wi = GEN['wi']
        bbr = GEN['bbr']
        bbi = GEN['bbi']
        tb1 = GEN['tb1']
        Fr = GEN['Fr']
        Fi = GEN['Fi']
        Ft = GEN['Ft']
        Gr = GEN['Gr']
        Gi = GEN['Gi']
        Gt = GEN['Gt']
        txs = GEN['txs']
        gsb = GEN['gsb']
        tbf = GEN['tbf']
        TXd = TXd2[la]; Gd = Gd2[la]
        R = "g_"
        dma(are[:], a_re[la].rearrange("g p -> p g"), w=[R + "are"], slow=True)
        dma(aim[:], a_im[la].rearrange("g p -> p g"), w=[R + "aim"], slow=True)
        dma(dtt[:], log_dt[la].partition_broadcast(64), w=[R + "dtt"])
        dma(bre_t[:], b_re[la].rearrange("g p c -> p g c"), w=[R + "bre"])
        dma(bim_t[:], b_im[la].rearrange("g p c -> p g c"), w=[R + "bim"])
        dma(craw[:, 0, :, :], c_re[la].rearrange("(b g) c p -> (g c) b p", g=8), w=[R + "craw"])
        dma(craw[:, 1, :, :], c_im[la].rearrange("(b g) c p -> (g c) b p", g=8), w=[R + "craw"])
        for il in range(8):
            dma(dcol[16 * il:16 * il + 16, :], ssm_d[la].rearrange("(g c) -> c g", c=16), w=[R + "dcol"], slow=True)
        for ri, dst in ((0, cre_t), (1, cim_t)):
            for b in range(8):
                pe(lambda e, ri=ri, b=b: e.transpose(out=psA[0:64, b * 128:(b + 1) * 128], in_=craw[:, ri, b, :], identity=ident[:, :]),
                   r=[R + "craw", "ident"], w=["psA_lo"], mark=(b == 7))
            dve(lambda e, dst=dst: e.tensor_copy(out=dst[:, :, :], in_=psA[0:64, 0:1024].rearrange("p (g c) -> p g c", c=16)),
                r=["psA_lo"], w=[R + "c"])
        en = GEN['en']; er = GEN['er']; ep = GEN['ep']; e2 = GEN['e2']; et = GEN['et']
        dve(lambda e: e.tensor_scalar(out=en[:], in0=dtt[:], scalar1=1.0 / math.log(2.0), scalar2=MAGIC, op0=ALU.mult, op1=ALU.add),
            r=[R + "dtt"], w=[R + "en"])
        dve(lambda e: e.tensor_scalar(out=en[:], in0=en[:], scalar1=MAGIC, scalar2=None, op0=ALU.subtract), r=[R + "en"], w=[R + "en"])
        dve(lambda e: e.scalar_tensor_tensor(out=er[:], in0=en[:], scalar=-0.693359375, in1=dtt[:], op0=ALU.mult, op1=ALU.add),
            r=[R + "en", R + "dtt"], w=[R + "er"])
        dve(lambda e: e.scalar_tensor_tensor(out=er[:], in0=en[:], scalar=2.12194440e-4, in1=er[:], op0=ALU.mult, op1=ALU.add),
            r=[R + "en", R + "er"], w=[R + "er"])
        dve(lambda e: e.tensor_scalar(out=ep[:], in0=er[:], scalar1=1.0 / 5040.0, scalar2=None, op0=ALU.mult), r=[R + "er"], w=[R + "ep"])
        for ck in (1.0 / 720.0, 1.0 / 120.0, 1.0 / 24.0, 1.0 / 6.0, 0.5, 1.0):
            dve(lambda e, ck=ck: e.scalar_tensor_tensor(out=ep[:], in0=ep[:], scalar=ck, in1=er[:], op0=ALU.add, op1=ALU.mult),
                r=[R + "ep", R + "er"], w=[R + "ep"])
        dve(lambda e: e.tensor_scalar(out=ep[:], in0=ep[:], scalar1=1.0, scalar2=None, op0=ALU.add), r=[R + "ep"], w=[R + "ep"])
        pool(lambda e: e.memset(e2[:], 0.0), w=[R + "e2"])
        for kk in range(-16, 5):
            dve(lambda e, kk=kk: e.tensor_scalar(out=et[:], in0=en[:], scalar1=float(kk), scalar2=float(2.0 ** kk), op0=ALU.is_equal, op1=ALU.mult),
                r=[R + "en"], w=[R + "et"])
            dve(lambda e: e.tensor_tensor(out=e2[:], in0=e2[:], in1=et[:], op=ALU.add), r=[R + "e2", R + "et"], w=[R + "e2"])
        dve(lambda e: e.tensor_tensor(out=dtt[:], in0=ep[:], in1=e2[:], op=ALU.mult), r=[R + "ep", R + "e2"], w=[R + "dtt"])
        dve(lambda e: e.tensor_tensor(out=rho[:], in0=are[:], in1=dtt[:], op=ALU.mult), r=[R + "are", R + "dtt"], w=[R + "rho"])
        dve(lambda e: e.tensor_tensor(out=th[:], in0=aim[:], in1=dtt[:], op=ALU.mult), r=[R + "aim", R + "dtt"], w=[R + "th"])
        dve(lambda e: e.tensor_scalar(out=th[:], in0=th[:], scalar1=1.0 / TWO_PI, scalar2=None, op0=ALU.mult), r=[R + "th"], w=[R + "th"])
        pool(lambda e: e.iota(ev[:, 0:23], pattern=[[-1, 23]], base=15, channel_multiplier=0, allow_small_or_imprecise_dtypes=True), w=[R + "ev"])
        pool(lambda e: e.iota(ev[:, 23:40], pattern=[[1, 17]], base=0, channel_multiplier=0, allow_small_or_imprecise_dtypes=True), r=[R + "ev"], w=[R + "ev"])
        evb = ev[:, :].unsqueeze(1).broadcast_to([64, GB, 40])
        for gb in range(64 // GB):
            g0 = gb * GB
            gs = slice(g0, g0 + GB)
            dve(lambda e, gs=gs: e.tensor_tensor(out=pa[:], in0=th[:, gs].unsqueeze(2).broadcast_to([64, GB, 40]), in1=evb, op=ALU.mult),
                r=[R + "th", R + "ev"], w=[R + "pa"])
            dve(lambda e, gs=gs: e.tensor_tensor(out=pm[:], in0=rho[:, gs].unsqueeze(2).broadcast_to([64, GB, 40]), in1=evb, op=ALU.mult),
                r=[R + "rho", R + "ev"], w=[R + "pm"])
            act(lambda e: e.activation(out=pm[:], in_=pm[:], func=AF.Exp), r=[R + "pm"], w=[R + "pm"])
            range_sin(Pi[:], pa[:], (pt1[:], pt2[:]), [R + "pa"], [R + "Pi"], 64, 0.0)
            range_sin(Pr[:], pa[:], (pt1[:], pt2[:]), [R + "pa"], [R + "Pr"], 64, 0.25)
            dve(lambda e: e.tensor_tensor(out=Pr[:], in0=Pr[:], in1=pm[:], op=ALU.mult), r=[R + "Pr", R + "pm"], w=[R + "Pr"])
            dve(lambda e: e.tensor_tensor(out=Pi[:], in0=Pi[:], in1=pm[:], op=ALU.mult), r=[R + "Pi", R + "pm"], w=[R + "Pi"])
            abr = Pr[:, :, 24]; abi = Pi[:, :, 24]
            ar_ = are[:, gs]; ai_ = aim[:, gs]
            dve(lambda e, abr=abr: e.tensor_scalar(out=q1[:], in0=abr, scalar1=-1.0, scalar2=None, op0=ALU.add), r=[R + "Pr"], w=[R + "q1"])
            dve(lambda e, ar_=ar_: e.tensor_tensor(out=q2[:], in0=ar_, in1=ar_, op=ALU.mult), r=[R + "are"], w=[R + "q2"])
            dve(lambda e, ai_=ai_: e.tensor_tensor(out=q3[:], in0=ai_, in1=ai_, op=ALU.mult), r=[R + "aim"], w=[R + "q3"])
            dve(lambda e: e.tensor_tensor(out=q2[:], in0=q2[:], in1=q3[:], op=ALU.add), r=[R + "q2", R + "q3"], w=[R + "q2"])
            dve(lambda e: e.reciprocal(out=q2[:], in_=q2[:]), r=[R + "q2"], w=[R + "q2"])
            dve(lambda e, ar_=ar_: e.tensor_tensor(out=wr[:], in0=q1[:], in1=ar_, op=ALU.mult), r=[R + "q1", R + "are"], w=[R + "wr"])
            dve(lambda e, abi=abi, ai_=ai_: e.tensor_tensor(out=q3[:], in0=abi, in1=ai_, op=ALU.mult), r=[R + "Pi", R + "aim"], w=[R + "q3"])
            dve(lambda e: e.tensor_tensor(out=wr[:], in0=wr[:], in1=q3[:], op=ALU.add), r=[R + "wr", R + "q3"], w=[R + "wr"])
            dve(lambda e: e.tensor_tensor(out=wr[:], in0=wr[:], in1=q2[:], op=ALU.mult), r=[R + "wr", R + "q2"], w=[R + "wr"])
            dve(lambda e, abi=abi, ar_=ar_: e.tensor_tensor(out=wi[:], in0=abi, in1=ar_, op=ALU.mult), r=[R + "Pi", R + "are"], w=[R + "wi"])
            dve(lambda e, ai_=ai_: e.tensor_tensor(out=q3[:], in0=q1[:], in1=ai_, op=ALU.mult), r=[R + "q1", R + "aim"], w=[R + "q3"])
            dve(lambda e: e.tensor_tensor(out=wi[:], in0=wi[:], in1=q3[:], op=ALU.subtract), r=[R + "wi", R + "q3"], w=[R + "wi"])
            dve(lambda e: e.tensor_tensor(out=wi[:], in0=wi[:], in1=q2[:], op=ALU.mult), r=[R + "wi", R + "q2"], w=[R + "wi"])
            wrb = wr[:, :].unsqueeze(2).broadcast_to([64, GB, 16]); wib = wi[:, :].unsqueeze(2).broadcast_to([64, GB, 16])
            brs = bre_t[:, gs, :]; bis = bim_t[:, gs, :]
            dve(lambda e, brs=brs: e.tensor_tensor(out=bbr[:], in0=brs, in1=wrb, op=ALU.mult), r=[R + "bre", R + "wr"], w=[R + "bbr"])
            dve(lambda e, bis=bis: e.tensor_tensor(out=tb1[:], in0=bis, in1=wib, op=ALU.mult), r=[R + "bim", R + "wi"], w=[R + "tb1"])
            dve(lambda e: e.tensor_tensor(out=bbr[:], in0=bbr[:], in1=tb1[:], op=ALU.subtract), r=[R + "bbr", R + "tb1"], w=[R + "bbr"])
            dve(lambda e, bis=bis: e.tensor_tensor(out=bbi[:], in0=bis, in1=wrb, op=ALU.mult), r=[R + "bim", R + "wr"], w=[R + "bbi"])
            dve(lambda e, brs=brs: e.tensor_tensor(out=tb1[:], in0=brs, in1=wib, op=ALU.mult), r=[R + "bre", R + "wi"], w=[R + "tb1"])
            dve(lambda e: e.tensor_tensor(out=bbi[:], in0=bbi[:], in1=tb1[:], op=ALU.add), r=[R + "bbi", R + "tb1"], w=[R + "bbi"])
            dve(lambda e, gs=gs: e.tensor_copy(out=a16[:, 0, gs], in_=Pr[:, :, 39]), r=[R + "Pr", R + "a16"], w=[R + "a16"])
            dve(lambda e, gs=gs: e.tensor_copy(out=a16[:, 1, gs], in_=Pi[:, :, 39]), r=[R + "Pi", R + "a16"], w=[R + "a16"])
            PrF = Pr[:, :, 0:23].unsqueeze(3).broadcast_to([64, GB, 23, 16])
            PiF = Pi[:, :, 0:23].unsqueeze(3).broadcast_to([64, GB, 23, 16])
            bbrB = bbr[:, :, :].unsqueeze(2).broadcast_to([64, GB, 23, 16])
            bbiB = bbi[:, :, :].unsqueeze(2).broadcast_to([64, GB, 23, 16])
            dve(lambda e: e.tensor_tensor(out=Fr[:], in0=PrF, in1=bbrB, op=ALU.mult), r=[R + "Pr", R + "bbr"], w=[R + "Fr"])
            dve(lambda e: e.tensor_tensor(out=Ft[:], in0=PiF, in1=bbiB, op=ALU.mult), r=[R + "Pi", R + "bbi"], w=[R + "Ft"])
            dve(lambda e: e.tensor_tensor(out=Fr[:], in0=Fr[:], in1=Ft[:], op=ALU.subtract), r=[R + "Fr", R + "Ft"], w=[R + "Fr"])
            dve(lambda e: e.tensor_tensor(out=Fi[:], in0=PrF, in1=bbiB, op=ALU.mult), r=[R + "Pr", R + "bbi"], w=[R + "Fi"])
            dve(lambda e: e.tensor_tensor(out=Ft[:], in0=PiF, in1=bbrB, op=ALU.mult), r=[R + "Pi", R + "bbr"], w=[R + "Ft"])
            dve(lambda e: e.tensor_tensor(out=Fi[:], in0=Fi[:], in1=Ft[:], op=ALU.add), r=[R + "Fi", R + "Ft"], w=[R + "Fi"])
            PrG = Pr[:, :, 23:40].unsqueeze(3).broadcast_to([64, GB, 17, 16])
            PiG = Pi[:, :, 23:40].unsqueeze(3).broadcast_to([64, GB, 17, 16])
            crB = cre_t[:, gs, :].unsqueeze(2).broadcast_to([64, GB, 17, 16])
            ciB = cim_t[:, gs, :].unsqueeze(2).broadcast_to([64, GB, 17, 16])
            dve(lambda e, b=crB: e.tensor_tensor(out=Gr[:], in0=PrG, in1=b, op=ALU.mult), r=[R + "Pr", R + "c"], w=[R + "Gr"])
            dve(lambda e, b=ciB: e.tensor_tensor(out=Gt[:], in0=PiG, in1=b, op=ALU.mult), r=[R + "Pi", R + "c"], w=[R + "Gt"])
            dve(lambda e: e.tensor_tensor(out=Gr[:], in0=Gr[:], in1=Gt[:], op=ALU.subtract), r=[R + "Gr", R + "Gt"], w=[R + "Gr"])
            dve(lambda e, b=crB: e.tensor_tensor(out=Gi[:], in0=PiG, in1=b, op=ALU.mult), r=[R + "Pi", R + "c"], w=[R + "Gi"])
            dve(lambda e, b=ciB: e.tensor_tensor(out=Gt[:], in0=PrG, in1=b, op=ALU.mult), r=[R + "Pr", R + "c"], w=[R + "Gt"])
            dve(lambda e: e.tensor_tensor(out=Gi[:], in0=Gi[:], in1=Gt[:], op=ALU.add), r=[R + "Gi", R + "Gt"], w=[R + "Gi"])
            dve(lambda e: e.tensor_scalar(out=Gi[:], in0=Gi[:], scalar1=-1.0, scalar2=None, op0=ALU.mult), r=[R + "Gi"], w=[R + "Gi"])
            for g8 in range(GB):
                g = g0 + g8
                sl = g % 2
                fl = lambda ap: ap.rearrange("p a b -> p (a b)")
                pe(lambda e, g8=g8: e.matmul(psB[:, 0:256], lhsT=fl(Fr[:, g8, 15:23, :]), rhs=fl(Gr[:, g8, 0:16, :]), start=True, stop=False),
                   r=[R + "Fr", R + "Gr"], w=["psB"], mark=False)
                pe(lambda e, g8=g8: e.matmul(psB[:, 0:256], lhsT=fl(Fi[:, g8, 15:23, :]), rhs=fl(Gi[:, g8, 0:16, :]), start=False, stop=True),
                   r=[R + "Fi", R + "Gi"], w=["psB"])
                dve(lambda e: e.tensor_tensor(out=tbf[:], in0=psB[:, 0:256], in1=tmask[:, :, :].rearrange("p a b -> p (a b)"), op=ALU.mult),
                    r=["psB", "tmask"], w=[R + "tbf"])
                dve(lambda e, g=g, sl=sl: e.scalar_tensor_tensor(out=txs[sl][:, 0:256], in0=dsel[:], scalar=dcol[:, g:g + 1], in1=tbf[:],
                                                                 op0=ALU.mult, op1=ALU.add),
                    r=["dsel", R + "dcol", R + "tbf"], w=[R + "txs%d" % sl])
                for h in range(2):
                    for c, Fc in ((0, Fr), (1, Fi)):
                        pe(lambda e, g8=g8, h=h, c=c, Fc=Fc: e.transpose(out=psA[:, (h * 2 + c) * 64:(h * 2 + c) * 64 + 64],
                                                                          in_=fl(Fc[:, g8, 8 * h:8 * h + 8, :]), identity=ident[0:64, 0:64]),
                           r=[R + "Fr", R + "Fi", "ident"], w=["psA_lo"], mark=(h == 1 and c == 1))
                act(lambda e, sl=sl: e.copy(out=txs[sl][:, 256:512], in_=psA[:, 0:256]), r=["psA_lo", R + "txs%d" % sl], w=[R + "txs%d" % sl])
                dma(TXd[g], txs[sl][:], r=[R + "txs%d" % sl], w=["TXd%d" % g])
                pool(lambda e, g8=g8, sl=sl: e.tensor_copy(out=gsb[sl][:, 0, :], in_=fl(Gr[:, g8, 1:17, :])), r=[R + "Gr"], w=[R + "gsb%d" % sl])
                pool(lambda e, g8=g8, sl=sl: e.tensor_copy(out=gsb[sl][:, 1, :], in_=fl(Gi[:, g8, 1:17, :])), r=[R + "Gi", R + "gsb%d" % sl], w=[R + "gsb%d" % sl])
                dma(Gd[g], gsb[sl][:, :, :], r=[R + "gsb%d" % sl], w=["Gd%d" % g])
            yield
        dma(a16d.rearrange("c p g -> p c g"), a16[:], r=[R + "a16"], w=["a16d"])
        for par in range(2):
            for c in range(2):
                dma(a16p[64 * par:64 * par + 64, c, :], a16d.rearrange("c p (gh two) -> two c p gh", two=2)[par, c],
                    r=["a16d"], w=[R + "a16p"], slow=True)
        for lev in range(3):
            A1 = A1s[la][lev]; A2 = A2s[la][lev]
            dve(lambda e, A1=A1: e.tensor_copy(out=A1[:, :, :], in_=a16p[:, 0, :].unsqueeze(2).broadcast_to([128, 32, 2])), r=[R + "a16p"], w=["A1"])
            dve(lambda e, A2=A2: e.tensor_copy(out=A2[:, :, 1], in_=a16p[:, 1, :]), r=[R + "a16p"], w=["A2"])
            dve(lambda e, A2=A2: e.tensor_scalar(out=A2[:, :, 0], in0=a16p[:, 1, :], scalar1=-1.0, scalar2=None, op0=ALU.mult), r=[R + "a16p", "A2"], w=["A2"])
            if lev < 2:
                dve(lambda e: e.tensor_tensor(out=apw[:, :, :], in0=a16p[:, :, :], in1=a16p[:, :, :], op=ALU.mult), r=[R + "a16p"], w=["apw"])
                dve(lambda e: e.tensor_tensor(out=apt[:, 1, :], in0=a16p[:, 0, :], in1=a16p[:, 1, :], op=ALU.mult), r=[R + "a16p"], w=["apt"])
                dve(lambda e: e.tensor_tensor(out=a16p[:, 0, :], in0=apw[:, 0, :], in1=apw[:, 1, :], op=ALU.subtract), r=["apw", R + "a16p"], w=[R + "a16p"])
                dve(lambda e: e.tensor_scalar(out=a16p[:, 1, :], in0=apt[:, 1, :], scalar1=2.0, scalar2=None, op0=ALU.mult), r=["apt", R + "a16p"], w=[R + "a16p"])
        yield

    def s5_sequence(la, s, nk, src_tile, dst_tile, h0_fn, hout_fn, x_src_s=None, x_dst_s=None):
        stop_if = DBG["stop_if"]
        nk0 = nk
        x_s = x_src_s
        Uc = S5["Uc"]; SH = S5["SH"]
        TXd = TXd2[la]; Gd = Gd2[la]
        SHf = SH[:, :, :, :].rearrange("p a b c -> p (a b c)")
        SHb = SHf[:, 4096:8192].bitcast(BF16)
        psBb = psB.bitcast(BF16)

        def hb(j):
            return SHb[:, 1024 * j:1024 * (j + 1)]
        SB = [dict(F1=xt1, F2=t1, H1=xbt, H2=hmT, H3=szb, H4=ybt, psT=psT, psTn="psT"),
              dict(F1=t2, F2=szt, H1=S5["gt"], H2=yT, H3=S5["h3b"], H4=S5["h4b"], psT=psQ, psTn="psQ"),
              dict(F1=SHf[:, 0:1024], F2=SHf[:, 1024:2048], H1=hb(0), H2=hb(1).rearrange("p (c t) -> p c t", t=128), H3=hb(2), H4=hb(3),
                   psT=psBb[:, 0:1024], psTn="psB0"),
              dict(F1=SHf[:, 2048:3072], F2=SHf[:, 3072:4096], H1=hb(4), H2=hb(5).rearrange("p (c t) -> p c t", t=128), H3=hb(6), H4=hb(7),
                   psT=psBb[:, 1024:2048], psTn="psB1")]
        for kk in range(4):
            SB[kk]["psX"] = psA[:, 512 * kk:512 * kk + 512]; SB[kk]["psXn"] = "psA_q%d" % kk; SB[kk]["s1"] = S5["s1"][kk]

        def proj_h(B, K, W, wname, n, hf, bias_row=None):
            if bias_row is not None:
                pe(lambda e: e.matmul(B["psX"][:n, 0:512], lhsT=onesb[0:1, :n], rhs=bias_row[0:1, hf * 512:(hf + 1) * 512], start=True, stop=False),
                   r=["onesb", "bglur"], w=[B["psXn"]], mark=False)
            for c in range(8):
                pe(lambda e, c=c: e.matmul(B["psX"][:n, 0:512], lhsT=B["H2"][:, c, :n], rhs=W[:, c, hf * 512:(hf + 1) * 512],
                                           start=(c == 0 and bias_row is None), stop=(c == 7)),
                   r=["H2" + K, wname], w=[B["psXn"]], mark=(c == 7))

        allU = ["Uc%d" % g for g in range(64)]

        def tr8(B, k, src, srcn, dst, dstn, modulate, n=None):
            n = nk if n is None else n
            for c in range(8):
                pe(lambda e, c=c: e.transpose(out=B["psT"][:, c * 128:c * 128 + n], in_=src[:n, c * 128:(c + 1) * 128], identity=identb[:n, :n]),
                   r=[srcn, "identb"], w=[B["psTn"]], mark=(c == 7))
            yield
            if modulate:
                for c in range(8):
                    act(lambda e, c=c, cp=CUR["condP"]: e.activation(out=dst[:, c, :n], in_=B["psT"][:, c * 128:c * 128 + n], func=AF.Identity,
                                                                     bias=cp[:, c, s:s + 1], scale=cp[:, 8 + c, s:s + 1]),
                        r=[B["psTn"]], w=[dstn], mark=(c == 7))
            else:
                act(lambda e: e.copy(out=dst[:, :, :n], in_=B["psT"][:, :].rearrange("p (c t) -> p c t", t=128)[:, :, :n]),
                    r=[B["psTn"]], w=[dstn])
            yield

        def p1_task(i, sample=False):
            def gen(k):
                B = SB[k]; K = "_%d" % k
                n = DEC if sample else nk
                dma(B["H1"][:n, :], x_s if sample else src_tile(i), w=["H1" + K], eng="pool")
                yield
                yield from tr8(B, k, B["H1"], "H1" + K, B["H2"], "H2" + K, True, n)
                for hf in range(2):
                    proj_h(B, K, Win, "Win", n, hf)
                    yield
                    if sample:
                        act(lambda e, hf=hf: e.copy(out=B["H4"][:n, hf * 512:(hf + 1) * 512], in_=B["psX"][:n, 0:512]), r=[B["psXn"]], w=["H4" + K])
                    else:
                        dve(lambda e, hf=hf: e.tensor_copy(out=Uc[:n, 32 * hf:32 * hf + 32, i, :], in_=B["psX"][:n, 0:512].rearrange("p (g c) -> p g c", c=16)),
                            r=[B["psXn"]], w=allU[32 * hf:32 * hf + 32])
                if sample:
                    dma(ucs_d, B["H4"][:n, :], r=["H4" + K], w=["ucs_d"])
                    uv = ucs_d.rearrange("(k i) (g c) -> i k g c", i=16, c=16)
                    for ii in range(16):
                        dma(Uc[:4, :, ii, :], uv[ii], r=["ucs_d"], w=allU, slow=True)
                for hf in range(2):
                    proj_h(B, K, Win[:, :, 1024:2048], "Win", n, hf)
                    yield
                    act(lambda e, hf=hf: e.activation(out=B["H3"][:n, hf * 512:(hf + 1) * 512], in_=B["psX"][:n, 0:512], func=AF.Silu),
                        r=[B["psXn"]], w=["H3" + K])
                dma(zscr[i, :n, :], B["H3"][:n, :], r=["H3" + K], w=["zscr%d" % i])
            return gen

        def pq(k):
            pst = psT if k < 2 else psQ
            return pst, ("psT" if k < 2 else "psQ"), (k % 2) * 512, psA[:, 512 * k:512 * k + 512], "psA_q%d" % k

        def p2_task(g):
            def gen(k):
                par = g % 2; gh = g // 2; sl = k
                pst, pstn, pc0, psx, psxn = pq(k)
                dma(S5["TXs"][sl][:, :], TXd[g], r=["TXd%d" % g], w=["TXs%d" % sl])
                yield
                for h in range(2):
                    pe(lambda e, h=h: e.transpose(out=pst[:, pc0 + h * 128:pc0 + h * 128 + nk],
                                                  in_=Uc[:nk, g, 8 * h:8 * h + 8, :].rearrange("p a b -> p (a b)"), identity=identb[:nk, :nk]),
                       r=["Uc%d" % g, "identb"], w=[pstn], mark=(h == 1))
                yield
                act(lambda e: e.copy(out=S5["UTs"][sl][:, :, :nk], in_=pst[:, pc0:pc0 + 256].rearrange("p (h k) -> p h k", k=128)[:, :, :nk]),
                    r=[pstn], w=["UTs%d" % sl])
                yield
                for c in range(2):
                    for h in range(2):
                        pe(lambda e, c=c, h=h: e.matmul(psx[64 * par:64 * par + 64, c * 128:c * 128 + nk],
                                                        lhsT=S5["TXs"][sl][:, 256 + (h * 2 + c) * 64:256 + (h * 2 + c) * 64 + 64],
                                                        rhs=S5["UTs"][sl][:, h, :nk], start=(h == 0), stop=(h == 1)),
                           r=["TXs%d" % sl, "UTs%d" % sl], w=[psxn], mark=(c == 1 and h == 1))
                yield
                dve(lambda e: e.tensor_copy(out=SH[64 * par:64 * par + 64, gh, :, 1:nk + 1],
                                            in_=psx[64 * par:64 * par + 64, 0:256].rearrange("p (c k) -> p c k", k=128)[:, :, :nk]),
                    r=[psxn], w=["SH%d" % (gh // 16)])
            return gen

        def p4_task(g):
            def gen(k):
                par = g % 2; gh = g // 2; sl = k
                pst, pstn, pc0, psx, psxn = pq(k)
                dma(S5["TXs"][sl][:, 0:256], TXd[g][:, 0:256], r=["TXd%d" % g], w=["TXs%d" % sl])
                dma(S5["Gs"][sl][64 * par:64 * par + 64, :], Gd[g], r=["Gd%d" % g], w=["Gs%d" % sl], eng="act")
                act(lambda e: e.copy(out=S5["Hbs"][k][64 * par:64 * par + 64, :, :nk], in_=SH[64 * par:64 * par + 64, gh, :, 0:nk]),
                    r=["SH%d" % (gh // 16)], w=["Hb%d" % k])
                yield
                for h in range(2):
                    pe(lambda e, h=h: e.transpose(out=pst[:, pc0 + h * 128:pc0 + h * 128 + nk],
                                                  in_=Uc[:nk, g, 8 * h:8 * h + 8, :].rearrange("p a b -> p (a b)"), identity=identb[:nk, :nk]),
                       r=["Uc%d" % g, "identb"], w=[pstn], mark=(h == 1))
                yield
                act(lambda e: e.copy(out=S5["UTs"][sl][:, :, :nk], in_=pst[:, pc0:pc0 + 256].rearrange("p (h k) -> p h k", k=128)[:, :, :nk]),
                    r=[pstn], w=["UTs%d" % sl])
                yield
                o = psx[:nk, 0:256]
                pe(lambda e: e.matmul(o, lhsT=S5["UTs"][sl][:, 0, :nk], rhs=S5["TXs"][sl][:, 0:256], start=True, stop=False,
                                      skip_group_check=True), r=["UTs%d" % sl, "TXs%d" % sl], w=[psxn], mark=False)
                pe(lambda e: e.matmul(o[:, 128:256], lhsT=S5["UTs"][sl][:, 1, :nk], rhs=S5["TXs"][sl][:, 0:128], start=False, stop=False,
                                      skip_group_check=True), r=["UTs%d" % sl, "TXs%d" % sl], w=[psxn], mark=False)
                for c in range(2):
                    pe(lambda e, c=c: e.matmul(o, lhsT=S5["Hbs"][k][64 * par:64 * par + 64, c, :nk],
                                               rhs=S5["Gs"][sl][64 * par:64 * par + 64, c * 256:(c + 1) * 256],
                                               start=False, stop=(c == 1), skip_group_check=True),
                       r=["Hb%d" % k, "Gs%d" % sl], w=[psxn], mark=(c == 1))
                yield
                act(lambda e: e.activation(out=Uc[:nk, g, :, :], in_=psx[:nk, 0:256].rearrange("p (i c) -> p i c", c=16), func=AF.Gelu),
                    r=[psxn], w=["Uc%d" % g])
            return gen

        def p5_task(i, sample=False):
            def gen(k):
                B = SB[k]; K = "_%d" % k
                nk = DEC if sample else nk0
                F1 = B["F1"]; F2 = B["F2"]; s1k = B["s1"]
                dma(F1[:nk, :], x_src_s if sample else src_tile(i), w=["F1" + K])
                dma(B["H3"][:nk, :], zscr[i, :nk, :], r=["zscr%d" % i], w=["H3" + K], eng="pool")
                if sample:
                    gv = gcs_d.rearrange("(k i) (g c) -> i k g c", i=16, c=16)
                    for ii in range(16):
                        dma(gv[ii], Uc[:4, :, ii, :], r=allU, w=["gcs_d"], slow=True)
                    dma(B["H1"][:nk, :], gcs_d, r=["gcs_d"], w=["H1" + K])
                else:
                    dve(lambda e: e.tensor_copy(out=B["H1"][:nk, :].rearrange("p (g c) -> p g c", c=16), in_=Uc[:nk, :, i, :]), r=allU, w=["H1" + K])
                yield
                yield from tr8(B, k, B["H1"], "H1" + K, B["H2"], "H2" + K, False, nk)
                for hf in range(2):
                    proj_h(B, K, Wg, "Wg", nk, hf, bias_row=bglur)
                    yield
                    act(lambda e, hf=hf: e.activation(out=F2[:nk, hf * 512:(hf + 1) * 512], in_=B["psX"][:nk, 0:512], func=AF.Sigmoid),
                        r=[B["psXn"]], w=["F2" + K])
                dve(lambda e: e.tensor_tensor(out=F2[:nk, :], in0=F2[:nk, :], in1=B["H1"][:nk, :], op=ALU.mult), r=["F2" + K, "H1" + K], w=["F2" + K])
                dve(lambda e: e.tensor_tensor(out=B["H4"][:nk, :], in0=F2[:nk, :], in1=B["H3"][:nk, :], op=ALU.mult), r=["F2" + K, "H3" + K], w=["H4" + K])
                yield
                yield from tr8(B, k, B["H4"], "H4" + K, B["H2"], "H2" + K, False, nk)
                nt = nk
                for hf in range(2):
                    proj_h(B, K, Wo, "Wo", nk, hf)
                    yield
                    dve(lambda e, hf=hf: e.tensor_tensor(out=F2[:nt, hf * 512:(hf + 1) * 512], in0=B["psX"][:nt, 0:512],
                                                         in1=gateb[:nt, hf * 512:(hf + 1) * 512], op=ALU.mult),
                        r=[B["psXn"], "gateb"], w=["F2" + K])
                dve(lambda e: e.scalar_tensor_tensor(out=F1[:nt, :], in0=F1[:nt, :], scalar=ALPHA, in1=F2[:nt, :], op0=ALU.mult, op1=ALU.add),
                    r=["F1" + K, "F2" + K], w=["F1" + K])
                dve(lambda e: e.bn_stats(out=s1k[:nt, 0:6], in_=F1[:nt, 0:512]), r=["F1" + K], w=["s1" + K])
                dve(lambda e: e.bn_stats(out=s1k[:nt, 6:12], in_=F1[:nt, 512:1024]), r=["F1" + K, "s1" + K], w=["s1" + K])
                dve(lambda e: e.bn_aggr(out=s1k[:nt, 12:14], in_=s1k[:nt, 0:12]), r=["s1" + K], w=["s1b" + K])
                yield
                act(lambda e: e.activation(out=s1k[:nt, 14:15], in_=s1k[:nt, 13:14], func=AF.Sqrt, bias=epsb[:nt, 0:1]), r=["s1b" + K, "epsb"], w=["s1c" + K])
                dve(lambda e: e.reciprocal(out=s1k[:nt, 15:16], in_=s1k[:nt, 14:15]), r=["s1c" + K], w=["s1d" + K])
                dve(lambda e: e.scalar_tensor_tensor(out=F2[:nt, :], in0=F1[:nt, :], scalar=s1k[:nt, 12:13], in1=lngb[:nt, :], op0=ALU.subtract, op1=ALU.mult),
                    r=["F1" + K, "s1b" + K, "lngb"], w=["F2" + K])
                dve(lambda e: e.scalar_tensor_tensor(out=F2[:nt, :], in0=F2[:nt, :], scalar=s1k[:nt, 15:16], in1=lnbb[:nt, :], op0=ALU.mult, op1=ALU.add),
                    r=["F2" + K, "s1d" + K, "lnbb"], w=["F2" + K])
                dma(x_dst_s if sample else dst_tile(i), F2[:nt, :], r=["F2" + K], w=["dst"], eng="pool")
            return gen

        if s == 2:
            run_streams([p1_task(0, True)], 1)
        else:
            run_streams([p1_task(i) for i in range(16)], 4)
        if s == 2:
            load_w(Win, CUR["next_win"], "Win", 2048)
        sc1 = S5["sc1"]; sc2 = S5["sc2"]

        def scan_gen(hf):
            gsl = slice(16 * hf, 16 * hf + 16)
            R = "SH%d" % hf; Z = "_%d" % hf
            TA = [xt1, t2][hf]; TB = [t1, szt][hf]
            TAn = ["F1_0", "F1_1"][hf]; TBn = ["F2_0", "F2_1"][hf]

            def cmac(dk, sk, lev, cnt):
                step = 2 << lev
                A1 = A1s[la][lev]; A2 = A2s[la][lev]
                j0 = 0
                while j0 < cnt:
                    m = min(32, cnt - j0)
                    d = SH[:, gsl, :, dk + j0 * step:dk + (j0 + m - 1) * step + 1:step]
                    sr = SH[:, gsl, :, sk + j0 * step:sk + (j0 + m - 1) * step + 1:step]
                    ta = TA[:, 0:16 * 2 * m].rearrange("p (g c m) -> p g c m", g=16, c=2)
                    tb = TB[:, 0:16 * 2 * m].rearrange("p (g c m) -> p g c m", g=16, c=2)
                    a1b = A1[:, gsl, :].unsqueeze(3).broadcast_to([128, 16, 2, m])
                    dve(lambda e, ta=ta, sr=sr, a1b=a1b: e.tensor_tensor(out=ta, in0=sr, in1=a1b, op=ALU.mult), r=[R, "A1"], w=[TAn])
                    pool(lambda e, tb=tb, sr=sr, A2=A2, m=m: e.tensor_tensor(out=tb[:, :, 0, :], in0=sr[:, :, 1, :],
                                                                             in1=A2[:, gsl, 0:1].broadcast_to([128, 16, m]), op=ALU.mult),
                         r=[R, "A2"], w=[TBn + "a"])
                    pool(lambda e, tb=tb, sr=sr, A2=A2, m=m: e.tensor_tensor(out=tb[:, :, 1, :], in0=sr[:, :, 0, :],
                                                                             in1=A2[:, gsl, 1:2].broadcast_to([128, 16, m]), op=ALU.mult),
                         r=[R, "A2"], w=[TBn + "b"])
                    dve(lambda e, ta=ta, tb=tb: e.tensor_tensor(out=ta, in0=ta, in1=tb, op=ALU.add), r=[TAn, TBn + "a", TBn + "b"], w=[TAn])
                    dve(lambda e, ta=ta, d=d: e.tensor_tensor(out=d, in0=d, in1=ta, op=ALU.add), r=[TAn, R], w=[R])
                    j0 += m
                    yield

            yield from cmac(2, 1, 0, nk // 2)
            yield from cmac(4, 2, 1, nk // 4)
            A1 = A1s[la][2]; A2 = A2s[la][2]
            for k in range(0, nk, 4):
                v0 = SH[:, gsl, :, k]; v1 = SH[:, gsl, :, k + 4]
                dve(lambda e, v0=v0: e.tensor_tensor(out=sc1[:, gsl, :], in0=A1[:, gsl, :], in1=v0, op=ALU.mult), r=[R, "A1"], w=["sc1" + Z])
                pool(lambda e, k=k: e.tensor_tensor(out=sc2[:, gsl, 0], in0=A2[:, gsl, 0], in1=SH[:, gsl, 1, k], op=ALU.mult), r=[R, "A2"], w=["sc2a" + Z])
                pool(lambda e, k=k: e.tensor_tensor(out=sc2[:, gsl, 1], in0=A2[:, gsl, 1], in1=SH[:, gsl, 0, k], op=ALU.mult), r=[R, "A2"], w=["sc2b" + Z])
                dve(lambda e: e.tensor_tensor(out=sc1[:, gsl, :], in0=sc1[:, gsl, :], in1=sc2[:, gsl, :], op=ALU.add),
                    r=["sc1" + Z, "sc2a" + Z, "sc2b" + Z], w=["sc1" + Z])
                dve(lambda e, v1=v1: e.tensor_tensor(out=v1, in0=v1, in1=sc1[:, gsl, :], op=ALU.add), r=["sc1" + Z, R], w=[R])
                if (k // 4) % 2 == 1:
                    yield
            yield from cmac(2, 0, 1, nk // 4)
            yield from cmac(1, 0, 0, nk // 2)

        P.barrier()
        h0_fn()
        run_streams([p2_task(g) for g in range(32)], 4)
        run_streams([p2_task(g) for g in range(32, 64)], 4, extra=[scan_gen(0)])
        run_streams([p4_task(g) for g in range(32)], 4, extra=[scan_gen(1)])
        hout_fn()
        run_streams([p4_task(g) for g in range(32, 64)], 4)
        P.barrier()
        if s == 2:
            run_streams([p5_task(0, True)], 1)
        else:
            run_streams([p5_task(i) for i in range(16)], 4)

    def s5_layer(la, l, src, dst):
        stop_if = DBG["stop_if"]
        layer_ln(l)
        if la > 0:
            load_w(Wg, w_glu[la], "Wg", 1024)
            load_w(Wo, w_out_a[la], "Wo", 1024)
            dma(bglur[0:1, :], b_glu[la:la + 1, :], w=["bglur"], eng="pool")
        CUR["next_win"] = w_in_a[1] if la == 0 else w_in_b[0]
        push_scope()
        alloc_s5_seq()
        for s in range(3):
            load_gate(s)
            if s < 2:
                nk = 128
                sv = src[0][s].rearrange("(k i) d -> i k d", i=16)
                dv = dst[0][s].rearrange("(k i) d -> i k d", i=16)
                def h0_fn():
                    pool(lambda e: e.memset(S5["SH"][:, :, :, 0], 0.0), w=["SH0", "SH1"])
                def hout_fn(s=s):
                    for par in range(2):
                        for c in range(2):
                            dma(ssm_p[la, s].rearrange("(gh two) p c -> two c p gh", two=2)[par, c], S5["SH"][64 * par:64 * par + 64, :, c, 128],
                                r=["SH0", "SH1"], w=["ssm_out"], slow=True)
            else:
                nk = 4
                sv = src[1].rearrange("(k i) d -> i k d", i=16)
                dv = dst[1].rearrange("(k i) d -> i k d", i=16)
                def h0_fn():
                    for par in range(2):
                        for c in range(2):
                            dma(S5["SH"][64 * par:64 * par + 64, :, c, 0], st_in[la].rearrange("(gh two) p c -> two c p gh", two=2)[par, c],
                                w=["SH0", "SH1"], slow=True)
                def hout_fn():
                    for par in range(2):
                        for c in range(2):
                            dma(ssm_s[la].rearrange("(gh two) p c -> two c p gh", two=2)[par, c], S5["SH"][64 * par:64 * par + 64, :, c, 4],
                                r=["SH0", "SH1"], w=["ssm_out"], slow=True)
            s5_sequence(la, s, nk, lambda i, sv=sv: sv[i], lambda i, dv=dv: dv[i], h0_fn, hout_fn, src[1], dst[1])
        pop_scope()

    AT = {}

    def alloc_attn():
        AT["kf"] = T([128, 256]); AT["vf"] = T([128, 256]); AT["kb"] = T([128, 256], BF16)
        AT["kTs"] = [T([64, 4, 128], BF16) for _ in range(2)]
        AT["vbs"] = [T([128, 256], BF16) for _ in range(2)]
        AT["qb"] = T([128, D], BF16); AT["qT"] = T([64, 16, 128], BF16)
        AT["r1"] = T([128, 16, 32]); AT["r2"] = T([128, 16, 32])
        AT["sm"] = [T([128, 4, 256]) for _ in range(2)]; AT["eb"] = [T([128, 4, 256], BF16) for _ in range(2)]
        AT["eT"] = [T([128, 8, 128], BF16) for _ in range(2)]
        AT["st"] = [T([128, 20]) for _ in range(2)]; AT["rinv"] = T([128, 16])
        AT["xt"] = [T([128, D]) for _ in range(2)]; AT["sinkb"] = T([128, 16]); AT["nsinkb"] = T([128, 16])
        AT["ogb"] = T([128, D], BF16)
        AT["ckc"] = T([128, 256]); AT["ckb"] = T([128, 256], BF16)
        AT["xTp"] = T([128, 8, 128], BF16); AT["Wkv"] = T([128, 8, 512], BF16)

    def rope(src_ps, psname, nh, nt, j, out_ap, wname):
        sv = src_ps.rearrange("p (h d) -> p h d", d=64)
        ov = out_ap.rearrange("p (h d) -> p h d", d=64)
        cb = cosT[:nt, j:j + 1, :].broadcast_to([nt, nh, 32]); sb = sinT[:nt, j:j + 1, :].broadcast_to([nt, nh, 32])
        dve(lambda e: e.tensor_tensor(out=AT["r1"][:nt, :nh, :], in0=sv[:, :, 0:32], in1=cb, op=ALU.mult), r=[psname, "cosT"], w=["r1"])
        dve(lambda e: e.tensor_tensor(out=AT["r2"][:nt, :nh, :], in0=sv[:, :, 32:64], in1=sb, op=ALU.mult), r=[psname, "sinT"], w=["r2"])
        dve(lambda e: e.tensor_tensor(out=ov[:, :, 0:32], in0=AT["r1"][:nt, :nh, :], in1=AT["r2"][:nt, :nh, :], op=ALU.subtract), r=["r1", "r2"], w=[wname])
        dve(lambda e: e.tensor_tensor(out=AT["r1"][:nt, :nh, :], in0=sv[:, :, 32:64], in1=cb, op=ALU.mult), r=[psname, "cosT"], w=["r1"])
        dve(lambda e: e.tensor_tensor(out=AT["r2"][:nt, :nh, :], in0=sv[:, :, 0:32], in1=sb, op=ALU.mult), r=[psname, "sinT"], w=["r2"])
        dve(lambda e: e.tensor_tensor(out=ov[:, :, 32:64], in0=AT["r1"][:nt, :nh, :], in1=AT["r2"][:nt, :nh, :], op=ALU.add), r=["r1", "r2", wname], w=[wname])

    def kT_from(kb_t, rname, nt, slot):
        for h in range(4):
            pe(lambda e, h=h: e.transpose(out=psT[0:64, h * 128:h * 128 + nt], in_=kb_t[:nt, h * 64:(h + 1) * 64], identity=identb[:nt, :nt]),
               r=[rname, "identb"], w=["psT"], mark=(h == 3))
        dve(lambda e: e.tensor_copy(out=AT["kTs"][slot][:, :, :nt], in_=psT[0:64, 0:512].rearrange("p (h t) -> p h t", t=128)[:, :, :nt]),
            r=["psT"], w=["kT%d" % slot])

    def attn_pro(first_attn, s, nt, X, xn, cur, rope_j, t0, kv_out):
        act(lambda e: e.copy(out=xbt[:nt, :], in_=X[:nt, :]), r=[xn], w=["xbt"])
        yield
        for c in range(8):
            pe(lambda e, c=c: e.transpose(out=psT[:, c * 128:c * 128 + nt], in_=xbt[:nt, c * 128:(c + 1) * 128],
                                          identity=identb[:nt, :nt]), r=["xbt", "identb"], w=["psT"], mark=(c == 7))
        yield
        for c in range(8):
            act(lambda e, c=c, cp=CUR["condP"]: e.activation(out=hmT[:, c, :nt], in_=psT[:, c * 128:c * 128 + nt], func=AF.Identity,
                                                             bias=cp[:, c, s:s + 1], scale=cp[:, 8 + c, s:s + 1]),
                r=["psT"], w=["hmT"], mark=(c == 7))
        if first_attn:
            dve(lambda e: e.tensor_copy(out=AT["xTp"][:, :, :nt], in_=psT[:, :].rearrange("p (c t) -> p c t", t=128)[:, :, :nt]),
                r=["psT"], w=["xTp"])
        yield
        if first_attn:
            proj(psA[:, 1024:2048], AT["xTp"], "xTp", AT["Wkv"], "Wkv", nt, 512, ["psA_hi"])
            yield
            rope(psA[:nt, 1024:1280], "psA_hi", 4, nt, rope_j, AT["kf"][:nt, :], "kf")
            act(lambda e: e.copy(out=AT["vf"][:nt, :], in_=psA[:nt, 1280:1536]), r=["psA_hi"], w=["vf"])
            act(lambda e: e.copy(out=AT["kb"][:nt, :], in_=AT["kf"][:nt, :]), r=["kf"], w=["kb"])
            dve(lambda e: e.tensor_copy(out=AT["vbs"][cur][:nt, :], in_=AT["vf"][:nt, :]), r=["vf"], w=["vb%d" % cur])
            yield
            kT_from(AT["kb"], "kb", nt, cur)
            dma(ktscr[s, :, :, t0:t0 + nt], AT["kTs"][cur][:, :, :nt], r=["kT%d" % cur], w=["ktscr"], eng="pool")
            dma(vscr[s, t0:t0 + nt, :], AT["vbs"][cur][:nt, :], r=["vb%d" % cur], w=["vscr"], eng="pool")
            if kv_out is not None:
                dma(kv_out[0], AT["kf"][:nt, :], r=["kf"], w=["kvout"], eng="pool")
                dma(kv_out[1], AT["vf"][:nt, :], r=["vf"], w=["kvout"], eng="pool")
            yield
        else:
            dma(AT["kTs"][cur][:, :, :nt], ktscr[s, :, :, t0:t0 + nt], w=["kT%d" % cur])
            dma(AT["vbs"][cur][:nt, :], vscr[s, t0:t0 + nt, :], w=["vb%d" % cur])
        proj(psA, hmT, "hmT", AT["Win"], AT["Winn"], nt, 1024, ["psA_lo"])
        yield
        proj(psA[:, 1024:2048], hmT, "hmT", AT["Win"][:, :, 1024:2048], AT["Winn"], nt, 1024, ["psA_hi"])
        yield
        rope(psA[:nt, 0:1024], "psA_lo", 16, nt, rope_j, AT["qb"][:nt, :], "qb")
        act(lambda e: e.activation(out=szt[:nt, :], in_=psA[:nt, 1024:2048], func=AF.Silu), r=["psA_hi"], w=["szt"])
        yield
        for half, (pst, pname) in enumerate(((psT, "psT"), (psQ, "psQ"))):
            for hh in range(8):
                h = half * 8 + hh
                pe(lambda e, h=h, hh=hh, pst=pst: e.transpose(out=pst[0:64, hh * 128:hh * 128 + nt], in_=AT["qb"][:nt, h * 64:(h + 1) * 64],
                                                              identity=identb[:nt, :nt]),
                   r=["qb", "identb"], w=[pname], mark=(hh == 7))
            yield
            dve(lambda e, half=half, pst=pst: e.tensor_copy(out=AT["qT"][:, half * 8:half * 8 + 8, :nt],
                                                            in_=pst[0:64, :].rearrange("p (h t) -> p h t", t=128)[:, :, :nt]),
                r=[pname], w=["qT"])
        yield

    def attn_epi(s, nt, X, xn, dst_ap):
        dve(lambda e: e.tensor_tensor(out=t1[:nt, :].rearrange("p (h d) -> p h d", d=64), in0=psA[:nt, 0:1024].rearrange("p (h d) -> p h d", d=64),
                                      in1=AT["rinv"][:nt, :].unsqueeze(2).broadcast_to([nt, 16, 64]), op=ALU.mult), r=["psA_lo", "rinv"], w=["t1"])
        dve(lambda e: e.tensor_tensor(out=AT["ogb"][:nt, :], in0=t1[:nt, :], in1=szt[:nt, :], op=ALU.mult), r=["t1", "szt"], w=["ogb"])
        yield
        for c in range(8):
            pe(lambda e, c=c: e.transpose(out=psQ[:, c * 128:c * 128 + nt], in_=AT["ogb"][:nt, c * 128:(c + 1) * 128],
                                          identity=identb[:nt, :nt]), r=["ogb", "identb"], w=["psQ"], mark=(c == 7))
        yield
        dve(lambda e: e.tensor_copy(out=yT[:, :, :nt], in_=psQ[:, :].rearrange("p (c t) -> p c t", t=128)[:, :, :nt]), r=["psQ"], w=["yT"])
        yield
        proj(psB, yT, "yT", AT["Wo"], AT["Won"], nt, 1024, ["psB"])
        yield
        resid_ln(psB[:nt, 0:1024], "psB", 0, s, nt, dst_ap, "dst", X=X, xn=xn)
        yield

    def attn_heads(nt, prev, cur, mask, mname):
        nkeys = 128 + nt

        def hg_task(hg):
            def gen(sl):
                kvh = hg
                ps_s = psB if sl == 0 else psA[:, 1024:2048]
                psn = "psB" if sl == 0 else "psA_hi"
                ps_e = psT if sl == 0 else psQ
                pen = "psT" if sl == 0 else "psQ"
                sm = AT["sm"][sl]; eb = AT["eb"][sl]; eT = AT["eT"][sl]; st = AT["st"][sl]
                S = "_%d" % sl
                has_mask = mask is not None
                if has_mask:
                    pairs = ((mLa, mRa), (mLb, mRb)) if mname == "maskG" else ((mLa, mRa), (mL1, mR0))
                    for half in range(2):
                        for pi, (ml, mr) in enumerate(pairs):
                            pe(lambda e, half=half, ml=ml, mr=mr, pi=pi: e.matmul(ps_s[:nt, half * 512:(half + 1) * 512], lhsT=ml[0:1, :nt], rhs=mr[0:1, :],
                                                                                   start=(pi == 0), stop=False, skip_group_check=True),
                               r=["mk"], w=[psn], mark=False)
                for hh in range(4):
                    h = 4 * hg + hh
                    pe(lambda e, h=h, hh=hh: e.matmul(ps_s[:nt, hh * 256:hh * 256 + 128], lhsT=AT["qT"][:, h, :nt], rhs=AT["kTs"][prev][:, kvh, :],
                                                      start=(not has_mask), stop=(not has_mask), skip_group_check=True), r=["qT", "kT%d" % prev], w=[psn], mark=False)
                    pe(lambda e, h=h, hh=hh: e.matmul(ps_s[:nt, hh * 256 + 128:hh * 256 + 128 + nt], lhsT=AT["qT"][:, h, :nt],
                                                      rhs=AT["kTs"][cur][:, kvh, :nt], start=(not has_mask), stop=True, skip_group_check=True),
                       r=["qT", "kT%d" % cur], w=[psn], mark=(hh == 3))
                yield
                psv = ps_s[:nt, 0:1024].rearrange("p (h k) -> p h k", k=256)[:, :, :nkeys]
                dve(lambda e: e.reduce_max(out=st[:nt, 0:4], in_=psv, axis=AX.X), r=[psn], w=["st" + S])
                dve(lambda e: e.scalar_tensor_tensor(out=st[:nt, 4:8], in0=st[:nt, 0:4], scalar=-0.125, in1=AT["nsinkb"][:nt, 4 * hg:4 * hg + 4],
                                                     op0=ALU.mult, op1=ALU.min), r=["st" + S, "sinkb"], w=["st" + S])
                yield
                for hh in range(4):
                    act(lambda e, hh=hh: e.activation(out=eb[:nt, hh, :nkeys], in_=ps_s[:nt, hh * 256:hh * 256 + nkeys], func=AF.Exp, scale=0.125,
                                                      bias=st[:nt, 4 + hh:5 + hh], accum_out=st[:nt, 8 + hh:9 + hh]),
                        r=[psn, "st" + S], w=["eb" + S, "stb" + S], mark=(hh == 3))
                dve(lambda e: e.tensor_tensor(out=st[:nt, 12:16], in0=AT["sinkb"][:nt, 4 * hg:4 * hg + 4], in1=st[:nt, 4:8], op=ALU.add),
                    r=["sinkb", "st" + S], w=["stc" + S])
                act(lambda e: e.activation(out=st[:nt, 12:16], in_=st[:nt, 12:16], func=AF.Exp), r=["stc" + S], w=["stc" + S])
                dve(lambda e: e.tensor_tensor(out=st[:nt, 16:20], in0=st[:nt, 8:12], in1=st[:nt, 12:16], op=ALU.add),
                    r=["stb" + S, "stc" + S], w=["std" + S])
                dve(lambda e: e.reciprocal(out=AT["rinv"][:nt, 4 * hg:4 * hg + 4], in_=st[:nt, 16:20]), r=["std" + S], w=["rinv"])
                yield
                for hh in range(4):
                    pe(lambda e, hh=hh: e.transpose(out=ps_e[:, (2 * hh) * 128:(2 * hh) * 128 + nt], in_=eb[:nt, hh, 0:128], identity=identb[:nt, :nt]),
                       r=["eb" + S, "identb"], w=[pen], mark=False)
                    pe(lambda e, hh=hh: e.transpose(out=ps_e[:nt, (2 * hh + 1) * 128:(2 * hh + 1) * 128 + nt], in_=eb[:nt, hh, 128:128 + nt],
                                                    identity=identb[:nt, :nt]),
                       r=["eb" + S, "identb"], w=[pen], mark=(hh == 3))
                yield
                act(lambda e: e.copy(out=eT[:, :, :nt], in_=ps_e[:, 0:1024].rearrange("p (b t) -> p b t", t=128)[:, :, :nt]), r=[pen], w=["eT" + S])
                yield
                for hh in range(4):
                    h = 4 * hg + hh
                    pe(lambda e, h=h, hh=hh: e.matmul(psA[:nt, h * 64:(h + 1) * 64], lhsT=eT[:, 2 * hh, :nt], rhs=AT["vbs"][prev][:, kvh * 64:(kvh + 1) * 64],
                                                      start=True, stop=False, skip_group_check=True), r=["eT" + S, "vb%d" % prev], w=["psA_lo"], mark=False)
                    pe(lambda e, h=h, hh=hh: e.matmul(psA[:nt, h * 64:(h + 1) * 64], lhsT=eT[:nt, 2 * hh + 1, :nt],
                                                      rhs=AT["vbs"][cur][:nt, kvh * 64:(kvh + 1) * 64],
                                                      start=False, stop=True, skip_group_check=True), r=["eT" + S, "vb%d" % cur], w=["psA_lo"], mark=(hh == 3))
            return gen

        run_streams([hg_task(hg) for hg in range(4)], 2)

    def run_gen(g):
        for _ in g:
            pass

    def attn_layer(lb, l, src, dst):
        first = (lb == 0)
        layer_ln(l)
        if first:
            push_scope()
            alloc_attn()
            AT["WinB"] = T([128, 8, 2048], BF16)
            load_w(Wo, w_out_b[0], "Wo", 1024)
            load_w(AT["Wkv"], w_kv, "Wkv", 512)
            load_w(AT["WinB"], w_in_b[1], "WinB", 2048)
            AT["Win"] = Win; AT["Wo"] = Wo; AT["Winn"] = "Win"; AT["Won"] = "Wo"
        else:
            load_w(Wo, w_out_b[1], "Wo", 1024)
            AT["Win"] = AT["WinB"]; AT["Wo"] = Wo; AT["Winn"] = "WinB"; AT["Won"] = "Wo"
        dma(AT["sinkb"][:], sinks[lb].partition_broadcast(128), w=["sinkb"])
        dve(lambda e: e.tensor_scalar(out=AT["nsinkb"][:], in0=AT["sinkb"][:], scalar1=-1.0, scalar2=None, op0=ALU.mult), r=["sinkb"], w=["sinkb"])
        tiles = []
        for s in range(2):
            for j in range(16):
                cur = j % 2
                tiles.append(dict(s=s, j=j, nt=128, sap=src[0][s, 128 * j:128 * (j + 1), :], dap=dst[0][s, 128 * j:128 * (j + 1), :],
                                  cur=cur, prev=1 - cur, mask=(mask0 if j == 0 else maskG), mname=("mask0" if j == 0 else "maskG"),
                                  rope_j=j, t0=128 * j, kv_out=((ck_p[s], cv_p[s]) if (first and j == 15) else None)))
        tiles.append(dict(s=2, j=0, nt=64, sap=src[1][:, :], dap=dst[1][:, :], cur=0, prev=1, mask=None, mname=None, rope_j=16, t0=0,
                          kv_out=((ck_s[:, :], cv_s[:, :]) if first else None)))
        XT = AT["xt"]

        def setup_seq(t):
            if t["s"] < 2:
                pool(lambda e: e.memset(AT["kTs"][1][:], 0.0), w=["kT1"])
                pool(lambda e: e.memset(AT["vbs"][1][:], 0.0), w=["vb1"])
            else:
                dma(AT["ckc"][:], ck_in, w=["ckc"])
                act(lambda e: e.copy(out=AT["ckb"][:], in_=AT["ckc"][:]), r=["ckc"], w=["ckb"])
                kT_from(AT["ckb"], "ckb", 128, 1)
                dma(AT["ckc"][:], cv_in, w=["ckc"])
                dve(lambda e: e.tensor_copy(out=AT["vbs"][1][:], in_=AT["ckc"][:]), r=["ckc"], w=["vb1"])

        def pro_of(idx):
            t = tiles[idx]
            return attn_pro(first, t["s"], t["nt"], XT[idx % 2], "axt%d" % (idx % 2), t["cur"], t["rope_j"], t["t0"], t["kv_out"])

        dma(XT[0][:128, :], tiles[0]["sap"], w=["axt0"])
        setup_seq(tiles[0])
        run_gen(pro_of(0))
        for idx, t in enumerate(tiles):
            xs = idx % 2
            if idx + 1 < len(tiles):
                nx = tiles[idx + 1]
                dma(XT[1 - xs][:nx["nt"], :], nx["sap"], w=["axt%d" % (1 - xs)])
            attn_heads(t["nt"], t["prev"], t["cur"], t["mask"], t["mname"])
            if t["j"] == 0:
                load_gate(t["s"])
            epi = attn_epi(t["s"], t["nt"], XT[xs], "axt%d" % xs, t["dap"])
            if idx + 1 < len(tiles):
                nx = tiles[idx + 1]
                if nx["j"] == 0:
                    setup_seq(nx)
                run_streams([], 1, extra=[epi, pro_of(idx + 1)])
            else:
                run_gen(epi)
        if not first:
            pop_scope()
        else:
            P.barrier()

    def dbg_dump(name, src_ap, shape, regions):
        o = nc.dram_tensor("dbg_" + name, list(shape), src_ap.dtype if hasattr(src_ap, "dtype") else F32, kind="ExternalOutput").ap()
        dma(o, src_ap, r=regions, w=["dbg_" + name], slow=True)

    def stop_if(tag, dumps):
        if DEBUG["stop"] == tag:
            P.barrier()
            for name, ap, shape in dumps():
                dbg_dump(name, ap, shape, [])
            raise _Stop()

    DBG["stop_if"] = stop_if
    try:
        load_w(Win, w_in_a[0], "Win", 2048)
        load_w(Wg, w_glu[0], "Wg", 1024)
        load_w(Wo, w_out_a[0], "Wo", 1024)
        dma(bglur[0:1, :], b_glu[0:1, :], w=["bglur"], eng="pool")
        push_scope()
        ada_alloc(); gen_alloc()

        def gen_both():
            yield from gen_run(0)
            yield from gen_run(1)
        run_streams([], 1, extra=[ada_all(), gen_both()])
        pop_scope()
        stop_if("S", lambda: [("condP0", condPs[0][:], [128, 24, 4]), ("condP3", condPs[3][:], [128, 24, 4]), ("gate_d", gate_d4, [4, 3, D]),
                              ("TXd", TXd2[0], [64, 128, 512]), ("Gd", Gd2[0], [64, 64, 512]),
                              ("A1", A1s[0][:], [128, 32, 2]), ("A2", A2s[0][:], [128, 32, 2]),
                              ("TXd1", TXd2[1], [64, 128, 512])])
        s5_layer(0, 0, (x_p, x_s), (xa_p, xa_s))
        stop_if("L0", lambda: [("xa_p", xa_p, [2, SEQ, D]), ("xa_s", xa_s, [DEC, D])])
        s5_layer(1, 1, (xa_p, xa_s), (xb_p, xb_s))
        stop_if("L1", lambda: [("xb_p", xb_p, [2, SEQ, D]), ("xb_s", xb_s, [DEC, D])])
        attn_layer(0, 2, (xb_p, xb_s), (xa_p, xa_s))
        stop_if("L2", lambda: [("xa_p", xa_p, [2, SEQ, D]), ("xa_s", xa_s, [DEC, D])])
        attn_layer(1, 3, (xa_p, xa_s), (y_p, y_s))
    except _Stop:
        pass
    P.barrier(["sp"])
    P.replay()
    P.close()
    return nc


_NC = None


def kernel(**inputs):
    global _NC
    f = lambda a: np.ascontiguousarray(np.asarray(a, dtype=np.float32))
    inp = {k: f(v) for k, v in inputs.items()}
    if _NC is None:
        _NC = build_nc()
    nc = _NC
    wnames = ["w_ada", "b_ada", "ln_g", "ln_b", "w_in_a", "ssm_a_re", "ssm_a_im", "ssm_b_re", "ssm_b_im", "ssm_c_re",
              "ssm_c_im", "ssm_d", "ssm_log_dt", "w_glu", "b_glu", "w_out_a", "w_kv", "w_in_b", "attn_sinks", "w_out_b"]
    in_maps = []
    for c in range(NCORES):
        m = {k: inp[k] for k in wnames}
        m["x_p"] = f(inp["x_prompt"][2 * c:2 * c + 2])
        m["x_s"] = f(inp["x_sample"][c])
        m["st_in"] = f(inp["state_ssm"][:, c])
        m["ck_in"] = f(inp["cache_k"][c].reshape(128, 256))
        m["cv_in"] = f(inp["cache_v"][c].reshape(128, 256))
        m["c_all"] = f(np.stack([inp["c_prompt"][2 * c], inp["c_prompt"][2 * c + 1], inp["c_sample"][c]]))
        in_maps.append(m)
    res = run_bass_kernel_spmd(nc, in_maps, core_ids=list(range(NCORES)))
    R = res.results
    y_prompt = np.concatenate([r["y_p"] for r in R], axis=0)
    y_sample = np.stack([r["y_s"] for r in R], axis=0)
    ssm_pp = np.concatenate([r["ssm_p"] for r in R], axis=1)
    ckp = np.concatenate([r["ck_p"] for r in R], axis=0).reshape(16, 128, 4, 64)
    cvp = np.concatenate([r["cv_p"] for r in R], axis=0).reshape(16, 128, 4, 64)
    ssm_ss = np.stack([r["ssm_s"] for r in R], axis=1)
    cks = np.stack([r["ck_s"] for r in R], axis=0).reshape(8, 64, 4, 64)
    cvs = np.stack([r["cv_s"] for r in R], axis=0).reshape(8, 64, 4, 64)
    return (y_prompt.astype(np.float32), y_sample.astype(np.float32), ssm_pp.astype(np.float32), ckp.astype(np.float32),
            cvp.astype(np.float32), ssm_ss.astype(np.float32), cks.astype(np.float32), cvs.astype(np.float32))
```

```python
import math
import numpy as np
import concourse.bass as bass
import concourse.mybir as mybir
from concourse.bass_utils import run_bass_kernel_spmd

F32 = mybir.dt.float32
BF16 = mybir.dt.bfloat16
AF = mybir.ActivationFunctionType
ALU = mybir.AluOpType
AX = mybir.AxisListType

D = 1024
SEQ = 2048
DEC = 64
NCORES = 8
ALPHA = (2.0 * 4) ** 0.25
EPS = 1e-5
MAGIC = 12582912.0
TWO_PI = 2.0 * math.pi


class Prog:
    ENGS = ("pe", "act", "dve", "pool", "sp")

    def __init__(self, nc):
        self.nc = nc
        self.ops = {e: [] for e in self.ENGS}
        self.cur = {}
        self.waited = {e: {} for e in self.ENGS}
        self.lastw = {}
        self.reads = {}
        self.pending = {e: ([], []) for e in self.ENGS}
        self.dma_pool = []
        self.dma_rr = 0
        self.swdge_pool = []
        self.swdge_rr = 0
        self.nsem = 0
        self.stack = []

    def new_sem(self):
        cm = self.nc.semaphore("s%d" % self.nsem)
        self.nsem += 1
        s = cm.__enter__()
        self.stack.append(cm)
        return s

    def setup(self, n_dma=32):
        for e in self.ENGS:
            self.cur[e] = [self.new_sem(), 0]
        for _ in range(n_dma):
            self.dma_pool.append([self.new_sem(), 0])
        for _ in range(24):
            self.swdge_pool.append([self.new_sem(), 0])

    def _need(self, eng, ev, waits):
        if ev is None:
            return
        sem, val, src = ev
        if src == eng and eng == "pe":
            return
        k = id(sem)
        if self.waited[eng].get(k, (None, 0))[1] >= val:
            return
        self.waited[eng][k] = (sem, val)
        waits.append((sem, val))

    @staticmethod
    def _best(waits):
        best = {}
        for sem, val in waits:
            k = id(sem)
            if k not in best or best[k][1] < val:
                best[k] = (sem, val)
        return list(best.values())

    def op(self, eng, fn, reads=(), writes=(), mark=True, dma=False):
        al = {"psA_lo": ("psA_q0", "psA_q1"), "psA_hi": ("psA_q2", "psA_q3"), "psB": ("psB0", "psB1")}
        reads = [x for r in reads for x in al.get(r, (r,))]
        writes = [x for r in writes for x in al.get(r, (r,))]
        writes = list(writes) + [r for r in reads if r.startswith("ps")]
        reads = [r for r in reads if not r.startswith("ps")]
        waits = []
        for r in reads:
            self._need(eng, self.lastw.get(r), waits)
        for w in writes:
            self._need(eng, self.lastw.get(w), waits)
            for ev in self.reads.get(w, ()):
                self._need(eng, ev, waits)
        pr, pw = self.pending[eng]
        if dma:
            if eng == "pool":
                slot = self.swdge_pool[self.swdge_rr % len(self.swdge_pool)]
                self.swdge_rr += 1
            else:
                slot = self.dma_pool[self.dma_rr % len(self.dma_pool)]
                self.dma_rr += 1
            if slot[1] > 0:
                self._need(eng, (slot[0], slot[1], "dma"), waits)
            slot[1] += 16
            ev = (slot[0], slot[1], "dma")
            self.ops[eng].append((fn, self._best(waits), (slot[0], 16)))
            rr, ww = list(reads), list(writes)
        elif mark:
            c = self.cur[eng]
            if c[1] >= 2000:
                c = self.cur[eng] = [self.new_sem(), 0]
            c[1] += 1
            ev = (c[0], c[1], eng)
            self.ops[eng].append((fn, self._best(waits), (c[0], 1)))
            rr, ww = list(reads) + pr, list(writes) + pw
            self.pending[eng] = ([], [])
        else:
            self.ops[eng].append((fn, self._best(waits), None))
            pr.extend(reads)
            pw.extend(writes)
            return None
        for r in rr:
            self.reads.setdefault(r, []).append(ev)
        for w in ww:
            self.lastw[w] = ev
            self.reads[w] = []
        return ev

    def barrier(self, engs=None):
        evs = list(self.lastw.values())
        for l in self.reads.values():
            evs.extend(l)
        for eng in (engs or self.ENGS):
            waits = []
            for ev in evs:
                if ev is not None:
                    sem, val, src = ev
                    k = id(sem)
                    if self.waited[eng].get(k, (None, 0))[1] >= val:
                        continue
                    self.waited[eng][k] = (sem, val)
                    waits.append((sem, val))
            self.ops[eng].append((None, self._best(waits), None))
        if engs is None:
            self.lastw = {}
            self.reads = {}

    def replay(self):
        nc = self.nc
        engmap = {"pe": "tensor", "act": "scalar", "dve": "vector", "pool": "gpsimd", "sp": "sync"}
        with nc.Block() as block:
            for e in self.ENGS:
                ops = self.ops[e]

                def body(engine, ops=ops):
                    for fn, waits, inc in ops:
                        for sem, val in waits:
                            engine.wait_ge(sem, val)
                        if fn is None:
                            continue
                        inst = fn(engine)
                        if inc is not None:
                            inst.then_inc(inc[0], inc[1])
                getattr(block, engmap[e])(body)

    def close(self):
        for cm in reversed(self.stack):
            cm.__exit__(None, None, None)


DEBUG = {"stop": None}


class _Stop(Exception):
    pass


def build_nc():
    nc = bass.Bass("TRN2", target_bir_lowering=False)

    def din(name, shape):
        return nc.dram_tensor(name, list(shape), F32, kind="ExternalInput").ap()

    def dout(name, shape):
        return nc.dram_tensor(name, list(shape), F32, kind="ExternalOutput").ap()

    def dscr(name, shape, dt=F32):
        return nc.dram_tensor(name, list(shape), dt).ap()

    x_p = din("x_p", [2, SEQ, D]); x_s = din("x_s", [DEC, D])
    st_in = din("st_in", [2, 64, 64, 2]); ck_in = din("ck_in", [128, 256]); cv_in = din("cv_in", [128, 256])
    c_all = din("c_all", [3, D])
    w_ada = din("w_ada", [4, D, 3 * D]); b_ada = din("b_ada", [4, 3 * D])
    ln_g = din("ln_g", [4, D]); ln_b = din("ln_b", [4, D])
    w_in_a = din("w_in_a", [2, D, 2 * D])
    a_re = din("ssm_a_re", [2, 64, 64]); a_im = din("ssm_a_im", [2, 64, 64])
    b_re = din("ssm_b_re", [2, 64, 64, 16]); b_im = din("ssm_b_im", [2, 64, 64, 16])
    c_re = din("ssm_c_re", [2, 64, 16, 64]); c_im = din("ssm_c_im", [2, 64, 16, 64])
    ssm_d = din("ssm_d", [2, D]); log_dt = din("ssm_log_dt", [2, 64])
    w_glu = din("w_glu", [2, D, D]); b_glu = din("b_glu", [2, D]); w_out_a = din("w_out_a", [2, D, D])
    w_kv = din("w_kv", [D, 512]); w_in_b = din("w_in_b", [2, D, 2 * D])
    sinks = din("attn_sinks", [2, 16]); w_out_b = din("w_out_b", [2, D, D])

    y_p = dout("y_p", [2, SEQ, D]); y_s = dout("y_s", [DEC, D])
    ssm_p = dout("ssm_p", [2, 2, 64, 64, 2]); ck_p = dout("ck_p", [2, 128, 256]); cv_p = dout("cv_p", [2, 128, 256])
    ssm_s = dout("ssm_s", [2, 64, 64, 2]); ck_s = dout("ck_s", [DEC, 256]); cv_s = dout("cv_s", [DEC, 256])

    xa_p = dscr("xa_p", [2, SEQ, D]); xa_s = dscr("xa_s", [DEC, D])
    xb_p = dscr("xb_p", [2, SEQ, D]); xb_s = dscr("xb_s", [DEC, D])
    zscr = dscr("zscr", [16, 128, D], BF16)
    ktscr = dscr("ktscr", [3, 64, 4, SEQ], BF16); vscr = dscr("vscr", [3, SEQ, 256], BF16)
    TXd2 = dscr("TXd", [2, 64, 128, 512], BF16); Gd2 = dscr("Gd", [2, 64, 64, 512], BF16)
    a16d = dscr("a16d", [2, 64, 64], F32)
    ucs_d = dscr("ucs_d", [DEC, D], BF16); gcs_d = dscr("gcs_d", [DEC, D], BF16)

    P = Prog(nc)
    P.setup()
    _cnt = [0]

    scopes = []

    def T(shape, dt=F32, name=None):
        _cnt[0] += 1
        nm = "t%d" % _cnt[0]
        if scopes:
            cm = nc.sbuf_tensor(nm, list(shape), dt)
            t = cm.__enter__()
            scopes[-1].append(cm)
            return t
        return nc.alloc_sbuf_tensor(nm, list(shape), dt)

    def push_scope():
        P.barrier()
        scopes.append([])

    def pop_scope():
        P.barrier()
        for cm in reversed(scopes.pop()):
            cm.__exit__(None, None, None)

    def dve(fn, r=(), w=(), mark=True): return P.op("dve", fn, r, w, mark)
    def act(fn, r=(), w=(), mark=True): return P.op("act", fn, r, w, mark)
    def pool(fn, r=(), w=(), mark=True): return P.op("pool", fn, r, w, mark)
    def pe(fn, r=(), w=(), mark=True): return P.op("pe", fn, r, w, mark)
    def dma(out, in_, r=(), w=(), eng="sp", slow=False):
        if slow:
            return P.op(eng, lambda e: e.dma_start(out=out, in_=in_, allow_slow_non_contiguous=True), r, w, dma=True)
        return P.op(eng, lambda e: e.dma_start(out=out, in_=in_), r, w, dma=True)

    psA = nc.alloc_psum_tensor("psA", [128, 2048], F32)
    psB = nc.alloc_psum_tensor("psB", [128, 1024], F32)
    psT = nc.alloc_psum_tensor("psT", [128, 1024], BF16)
    psQ = nc.alloc_psum_tensor("psQ", [128, 1024], BF16)

    ident = T([128, 128]); identb = T([128, 128], BF16)
    pool(lambda e: e.memset(ident[:], 0.0), w=["ident"])
    pool(lambda e: e.affine_select(out=ident[:], in_=ident[:], pattern=[[-1, 128]], compare_op=ALU.not_equal,
                                   fill=1.0, base=0, channel_multiplier=1), r=["ident"], w=["ident"])
    dve(lambda e: e.tensor_copy(out=identb[:], in_=ident[:]), r=["ident"], w=["identb"])
    dsel = T([128, 256])
    pool(lambda e: e.memset(dsel[:], 0.0), w=["dsel"])
    dve(lambda e: e.tensor_copy(out=dsel[:, 0:128], in_=ident[:]), r=["ident", "dsel"], w=["dsel"])
    tmask = T([128, 16, 16])
    pool(lambda e: e.memset(tmask[:], 1.0), w=["tmask"])
    pool(lambda e: e.affine_select(out=tmask[:], in_=tmask[:], pattern=[[16, 16], [0, 16]], compare_op=ALU.is_ge,
                                   fill=0.0, base=15, channel_multiplier=-1), r=["tmask"], w=["tmask"])
    mLa = T([1, 128], BF16); mLb = T([1, 128], BF16); mL1 = T([1, 128], BF16)
    mRa = T([1, 512], BF16); mRb = T([1, 512], BF16); mR0 = T([1, 512], BF16)
    pool(lambda e: e.memset(mLa[:], 0.0), w=["mk"]); pool(lambda e: e.memset(mLa[0:1, 0:64], 1.0), r=["mk"], w=["mk"])
    pool(lambda e: e.memset(mLb[:], 0.0), r=["mk"], w=["mk"]); pool(lambda e: e.memset(mLb[0:1, 64:128], 1.0), r=["mk"], w=["mk"])
    pool(lambda e: e.memset(mL1[:], 1.0), r=["mk"], w=["mk"])
    for hh2 in range(2):
        pool(lambda e, hh2=hh2: e.memset(mRa[0:1, hh2 * 256:hh2 * 256 + 192], 0.0), r=["mk"], w=["mk"])
        pool(lambda e, hh2=hh2: e.memset(mRa[0:1, hh2 * 256 + 192:hh2 * 256 + 256], -1e30), r=["mk"], w=["mk"])
        pool(lambda e, hh2=hh2: e.memset(mRb[0:1, hh2 * 256 + 64:hh2 * 256 + 256], 0.0), r=["mk"], w=["mk"])
        pool(lambda e, hh2=hh2: e.memset(mRb[0:1, hh2 * 256:hh2 * 256 + 64], -1e30), r=["mk"], w=["mk"])
        pool(lambda e, hh2=hh2: e.memset(mR0[0:1, hh2 * 256 + 128:hh2 * 256 + 256], 0.0), r=["mk"], w=["mk"])
        pool(lambda e, hh2=hh2: e.memset(mR0[0:1, hh2 * 256:hh2 * 256 + 128], -1e30), r=["mk"], w=["mk"])
    maskG = "maskG"; mask0 = "mask0"
    epsb = T([128, 1])
    pool(lambda e: e.memset(epsb[:], EPS), w=["epsb"])

    def range_sin(out, ang_over_2pi, shape_tmp, r, w, nparts, offs=0.0):
        t1 = shape_tmp[0]; t2 = shape_tmp[1]
        dve(lambda e: e.tensor_scalar(out=t1, in0=ang_over_2pi, scalar1=offs, scalar2=MAGIC, op0=ALU.add, op1=ALU.add),
            r=r, w=["rs_t1"])
        dve(lambda e: e.tensor_scalar(out=t1, in0=t1, scalar1=MAGIC, scalar2=None, op0=ALU.subtract), r=["rs_t1"], w=["rs_t1"])
        dve(lambda e: e.scalar_tensor_tensor(out=t2, in0=ang_over_2pi, scalar=offs, in1=t1, op0=ALU.add, op1=ALU.subtract),
            r=r + ["rs_t1"], w=["rs_t2"])
        act(lambda e: e.activation(out=out, in_=t2, func=AF.Sin, scale=TWO_PI), r=["rs_t2"], w=w)

    cosT = T([128, 17, 32]); sinT = T([128, 17, 32])
    push_scope()
    posf = T([128, 17]); invf = T([128, 32]); rang = T([128, 17, 32]); rt1 = T([128, 17, 32]); rt2 = T([128, 17, 32])
    pool(lambda e: e.iota(posf[:, 0:16], pattern=[[128, 16]], base=0, channel_multiplier=1,
                          allow_small_or_imprecise_dtypes=True), w=["posf"])
    pool(lambda e: e.iota(posf[:, 16:17], pattern=[[0, 1]], base=1024, channel_multiplier=1,
                          allow_small_or_imprecise_dtypes=True), r=["posf"], w=["posf"])
    for jj in range(32):
        pool(lambda e, jj=jj: e.memset(invf[:, jj:jj + 1], float(np.float32(np.power(np.float32(10000.0), np.float32(-jj / 32.0))))),
             r=["invf"], w=["invf"])
    dve(lambda e: e.tensor_tensor(out=rang[:], in0=posf[:, :].unsqueeze(2).broadcast_to([128, 17, 32]),
                                  in1=invf[:, :].unsqueeze(1).broadcast_to([128, 17, 32]), op=ALU.mult),
        r=["posf", "invf"], w=["rang"])
    dve(lambda e: e.tensor_scalar(out=rang[:], in0=rang[:], scalar1=1.0 / TWO_PI, scalar2=None, op0=ALU.mult), r=["rang"], w=["rang"])
    range_sin(sinT[:], rang[:], (rt1[:], rt2[:]), ["rang"], ["sinT"], 128, 0.0)
    range_sin(cosT[:], rang[:], (rt1[:], rt2[:]), ["rang"], ["cosT"], 128, 0.25)
    pop_scope()

    lngb = T([128, D]); lnbb = T([128, D])
    condPs = [T([128, 24, 4]) for _ in range(4)]; gateb = T([128, D])
    gate_d4 = dscr("gate_d", [4, 3, D])
    CUR = {}

    def load_gate(s):
        dma(gateb[:], gate_d4[CUR["l"], s].partition_broadcast(128), r=["gate_d"], w=["gateb"])

    ADA = {}

    def ada_alloc():
        ADA["cT"] = T([128, 3, 8]); ADA["cTf"] = T([128, 8, 4])
        ADA["wts"] = [T([128, 1536]) for _ in range(3)]
        ADA["bbT"] = T([128, 24]); ADA["zt"] = T([1, 128])

    def ada_all():
        cT = ADA["cT"]; cTf = ADA["cTf"]; wts = ADA["wts"]; bbT = ADA["bbT"]; zt = ADA["zt"]
        pso = psA[:, 1024:1120]
        pool(lambda e: e.memset(zt[:], 0.0), w=["zt"])
        for s in range(3):
            dma(cT[:, s, :], c_all[s].rearrange("(c p) -> p c", p=128), w=["cT"], slow=True)
        act(lambda e: e.activation(out=cT[:], in_=cT[:], func=AF.Silu), r=["cT"], w=["cT"])
        dve(lambda e: e.tensor_copy(out=cTf[:, :, 0:3], in_=cT[:, :, :].rearrange("p s c -> p c s")), r=["cT"], w=["cTf"])
        chunks = [(l, c, hf) for l in range(4) for c in range(8) for hf in range(2)]

        def load(i):
            l, c, hf = chunks[i]
            dma(wts[i % 3][:, :], w_ada[l, c * 128:(c + 1) * 128, hf * 1536:(hf + 1) * 1536], w=["wada%d" % (i % 3)],
                eng=("sp" if i % 2 == 0 else "pool"))
        load(0); load(1)
        for i, (l, c, hf) in enumerate(chunks):
            condP = condPs[l]
            if i + 2 < len(chunks):
                load(i + 2)
            yield
            wt = wts[i % 3]; wn = "wada%d" % (i % 3)
            if c == 0 and hf == 0:
                dma(bbT[:], b_ada[l].rearrange("(c p) -> p c", p=128), w=["bbT"], slow=True)
                pe(lambda e: e.matmul(pso, lhsT=zt[0:1, 0:128], rhs=zt[0:1, 0:96], start=True, stop=False, skip_group_check=True),
                   r=["zt"], w=["psA_hi"], mark=False)
            for cc in range(12):
                ch = hf * 12 + cc
                pe(lambda e, c=c, ch=ch, cc=cc, wt=wt: e.matmul(pso[:, ch * 4:ch * 4 + 3], lhsT=wt[:, cc * 128:(cc + 1) * 128],
                                                                rhs=cTf[:, c, 0:3], start=False, stop=(c == 7), skip_group_check=True),
                   r=[wn, "cTf"], w=["psA_hi"], mark=(cc == 11))
            if c == 7 and hf == 1:
                dve(lambda e, condP=condP: e.tensor_tensor(out=condP[:, :, 0:3], in0=pso.rearrange("p (a b) -> p a b", b=4)[:, :, 0:3],
                                                           in1=bbT[:, :].unsqueeze(2).broadcast_to([128, 24, 3]), op=ALU.add),
                    r=["psA_hi", "bbT"], w=["adaP%d" % l])
                dve(lambda e, condP=condP: e.tensor_scalar(out=condP[:, 8:16, 0:3], in0=condP[:, 8:16, 0:3], scalar1=1.0, scalar2=None, op0=ALU.add),
                    r=["adaP%d" % l], w=["adaP%d" % l])
                for s in range(3):
                    dma(gate_d4[l, s].rearrange("(c p) -> p c", p=128), condP[:, 16:24, s], r=["adaP%d" % l], w=["gate_d"], slow=True, eng="pool")
        yield

    def layer_ln(l):
        CUR["l"] = l; CUR["condP"] = condPs[l]
        dma(lngb[:], ln_g[l].partition_broadcast(128), w=["lngb"])
        dma(lnbb[:], ln_b[l].partition_broadcast(128), w=["lnbb"])

    Win = T([128, 8, 2048], BF16); Wg = T([128, 8, D], BF16); Wo = T([128, 8, D], BF16)

    def load_w(dst, src, name, ncols):
        v = src.rearrange("(c p) n -> p c n", p=128)
        for c in range(8):
            dma(dst[:, c, :], v[:, c, :], w=[name], eng="pool")

    xt1 = T([128, D]); xt = [xt1, xt1]
    xbt = T([128, D], BF16); hmT = T([128, 8, 128], BF16)
    t2 = T([128, D]); t1 = T([128, D]); junk = t1
    s1 = T([128, 16])
    szt = T([128, D]); sg = szt; szb = T([128, D], BF16)
    ybt = T([128, D], BF16); yT = T([128, 8, 128], BF16)

    def run_streams(tasks, ns, extra=(), stagger=0):
        free = list(range(ns))
        active = [(g, None) for g in extra]
        it = iter(tasks)
        rnd = 0
        started = 0
        exhausted = False
        while True:
            while free and not exhausted:
                if started < ns and rnd < started * stagger:
                    break
                try:
                    f = next(it)
                except StopIteration:
                    exhausted = True
                    break
                sl = free.pop(0)
                active.append((f(sl), sl))
                started += 1
            if not active:
                if exhausted or not free:
                    break
                rnd += 1
                continue
            for item in list(active):
                try:
                    next(item[0])
                except StopIteration:
                    active.remove(item)
                    if item[1] is not None:
                        free.append(item[1])
            rnd += 1

    def transpose_to(dstT, src_b, nt, rname, wname):
        for c in range(8):
            pe(lambda e, c=c: e.transpose(out=psT[:, c * 128:c * 128 + nt], in_=src_b[:nt, c * 128:(c + 1) * 128],
                                          identity=identb[:nt, :nt]), r=[rname, "identb"], w=["psT"], mark=(c == 7))
        dve(lambda e: e.tensor_copy(out=dstT[:, :, :nt], in_=psT[:, :].rearrange("p (c t) -> p c t", t=128)[:, :, :nt]),
            r=["psT"], w=[wname])

    def load_mod(src_ap, nt, slot, s, want_plain, X=None, xn="xt0"):
        if X is None:
            X = xt[slot]
            dma(X[:nt, :], src_ap, w=[xn])
        act(lambda e: e.copy(out=xbt[:nt, :], in_=X[:nt, :]), r=[xn], w=["xbt"])
        for c in range(8):
            pe(lambda e, c=c: e.transpose(out=psT[:, c * 128:c * 128 + nt], in_=xbt[:nt, c * 128:(c + 1) * 128],
                                          identity=identb[:nt, :nt]), r=["xbt", "identb"], w=["psT"], mark=(c == 7))
        for c in range(8):
            act(lambda e, c=c, cp=CUR["condP"]: e.activation(out=hmT[:, c, :nt], in_=psT[:, c * 128:c * 128 + nt], func=AF.Identity,
                                                             bias=cp[:, c, s:s + 1], scale=cp[:, 8 + c, s:s + 1]),
                r=["psT"], w=["hmT"], mark=(c == 7))
        if want_plain:
            dve(lambda e: e.tensor_copy(out=AT["xTp"][:, :, :nt], in_=psT[:, :].rearrange("p (c t) -> p c t", t=128)[:, :, :nt]),
                r=["psT"], w=["xTp"])

    def proj(ps_out, lhsT_t, rname, W, wname, nt, ncol, psname, bias_row=None):
        for n in range(ncol // 512):
            if bias_row is not None:
                pe(lambda e, n=n: e.matmul(ps_out[:nt, n * 512:(n + 1) * 512], lhsT=onesb[0:1, :nt], rhs=bias_row[0:1, n * 512:(n + 1) * 512],
                                           start=True, stop=False), r=["onesb", "bglur"],
                   w=(list(psname) if isinstance(psname, (list, tuple)) else [psname]), mark=False)
            for c in range(8):
                pe(lambda e, n=n, c=c: e.matmul(ps_out[:nt, n * 512:(n + 1) * 512], lhsT=lhsT_t[:, c, :nt],
                                                rhs=W[:, c, n * 512:(n + 1) * 512], start=(c == 0 and bias_row is None), stop=(c == 7)),
                   r=[rname, wname], w=(list(psname) if isinstance(psname, (list, tuple)) else [psname]),
                   mark=(c == 7 and n == ncol // 512 - 1))

    def resid_ln(ps_o, psname, slot, s, nt, dst_ap, wregion, X=None, xn="xt0"):
        if X is None:
            X = xt[slot]
        XO = t1; xon = "t1"
        dve(lambda e: e.tensor_tensor(out=t1[:nt, :], in0=ps_o, in1=gateb[:nt, :], op=ALU.mult), r=[psname, "gateb"], w=["t1"])
        dve(lambda e: e.scalar_tensor_tensor(out=t2[:nt, :], in0=X[:nt, :], scalar=ALPHA, in1=t1[:nt, :], op0=ALU.mult, op1=ALU.add),
            r=[xn, "t1"], w=["t2"])
        dve(lambda e: e.bn_stats(out=s1[:nt, 0:6], in_=t2[:nt, 0:512]), r=["t2"], w=["s1a"])
        dve(lambda e: e.bn_stats(out=s1[:nt, 6:12], in_=t2[:nt, 512:1024]), r=["t2", "s1a"], w=["s1a"])
        dve(lambda e: e.bn_aggr(out=s1[:nt, 12:14], in_=s1[:nt, 0:12]), r=["s1a"], w=["s1b"])
        act(lambda e: e.activation(out=s1[:nt, 14:15], in_=s1[:nt, 13:14], func=AF.Sqrt, bias=epsb[:nt, 0:1]), r=["s1b", "epsb"], w=["s1c"])
        dve(lambda e: e.reciprocal(out=s1[:nt, 15:16], in_=s1[:nt, 14:15]), r=["s1c"], w=["s1d"])
        dve(lambda e: e.scalar_tensor_tensor(out=XO[:nt, :], in0=t2[:nt, :], scalar=s1[:nt, 12:13], in1=lngb[:nt, :], op0=ALU.subtract, op1=ALU.mult),
            r=["t2", "s1b", "lngb"], w=[xon])
        dve(lambda e: e.scalar_tensor_tensor(out=XO[:nt, :], in0=XO[:nt, :], scalar=s1[:nt, 15:16], in1=lnbb[:nt, :], op0=ALU.mult, op1=ALU.add),
            r=[xon, "s1d", "lnbb"], w=[xon])
        dma(dst_ap, XO[:nt, :], r=[xon], w=[wregion], eng="pool")

    A1s = [[T([128, 32, 2]) for _ in range(3)] for _ in range(2)]; A2s = [[T([128, 32, 2]) for _ in range(3)] for _ in range(2)]
    apw = T([128, 2, 32]); apt = T([128, 2, 32])
    GEN = {}
    bglur = T([1, D], BF16); onesb = T([1, 128], BF16)
    pool(lambda e: e.memset(onesb[:], 1.0), w=["onesb"])
    S5 = {}
    DBG = {}

    def alloc_s5_seq():
        S5["Uc"] = T([128, 64, 16, 16], BF16)
        S5["UTs"] = [T([128, 2, 128], BF16) for _ in range(4)]
        S5["TXs"] = [T([128, 512], BF16) for _ in range(4)]
        S5["Gs"] = [T([128, 512], BF16) for _ in range(4)]
        S5["SH"] = T([128, 32, 2, 130])
        S5["Hbs"] = [T([128, 2, 128], BF16) for _ in range(4)]
        S5["sc1"] = T([128, 32, 2]); S5["sc2"] = T([128, 32, 2])
        S5["gt"] = T([128, D], BF16)
        S5["h3b"] = T([128, D], BF16); S5["h4b"] = T([128, D], BF16)
        S5["s1"] = [T([128, 16]) for _ in range(4)]

    GB = 4

    def gen_alloc():
        GEN['are'] = T([64, 64])
        GEN['aim'] = T([64, 64])
        GEN['dtt'] = T([64, 64])
        GEN['rho'] = T([64, 64])
        GEN['th'] = T([64, 64])
        GEN['bre_t'] = T([64, 64, 16])
        GEN['bim_t'] = T([64, 64, 16])
        GEN['craw'] = T([128, 2, 8, 64])
        GEN['cre_t'] = T([64, 64, 16])
        GEN['cim_t'] = T([64, 64, 16])
        GEN['dcol'] = T([128, 64])
        GEN['ev'] = T([64, 40])
        GEN['a16'] = T([64, 2, 64])
        GEN['a16p'] = T([128, 2, 32])
        GEN['Pr'] = T([64, GB, 40])
        GEN['Pi'] = T([64, GB, 40])
        GEN['pa'] = T([64, GB, 40])
        GEN['pt1'] = T([64, GB, 40])
        GEN['pt2'] = T([64, GB, 40])
        GEN['pm'] = T([64, GB, 40])
        GEN['q1'] = T([64, GB])
        GEN['q2'] = T([64, GB])
        GEN['q3'] = T([64, GB])
        GEN['wr'] = T([64, GB])
        GEN['wi'] = T([64, GB])
        GEN['bbr'] = T([64, GB, 16])
        GEN['bbi'] = T([64, GB, 16])
        GEN['tb1'] = T([64, GB, 16])
        GEN['Fr'] = T([64, GB, 23, 16])
        GEN['Fi'] = T([64, GB, 23, 16])
        GEN['Ft'] = T([64, GB, 23, 16])
        GEN['Gr'] = T([64, GB, 17, 16])
        GEN['Gi'] = T([64, GB, 17, 16])
        GEN['Gt'] = T([64, GB, 17, 16])
        GEN['txs'] = [T([128, 512], BF16) for i in range(2)]
        GEN['gsb'] = [T([64, 2, 256], BF16) for i in range(2)]
        GEN['tbf'] = T([128, 256])

        for nm in ('en', 'er', 'ep', 'e2', 'et', 'wA', 'wB', 'wC', 'wD', 'wE', 'wrA', 'wiA'):
            GEN[nm] = T([64, 64])

    def gen_run(la):
        are = GEN['are']
        aim = GEN['aim']
        dtt = GEN['dtt']
        rho = GEN['rho']
        th = GEN['th']
        bre_t = GEN['bre_t']
        bim_t = GEN['bim_t']
        craw = GEN['craw']
        cre_t = GEN['cre_t']
        cim_t = GEN['cim_t']
        dcol = GEN['dcol']
        ev = GEN['ev']
        a16 = GEN['a16']
        a16p = GEN['a16p']
        Pr = GEN['Pr']
        Pi = GEN['Pi']
        pa = GEN['pa']
        pt1 = GEN['pt1']
        pt2 = GEN['pt2']
        pm = GEN['pm']
        q1 = GEN['q1']
        q2 = GEN['q2']
        q3 = GEN['q3']
        wr = GEN['wr']
        wi = GEN['wi']
        bbr = GEN['bbr']
        bbi = GEN['bbi']
        tb1 = GEN['tb1']
        Fr = GEN['Fr']
        Fi = GEN['Fi']
        Ft = GEN['Ft']
        Gr = GEN['Gr']
        Gi = GEN['Gi']
        Gt = GEN['Gt']
        txs = GEN['txs']
        gsb = GEN['gsb']
        tbf = GEN['tbf']
        TXd = TXd2[la]; Gd = Gd2[la]
        R = "g_"
        dma(are[:], a_re[la].rearrange("g p -> p g"), w=[R + "are"], slow=True)
        dma(aim[:], a_im[la].rearrange("g p -> p g"), w=[R + "aim"], slow=True)
        dma(dtt[:], log_dt[la].partition_broadcast(64), w=[R + "dtt"])
        dma(bre_t[:], b_re[la].rearrange("g p c -> p g c"), w=[R + "bre"])
        dma(bim_t[:], b_im[la].rearrange("g p c -> p g c"), w=[R + "bim"])
        dma(craw[:, 0, :, :], c_re[la].rearrange("(b g) c p -> (g c) b p", g=8), w=[R + "craw"])
        dma(craw[:, 1, :, :], c_im[la].rearrange("(b g) c p -> (g c) b p", g=8), w=[R + "craw"])
        for il in range(8):
            dma(dcol[16 * il:16 * il + 16, :], ssm_d[la].rearrange("(g c) -> c g", c=16), w=[R + "dcol"], slow=True)
        for ri, dst in ((0, cre_t), (1, cim_t)):
            for b in range(8):
                pe(lambda e, ri=ri, b=b: e.transpose(out=psA[0:64, b * 128:(b + 1) * 128], in_=craw[:, ri, b, :], identity=ident[:, :]),
                   r=[R + "craw", "ident"], w=["psA_lo"], mark=(b == 7))
            dve(lambda e, dst=dst: e.tensor_copy(out=dst[:, :, :], in_=psA[0:64, 0:1024].rearrange("p (g c) -> p g c", c=16)),
                r=["psA_lo"], w=[R + "c"])
        en = GEN['en']; er = GEN['er']; ep = GEN['ep']; e2 = GEN['e2']; et = GEN['et']
        dve(lambda e: e.tensor_scalar(out=en[:], in0=dtt[:], scalar1=1.0 / math.log(2.0), scalar2=MAGIC, op0=ALU.mult, op1=ALU.add),
            r=[R + "dtt"], w=[R + "en"])
        dve(lambda e: e.tensor_scalar(out=en[:], in0=en[:], scalar1=MAGIC, scalar2=None, op0=ALU.subtract), r=[R + "en"], w=[R + "en"])
        dve(lambda e: e.scalar_tensor_tensor(out=er[:], in0=en[:], scalar=-0.693359375, in1=dtt[:], op0=ALU.mult, op1=ALU.add),
            r=[R + "en", R + "dtt"], w=[R + "er"])
        dve(lambda e: e.scalar_tensor_tensor(out=er[:], in0=en[:], scalar=2.12194440e-4, in1=er[:], op0=ALU.mult, op1=ALU.add),
            r=[R + "en", R + "er"], w=[R + "er"])
        dve(lambda e: e.tensor_scalar(out=ep[:], in0=er[:], scalar1=1.0 / 5040.0, scalar2=None, op0=ALU.mult), r=[R + "er"], w=[R + "ep"])
        for ck in (1.0 / 720.0, 1.0 / 120.0, 1.0 / 24.0, 1.0 / 6.0, 0.5, 1.0):
            dve(lambda e, ck=ck: e.scalar_tensor_tensor(out=ep[:], in0=ep[:], scalar=ck, in1=er[:], op0=ALU.add, op1=ALU.mult),
                r=[R + "ep", R + "er"], w=[R + "ep"])
        dve(lambda e: e.tensor_scalar(out=ep[:], in0=ep[:], scalar1=1.0, scalar2=None, op0=ALU.add), r=[R + "ep"], w=[R + "ep"])
        pool(lambda e: e.memset(e2[:], 0.0), w=[R + "e2"])
        for kk in range(-16, 5):
            dve(lambda e, kk=kk: e.tensor_scalar(out=et[:], in0=en[:], scalar1=float(kk), scalar2=float(2.0 ** kk), op0=ALU.is_equal, op1=ALU.mult),
                r=[R + "en"], w=[R + "et"])
            dve(lambda e: e.tensor_tensor(out=e2[:], in0=e2[:], in1=et[:], op=ALU.add), r=[R + "e2", R + "et"], w=[R + "e2"])
        dve(lambda e: e.tensor_tensor(out=dtt[:], in0=ep[:], in1=e2[:], op=ALU.mult), r=[R + "ep", R + "e2"], w=[R + "dtt"])
        dve(lambda e: e.tensor_tensor(out=rho[:], in0=are[:], in1=dtt[:], op=ALU.mult), r=[R + "are", R + "dtt"], w=[R + "rho"])
        dve(lambda e: e.tensor_tensor(out=th[:], in0=aim[:], in1=dtt[:], op=ALU.mult), r=[R + "aim", R + "dtt"], w=[R + "th"])
        dve(lambda e: e.tensor_scalar(out=th[:], in0=th[:], scalar1=1.0 / TWO_PI, scalar2=None, op0=ALU.mult), r=[R + "th"], w=[R + "th"])
        pool(lambda e: e.iota(ev[:, 0:23], pattern=[[-1, 23]], base=15, channel_multiplier=0, allow_small_or_imprecise_dtypes=True), w=[R + "ev"])
        pool(lambda e: e.iota(ev[:, 23:40], pattern=[[1, 17]], base=0, channel_multiplier=0, allow_small_or_imprecise_dtypes=True), r=[R + "ev"], w=[R + "ev"])
        evb = ev[:, :].unsqueeze(1).broadcast_to([64, GB, 40])
        wA = GEN['wA']; wB = GEN['wB']; wC = GEN['wC']; wD = GEN['wD']; wE = GEN['wE']; wrA = GEN['wrA']; wiA = GEN['wiA']
        range_sin(wB[:], th[:], (wD[:], wE[:]), [R + "th"], [R + "wB"], 64, 0.0)
        range_sin(wA[:], th[:], (wD[:], wE[:]), [R + "th"], [R + "wA"], 64, 0.25)
        act(lambda e: e.activation(out=wC[:], in_=rho[:], func=AF.Exp), r=[R + "rho"], w=[R + "wC"])
        dve(lambda e: e.tensor_tensor(out=wA[:], in0=wA[:], in1=wC[:], op=ALU.mult), r=[R + "wA", R + "wC"], w=[R + "wA"])
        dve(lambda e: e.tensor_tensor(out=wB[:], in0=wB[:], in1=wC[:], op=ALU.mult), r=[R + "wB", R + "wC"], w=[R + "wB"])
        dve(lambda e: e.tensor_scalar(out=wA[:], in0=wA[:], scalar1=-1.0, scalar2=None, op0=ALU.add), r=[R + "wA"], w=[R + "wA"])
        dve(lambda e: e.tensor_tensor(out=wC[:], in0=are[:], in1=are[:], op=ALU.mult), r=[R + "are", R + "wC"], w=[R + "wC"])
        dve(lambda e: e.tensor_tensor(out=wD[:], in0=aim[:], in1=aim[:], op=ALU.mult), r=[R + "aim"], w=[R + "wD", "rs_t1"])
        dve(lambda e: e.tensor_tensor(out=wC[:], in0=wC[:], in1=wD[:], op=ALU.add), r=[R + "wC", R + "wD"], w=[R + "wC"])
        dve(lambda e: e.reciprocal(out=wC[:], in_=wC[:]), r=[R + "wC"], w=[R + "wC"])
        dve(lambda e: e.tensor_tensor(out=wrA[:], in0=wA[:], in1=are[:], op=ALU.mult), r=[R + "wA", R + "are"], w=[R + "wrA"])
        dve(lambda e: e.tensor_tensor(out=wD[:], in0=wB[:], in1=aim[:], op=ALU.mult), r=[R + "wB", R + "aim", R + "wD"], w=[R + "wD"])
        dve(lambda e: e.tensor_tensor(out=wrA[:], in0=wrA[:], in1=wD[:], op=ALU.add), r=[R + "wrA", R + "wD"], w=[R + "wrA"])
        dve(lambda e: e.tensor_tensor(out=wrA[:], in0=wrA[:], in1=wC[:], op=ALU.mult), r=[R + "wrA", R + "wC"], w=[R + "wrA"])
        dve(lambda e: e.tensor_tensor(out=wiA[:], in0=wB[:], in1=are[:], op=ALU.mult), r=[R + "wB", R + "are"], w=[R + "wiA"])
        dve(lambda e: e.tensor_tensor(out=wD[:], in0=wA[:], in1=aim[:], op=ALU.mult), r=[R + "wA", R + "aim", R + "wD"], w=[R + "wD"])
        dve(lambda e: e.tensor_tensor(out=wiA[:], in0=wiA[:], in1=wD[:], op=ALU.subtract), r=[R + "wiA", R + "wD"], w=[R + "wiA"])
        dve(lambda e: e.tensor_tensor(out=wiA[:], in0=wiA[:], in1=wC[:], op=ALU.mult), r=[R + "wiA", R + "wC"], w=[R + "wiA"])
        for gb in range(64 // GB):
            g0 = gb * GB
            gs = slice(g0, g0 + GB)
            dve(lambda e, gs=gs: e.tensor_tensor(out=pa[:], in0=th[:, gs].unsqueeze(2).broadcast_to([64, GB, 40]), in1=evb, op=ALU.mult),
                r=[R + "th", R + "ev"], w=[R + "pa"])
            dve(lambda e, gs=gs: e.tensor_tensor(out=pm[:], in0=rho[:, gs].unsqueeze(2).broadcast_to([64, GB, 40]), in1=evb, op=ALU.mult),
                r=[R + "rho", R + "ev"], w=[R + "pm"])
            act(lambda e: e.activation(out=pm[:], in_=pm[:], func=AF.Exp), r=[R + "pm"], w=[R + "pm"])
            range_sin(Pi[:], pa[:], (pt1[:], pt2[:]), [R + "pa"], [R + "Pi"], 64, 0.0)
            range_sin(Pr[:], pa[:], (pt1[:], pt2[:]), [R + "pa"], [R + "Pr"], 64, 0.25)
            dve(lambda e: e.tensor_tensor(out=Pr[:], in0=Pr[:], in1=pm[:], op=ALU.mult), r=[R + "Pr", R + "pm"], w=[R + "Pr"])
            dve(lambda e: e.tensor_tensor(out=Pi[:], in0=Pi[:], in1=pm[:], op=ALU.mult), r=[R + "Pi", R + "pm"], w=[R + "Pi"])
            wrb = wrA[:, gs].unsqueeze(2).broadcast_to([64, GB, 16]); wib = wiA[:, gs].unsqueeze(2).broadcast_to([64, GB, 16])
            brs = bre_t[:, gs, :]; bis = bim_t[:, gs, :]
            dve(lambda e, brs=brs, wrb=wrb: e.tensor_tensor(out=bbr[:], in0=brs, in1=wrb, op=ALU.mult), r=[R + "bre", R + "wrA"], w=[R + "bbr"])
            dve(lambda e, bis=bis, wib=wib: e.tensor_tensor(out=tb1[:], in0=bis, in1=wib, op=ALU.mult), r=[R + "bim", R + "wiA"], w=[R + "tb1"])
            dve(lambda e: e.tensor_tensor(out=bbr[:], in0=bbr[:], in1=tb1[:], op=ALU.subtract), r=[R + "bbr", R + "tb1"], w=[R + "bbr"])
            dve(lambda e, bis=bis, wrb=wrb: e.tensor_tensor(out=bbi[:], in0=bis, in1=wrb, op=ALU.mult), r=[R + "bim", R + "wrA"], w=[R + "bbi"])
            dve(lambda e, brs=brs, wib=wib: e.tensor_tensor(out=tb1[:], in0=brs, in1=wib, op=ALU.mult), r=[R + "bre", R + "wiA"], w=[R + "tb1"])
            dve(lambda e: e.tensor_tensor(out=bbi[:], in0=bbi[:], in1=tb1[:], op=ALU.add), r=[R + "bbi", R + "tb1"], w=[R + "bbi"])
            dve(lambda e, gs=gs: e.tensor_copy(out=a16[:, 0, gs], in_=Pr[:, :, 39]), r=[R + "Pr", R + "a16"], w=[R + "a16"])
            dve(lambda e, gs=gs: e.tensor_copy(out=a16[:, 1, gs], in_=Pi[:, :, 39]), r=[R + "Pi", R + "a16"], w=[R + "a16"])
            PrF = Pr[:, :, 0:23].unsqueeze(3).broadcast_to([64, GB, 23, 16])
            PiF = Pi[:, :, 0:23].unsqueeze(3).broadcast_to([64, GB, 23, 16])
            bbrB = bbr[:, :, :].unsqueeze(2).broadcast_to([64, GB, 23, 16])
            bbiB = bbi[:, :, :].unsqueeze(2).broadcast_to([64, GB, 23, 16])
            dve(lambda e: e.tensor_tensor(out=Fr[:], in0=PrF, in1=bbrB, op=ALU.mult), r=[R + "Pr", R + "bbr"], w=[R + "Fr"])
            dve(lambda e: e.tensor_tensor(out=Ft[:], in0=PiF, in1=bbiB, op=ALU.mult), r=[R + "Pi", R + "bbi"], w=[R + "Ft"])
            dve(lambda e: e.tensor_tensor(out=Fr[:], in0=Fr[:], in1=Ft[:], op=ALU.subtract), r=[R + "Fr", R + "Ft"], w=[R + "Fr"])
            dve(lambda e: e.tensor_tensor(out=Fi[:], in0=PrF, in1=bbiB, op=ALU.mult), r=[R + "Pr", R + "bbi"], w=[R + "Fi"])
            dve(lambda e: e.tensor_tensor(out=Ft[:], in0=PiF, in1=bbrB, op=ALU.mult), r=[R + "Pi", R + "bbr"], w=[R + "Ft"])
            dve(lambda e: e.tensor_tensor(out=Fi[:], in0=Fi[:], in1=Ft[:], op=ALU.add), r=[R + "Fi", R + "Ft"], w=[R + "Fi"])
            PrG = Pr[:, :, 23:40].unsqueeze(3).broadcast_to([64, GB, 17, 16])
            PiG = Pi[:, :, 23:40].unsqueeze(3).broadcast_to([64, GB, 17, 16])
            crB = cre_t[:, gs, :].unsqueeze(2).broadcast_to([64, GB, 17, 16])
            ciB = cim_t[:, gs, :].unsqueeze(2).broadcast_to([64, GB, 17, 16])
            dve(lambda e, b=crB: e.tensor_tensor(out=Gr[:], in0=PrG, in1=b, op=ALU.mult), r=[R + "Pr", R + "c"], w=[R + "Gr"])
            dve(lambda e, b=ciB: e.tensor_tensor(out=Gt[:], in0=PiG, in1=b, op=ALU.mult), r=[R + "Pi", R + "c"], w=[R + "Gt"])
            dve(lambda e: e.tensor_tensor(out=Gr[:], in0=Gr[:], in1=Gt[:], op=ALU.subtract), r=[R + "Gr", R + "Gt"], w=[R + "Gr"])
            dve(lambda e, b=crB: e.tensor_tensor(out=Gi[:], in0=PiG, in1=b, op=ALU.mult), r=[R + "Pi", R + "c"], w=[R + "Gi"])
            dve(lambda e, b=ciB: e.tensor_tensor(out=Gt[:], in0=PrG, in1=b, op=ALU.mult), r=[R + "Pr", R + "c"], w=[R + "Gt"])
            dve(lambda e: e.tensor_tensor(out=Gi[:], in0=Gi[:], in1=Gt[:], op=ALU.add), r=[R + "Gi", R + "Gt"], w=[R + "Gi"])
            dve(lambda e: e.tensor_scalar(out=Gi[:], in0=Gi[:], scalar1=-1.0, scalar2=None, op0=ALU.mult), r=[R + "Gi"], w=[R + "Gi"])
            for g8 in range(GB):
                g = g0 + g8
                sl = g % 2
                fl = lambda ap: ap.rearrange("p a b -> p (a b)")
                pe(lambda e, g8=g8: e.matmul(psB[:, 0:256], lhsT=fl(Fr[:, g8, 15:23, :]), rhs=fl(Gr[:, g8, 0:16, :]), start=True, stop=False),
                   r=[R + "Fr", R + "Gr"], w=["psB"], mark=False)
                pe(lambda e, g8=g8: e.matmul(psB[:, 0:256], lhsT=fl(Fi[:, g8, 15:23, :]), rhs=fl(Gi[:, g8, 0:16, :]), start=False, stop=True),
                   r=[R + "Fi", R + "Gi"], w=["psB"])
                dve(lambda e: e.tensor_tensor(out=tbf[:], in0=psB[:, 0:256], in1=tmask[:, :, :].rearrange("p a b -> p (a b)"), op=ALU.mult),
                    r=["psB", "tmask"], w=[R + "tbf"])
                dve(lambda e, g=g, sl=sl: e.scalar_tensor_tensor(out=txs[sl][:, 0:256], in0=dsel[:], scalar=dcol[:, g:g + 1], in1=tbf[:],
                                                                 op0=ALU.mult, op1=ALU.add),
                    r=["dsel", R + "dcol", R + "tbf"], w=[R + "txs%d" % sl])
                for h in range(2):
                    for c, Fc in ((0, Fr), (1, Fi)):
                        pe(lambda e, g8=g8, h=h, c=c, Fc=Fc: e.transpose(out=psA[:, (h * 2 + c) * 64:(h * 2 + c) * 64 + 64],
                                                                          in_=fl(Fc[:, g8, 8 * h:8 * h + 8, :]), identity=ident[0:64, 0:64]),
                           r=[R + "Fr", R + "Fi", "ident"], w=["psA_lo"], mark=(h == 1 and c == 1))
                act(lambda e, sl=sl: e.copy(out=txs[sl][:, 256:512], in_=psA[:, 0:256]), r=["psA_lo", R + "txs%d" % sl], w=[R + "txs%d" % sl])
                dma(TXd[g], txs[sl][:], r=[R + "txs%d" % sl], w=["TXd%d" % g])
                pool(lambda e, g8=g8, sl=sl: e.tensor_copy(out=gsb[sl][:, 0, :], in_=fl(Gr[:, g8, 1:17, :])), r=[R + "Gr"], w=[R + "gsb%d" % sl])
                pool(lambda e, g8=g8, sl=sl: e.tensor_copy(out=gsb[sl][:, 1, :], in_=fl(Gi[:, g8, 1:17, :])), r=[R + "Gi", R + "gsb%d" % sl], w=[R + "gsb%d" % sl])
                dma(Gd[g], gsb[sl][:, :, :], r=[R + "gsb%d" % sl], w=["Gd%d" % g])
            yield
        dma(a16d.rearrange("c p g -> p c g"), a16[:], r=[R + "a16"], w=["a16d"])
        for par in range(2):
            for c in range(2):
                dma(a16p[64 * par:64 * par + 64, c, :], a16d.rearrange("c p (gh two) -> two c p gh", two=2)[par, c],
                    r=["a16d"], w=[R + "a16p"], slow=True)
        for lev in range(3):
            A1 = A1s[la][lev]; A2 = A2s[la][lev]
            dve(lambda e, A1=A1: e.tensor_copy(out=A1[:, :, :], in_=a16p[:, 0, :].unsqueeze(2).broadcast_to([128, 32, 2])), r=[R + "a16p"], w=["A1"])
            dve(lambda e, A2=A2: e.tensor_copy(out=A2[:, :, 1], in_=a16p[:, 1, :]), r=[R + "a16p"], w=["A2"])
            dve(lambda e, A2=A2: e.tensor_scalar(out=A2[:, :, 0], in0=a16p[:, 1, :], scalar1=-1.0, scalar2=None, op0=ALU.mult), r=[R + "a16p", "A2"], w=["A2"])
            if lev < 2:
                dve(lambda e: e.tensor_tensor(out=apw[:, :, :], in0=a16p[:, :, :], in1=a16p[:, :, :], op=ALU.mult), r=[R + "a16p"], w=["apw"])
                dve(lambda e: e.tensor_tensor(out=apt[:, 1, :], in0=a16p[:, 0, :], in1=a16p[:, 1, :], op=ALU.mult), r=[R + "a16p"], w=["apt"])
                dve(lambda e: e.tensor_tensor(out=a16p[:, 0, :], in0=apw[:, 0, :], in1=apw[:, 1, :], op=ALU.subtract), r=["apw", R + "a16p"], w=[R + "a16p"])
                dve(lambda e: e.tensor_scalar(out=a16p[:, 1, :], in0=apt[:, 1, :], scalar1=2.0, scalar2=None, op0=ALU.mult), r=["apt", R + "a16p"], w=[R + "a16p"])
        yield

    def s5_sequence(la, s, nk, src_tile, dst_tile, h0_fn, hout_fn, x_src_s=None, x_dst_s=None):
        stop_if = DBG["stop_if"]
        nk0 = nk
        x_s = x_src_s
        Uc = S5["Uc"]; SH = S5["SH"]
        TXd = TXd2[la]; Gd = Gd2[la]
        SHf = SH[:, :, :, :].rearrange("p a b c -> p (a b c)")
        SHb = SHf[:, 4096:8192].bitcast(BF16)
        psBb = psB.bitcast(BF16)

        def hb(j):
            return SHb[:, 1024 * j:1024 * (j + 1)]
        SB = [dict(F1=xt1, F2=t1, H1=xbt, H2=hmT, H3=szb, H4=ybt, psT=psT, psTn="psT"),
              dict(F1=t2, F2=szt, H1=S5["gt"], H2=yT, H3=S5["h3b"], H4=S5["h4b"], psT=psQ, psTn="psQ"),
              dict(F1=SHf[:, 0:1024], F2=SHf[:, 1024:2048], H1=hb(0), H2=hb(1).rearrange("p (c t) -> p c t", t=128), H3=hb(2), H4=hb(3),
                   psT=psBb[:, 0:1024], psTn="psB0"),
              dict(F1=SHf[:, 2048:3072], F2=SHf[:, 3072:4096], H1=hb(4), H2=hb(5).rearrange("p (c t) -> p c t", t=128), H3=hb(6), H4=hb(7),
                   psT=psBb[:, 1024:2048], psTn="psB1")]
        for kk in range(4):
            SB[kk]["psX"] = psA[:, 512 * kk:512 * kk + 512]; SB[kk]["psXn"] = "psA_q%d" % kk; SB[kk]["s1"] = S5["s1"][kk]

        def proj_h(B, K, W, wname, n, hf, bias_row=None):
            if bias_row is not None:
                pe(lambda e: e.matmul(B["psX"][:n, 0:512], lhsT=onesb[0:1, :n], rhs=bias_row[0:1, hf * 512:(hf + 1) * 512], start=True, stop=False),
                   r=["onesb", "bglur"], w=[B["psXn"]], mark=False)
            for c in range(8):
                pe(lambda e, c=c: e.matmul(B["psX"][:n, 0:512], lhsT=B["H2"][:, c, :n], rhs=W[:, c, hf * 512:(hf + 1) * 512],
                                           start=(c == 0 and bias_row is None), stop=(c == 7)),
                   r=["H2" + K, wname], w=[B["psXn"]], mark=(c == 7))

        allU = ["Uc%d" % g for g in range(64)]

        def tr8(B, k, src, srcn, dst, dstn, modulate, n=None):
            n = nk if n is None else n
            for c in range(8):
                pe(lambda e, c=c: e.transpose(out=B["psT"][:, c * 128:c * 128 + n], in_=src[:n, c * 128:(c + 1) * 128], identity=identb[:n, :n]),
                   r=[srcn, "identb"], w=[B["psTn"]], mark=(c == 7))
            yield
            if modulate:
                for c in range(8):
                    act(lambda e, c=c, cp=CUR["condP"]: e.activation(out=dst[:, c, :n], in_=B["psT"][:, c * 128:c * 128 + n], func=AF.Identity,
                                                                     bias=cp[:, c, s:s + 1], scale=cp[:, 8 + c, s:s + 1]),
                        r=[B["psTn"]], w=[dstn], mark=(c == 7))
            else:
                act(lambda e: e.copy(out=dst[:, :, :n], in_=B["psT"][:, :].rearrange("p (c t) -> p c t", t=128)[:, :, :n]),
                    r=[B["psTn"]], w=[dstn])
            yield

        def p1_task(i, sample=False):
            def gen(k):
                B = SB[k]; K = "_%d" % k
                n = DEC if sample else nk
                dma(B["H1"][:n, :], x_s if sample else src_tile(i), w=["H1" + K], eng="pool")
                yield
                yield from tr8(B, k, B["H1"], "H1" + K, B["H2"], "H2" + K, True, n)
                for hf in range(2):
                    proj_h(B, K, Win, "Win", n, hf)
                    yield
                    if sample:
                        act(lambda e, hf=hf: e.copy(out=B["H4"][:n, hf * 512:(hf + 1) * 512], in_=B["psX"][:n, 0:512]), r=[B["psXn"]], w=["H4" + K])
                    else:
                        dve(lambda e, hf=hf: e.tensor_copy(out=Uc[:n, 32 * hf:32 * hf + 32, i, :], in_=B["psX"][:n, 0:512].rearrange("p (g c) -> p g c", c=16)),
                            r=[B["psXn"]], w=allU[32 * hf:32 * hf + 32])
                if sample:
                    dma(ucs_d, B["H4"][:n, :], r=["H4" + K], w=["ucs_d"])
                    uv = ucs_d.rearrange("(k i) (g c) -> i k g c", i=16, c=16)
                    for ii in range(16):
                        dma(Uc[:4, :, ii, :], uv[ii], r=["ucs_d"], w=allU, slow=True)
                for hf in range(2):
                    proj_h(B, K, Win[:, :, 1024:2048], "Win", n, hf)
                    yield
                    act(lambda e, hf=hf: e.activation(out=B["H3"][:n, hf * 512:(hf + 1) * 512], in_=B["psX"][:n, 0:512], func=AF.Silu),
                        r=[B["psXn"]], w=["H3" + K])
                dma(zscr[i, :n, :], B["H3"][:n, :], r=["H3" + K], w=["zscr%d" % i])
            return gen

        def pq(k):
            pst = psT if k < 2 else psQ
            return pst, ("psT" if k < 2 else "psQ"), (k % 2) * 512, psA[:, 512 * k:512 * k + 512], "psA_q%d" % k

        def p2_task(g):
            def gen(k):
                par = g % 2; gh = g // 2; sl = k
                pst, pstn, pc0, psx, psxn = pq(k)
                dma(S5["TXs"][sl][:, :], TXd[g], r=["TXd%d" % g], w=["TXs%d" % sl])
                yield
                for h in range(2):
                    pe(lambda e, h=h: e.transpose(out=pst[:, pc0 + h * 128:pc0 + h * 128 + nk],
                                                  in_=Uc[:nk, g, 8 * h:8 * h + 8, :].rearrange("p a b -> p (a b)"), identity=identb[:nk, :nk]),
                       r=["Uc%d" % g, "identb"], w=[pstn], mark=(h == 1))
                yield
                act(lambda e: e.copy(out=S5["UTs"][sl][:, :, :nk], in_=pst[:, pc0:pc0 + 256].rearrange("p (h k) -> p h k", k=128)[:, :, :nk]),
                    r=[pstn], w=["UTs%d" % sl])
                yield
                for c in range(2):
                    for h in range(2):
                        pe(lambda e, c=c, h=h: e.matmul(psx[64 * par:64 * par + 64, c * 128:c * 128 + nk],
                                                        lhsT=S5["TXs"][sl][:, 256 + (h * 2 + c) * 64:256 + (h * 2 + c) * 64 + 64],
                                                        rhs=S5["UTs"][sl][:, h, :nk], start=(h == 0), stop=(h == 1)),
                           r=["TXs%d" % sl, "UTs%d" % sl], w=[psxn], mark=(c == 1 and h == 1))
                yield
                dve(lambda e: e.tensor_copy(out=SH[64 * par:64 * par + 64, gh, :, 1:nk + 1],
                                            in_=psx[64 * par:64 * par + 64, 0:256].rearrange("p (c k) -> p c k", k=128)[:, :, :nk]),
                    r=[psxn], w=["SH%d" % (gh // 16)])
            return gen

        def p4_task(g):
            def gen(k):
                par = g % 2; gh = g // 2; sl = k
                pst, pstn, pc0, psx, psxn = pq(k)
                dma(S5["TXs"][sl][:, 0:256], TXd[g][:, 0:256], r=["TXd%d" % g], w=["TXs%d" % sl])
                dma(S5["Gs"][sl][64 * par:64 * par + 64, :], Gd[g], r=["Gd%d" % g], w=["Gs%d" % sl], eng="act")
                act(lambda e: e.copy(out=S5["Hbs"][k][64 * par:64 * par + 64, :, :nk], in_=SH[64 * par:64 * par + 64, gh, :, 0:nk]),
                    r=["SH%d" % (gh // 16)], w=["Hb%d" % k])
                yield
                for h in range(2):
                    pe(lambda e, h=h: e.transpose(out=pst[:, pc0 + h * 128:pc0 + h * 128 + nk],
                                                  in_=Uc[:nk, g, 8 * h:8 * h + 8, :].rearrange("p a b -> p (a b)"), identity=identb[:nk, :nk]),
                       r=["Uc%d" % g, "identb"], w=[pstn], mark=(h == 1))
                yield
                act(lambda e: e.copy(out=S5["UTs"][sl][:, :, :nk], in_=pst[:, pc0:pc0 + 256].rearrange("p (h k) -> p h k", k=128)[:, :, :nk]),
                    r=[pstn], w=["UTs%d" % sl])
                yield
                o = psx[:nk, 0:256]
                pe(lambda e: e.matmul(o, lhsT=S5["UTs"][sl][:, 0, :nk], rhs=S5["TXs"][sl][:, 0:256], start=True, stop=False,
                                      skip_group_check=True), r=["UTs%d" % sl, "TXs%d" % sl], w=[psxn], mark=False)
                pe(lambda e: e.matmul(o[:, 128:256], lhsT=S5["UTs"][sl][:, 1, :nk], rhs=S5["TXs"][sl][:, 0:128], start=False, stop=False,
                                      skip_group_check=True), r=["UTs%d" % sl, "TXs%d" % sl], w=[psxn], mark=False)
                for c in range(2):
                    pe(lambda e, c=c: e.matmul(o, lhsT=S5["Hbs"][k][64 * par:64 * par + 64, c, :nk],
                                               rhs=S5["Gs"][sl][64 * par:64 * par + 64, c * 256:(c + 1) * 256],
                                               start=False, stop=(c == 1), skip_group_check=True),
                       r=["Hb%d" % k, "Gs%d" % sl], w=[psxn], mark=(c == 1))
                yield
                act(lambda e: e.activation(out=Uc[:nk, g, :, :], in_=psx[:nk, 0:256].rearrange("p (i c) -> p i c", c=16), func=AF.Gelu),
                    r=[psxn], w=["Uc%d" % g])
            return gen

        def p5_task(i, sample=False):
            def gen(k):
                B = SB[k]; K = "_%d" % k
                nk = DEC if sample else nk0
                F1 = B["F1"]; F2 = B["F2"]; s1k = B["s1"]
                dma(F1[:nk, :], x_src_s if sample else src_tile(i), w=["F1" + K])
                dma(B["H3"][:nk, :], zscr[i, :nk, :], r=["zscr%d" % i], w=["H3" + K], eng="pool")
                if sample:
                    gv = gcs_d.rearrange("(k i) (g c) -> i k g c", i=16, c=16)
                    for ii in range(16):
                        dma(gv[ii], Uc[:4, :, ii, :], r=allU, w=["gcs_d"], slow=True)
                    dma(B["H1"][:nk, :], gcs_d, r=["gcs_d"], w=["H1" + K])
                else:
                    dve(lambda e: e.tensor_copy(out=B["H1"][:nk, :].rearrange("p (g c) -> p g c", c=16), in_=Uc[:nk, :, i, :]), r=allU, w=["H1" + K])
                yield
                yield from tr8(B, k, B["H1"], "H1" + K, B["H2"], "H2" + K, False, nk)
                for hf in range(2):
                    proj_h(B, K, Wg, "Wg", nk, hf, bias_row=bglur)
                    yield
                    act(lambda e, hf=hf: e.activation(out=F2[:nk, hf * 512:(hf + 1) * 512], in_=B["psX"][:nk, 0:512], func=AF.Sigmoid),
                        r=[B["psXn"]], w=["F2" + K])
                dve(lambda e: e.tensor_tensor(out=F2[:nk, :], in0=F2[:nk, :], in1=B["H1"][:nk, :], op=ALU.mult), r=["F2" + K, "H1" + K], w=["F2" + K])
                dve(lambda e: e.tensor_tensor(out=B["H4"][:nk, :], in0=F2[:nk, :], in1=B["H3"][:nk, :], op=ALU.mult), r=["F2" + K, "H3" + K], w=["H4" + K])
                yield
                yield from tr8(B, k, B["H4"], "H4" + K, B["H2"], "H2" + K, False, nk)
                nt = nk
                for hf in range(2):
                    proj_h(B, K, Wo, "Wo", nk, hf)
                    yield
                    dve(lambda e, hf=hf: e.tensor_tensor(out=F2[:nt, hf * 512:(hf + 1) * 512], in0=B["psX"][:nt, 0:512],
                                                         in1=gateb[:nt, hf * 512:(hf + 1) * 512], op=ALU.mult),
                        r=[B["psXn"], "gateb"], w=["F2" + K])
                dve(lambda e: e.scalar_tensor_tensor(out=F1[:nt, :], in0=F1[:nt, :], scalar=ALPHA, in1=F2[:nt, :], op0=ALU.mult, op1=ALU.add),
                    r=["F1" + K, "F2" + K], w=["F1" + K])
                dve(lambda e: e.bn_stats(out=s1k[:nt, 0:6], in_=F1[:nt, 0:512]), r=["F1" + K], w=["s1" + K])
                dve(lambda e: e.bn_stats(out=s1k[:nt, 6:12], in_=F1[:nt, 512:1024]), r=["F1" + K, "s1" + K], w=["s1" + K])
                dve(lambda e: e.bn_aggr(out=s1k[:nt, 12:14], in_=s1k[:nt, 0:12]), r=["s1" + K], w=["s1b" + K])
                yield
                act(lambda e: e.activation(out=s1k[:nt, 14:15], in_=s1k[:nt, 13:14], func=AF.Sqrt, bias=epsb[:nt, 0:1]), r=["s1b" + K, "epsb"], w=["s1c" + K])
                dve(lambda e: e.reciprocal(out=s1k[:nt, 15:16], in_=s1k[:nt, 14:15]), r=["s1c" + K], w=["s1d" + K])
                dve(lambda e: e.scalar_tensor_tensor(out=F2[:nt, :], in0=F1[:nt, :], scalar=s1k[:nt, 12:13], in1=lngb[:nt, :], op0=ALU.subtract, op1=ALU.mult),
                    r=["F1" + K, "s1b" + K, "lngb"], w=["F2" + K])
                dve(lambda e: e.scalar_tensor_tensor(out=F2[:nt, :], in0=F2[:nt, :], scalar=s1k[:nt, 15:16], in1=lnbb[:nt, :], op0=ALU.mult, op1=ALU.add),
                    r=["F2" + K, "s1d" + K, "lnbb"], w=["F2" + K])
                dma(x_dst_s if sample else dst_tile(i), F2[:nt, :], r=["F2" + K], w=["dst"], eng="pool")
            return gen

        if s == 2:
            run_streams([p1_task(0, True)], 1)
        else:
            run_streams([p1_task(i) for i in range(16)], 4)
        if s == 2:
            load_w(Win, CUR["next_win"], "Win", 2048)
        sc1 = S5["sc1"]; sc2 = S5["sc2"]

        def scan_gen(hf):
            gsl = slice(16 * hf, 16 * hf + 16)
            R = "SH%d" % hf; Z = "_%d" % hf
            TA = [xt1, t2][hf]; TB = [t1, szt][hf]
            TAn = ["F1_0", "F1_1"][hf]; TBn = ["F2_0", "F2_1"][hf]

            def cmac(dk, sk, lev, cnt):
                step = 2 << lev
                A1 = A1s[la][lev]; A2 = A2s[la][lev]
                j0 = 0
                while j0 < cnt:
                    m = min(32, cnt - j0)
                    d = SH[:, gsl, :, dk + j0 * step:dk + (j0 + m - 1) * step + 1:step]
                    sr = SH[:, gsl, :, sk + j0 * step:sk + (j0 + m - 1) * step + 1:step]
                    ta = TA[:, 0:16 * 2 * m].rearrange("p (g c m) -> p g c m", g=16, c=2)
                    tb = TB[:, 0:16 * 2 * m].rearrange("p (g c m) -> p g c m", g=16, c=2)
                    a1b = A1[:, gsl, :].unsqueeze(3).broadcast_to([128, 16, 2, m])
                    dve(lambda e, ta=ta, sr=sr, a1b=a1b: e.tensor_tensor(out=ta, in0=sr, in1=a1b, op=ALU.mult), r=[R, "A1"], w=[TAn])
                    pool(lambda e, tb=tb, sr=sr, A2=A2, m=m: e.tensor_tensor(out=tb[:, :, 0, :], in0=sr[:, :, 1, :],
                                                                             in1=A2[:, gsl, 0:1].broadcast_to([128, 16, m]), op=ALU.mult),
                         r=[R, "A2"], w=[TBn + "a"])
                    pool(lambda e, tb=tb, sr=sr, A2=A2, m=m: e.tensor_tensor(out=tb[:, :, 1, :], in0=sr[:, :, 0, :],
                                                                             in1=A2[:, gsl, 1:2].broadcast_to([128, 16, m]), op=ALU.mult),
                         r=[R, "A2"], w=[TBn + "b"])
                    dve(lambda e, ta=ta, tb=tb: e.tensor_tensor(out=ta, in0=ta, in1=tb, op=ALU.add), r=[TAn, TBn + "a", TBn + "b"], w=[TAn])
                    dve(lambda e, ta=ta, d=d: e.tensor_tensor(out=d, in0=d, in1=ta, op=ALU.add), r=[TAn, R], w=[R])
                    j0 += m
                    yield

            yield from cmac(2, 1, 0, nk // 2)
            yield from cmac(4, 2, 1, nk // 4)
            A1 = A1s[la][2]; A2 = A2s[la][2]
            for k in range(0, nk, 4):
                v0 = SH[:, gsl, :, k]; v1 = SH[:, gsl, :, k + 4]
                dve(lambda e, v0=v0: e.tensor_tensor(out=sc1[:, gsl, :], in0=A1[:, gsl, :], in1=v0, op=ALU.mult), r=[R, "A1"], w=["sc1" + Z])
                pool(lambda e, k=k: e.tensor_tensor(out=sc2[:, gsl, 0], in0=A2[:, gsl, 0], in1=SH[:, gsl, 1, k], op=ALU.mult), r=[R, "A2"], w=["sc2a" + Z])
                pool(lambda e, k=k: e.tensor_tensor(out=sc2[:, gsl, 1], in0=A2[:, gsl, 1], in1=SH[:, gsl, 0, k], op=ALU.mult), r=[R, "A2"], w=["sc2b" + Z])
                dve(lambda e: e.tensor_tensor(out=sc1[:, gsl, :], in0=sc1[:, gsl, :], in1=sc2[:, gsl, :], op=ALU.add),
                    r=["sc1" + Z, "sc2a" + Z, "sc2b" + Z], w=["sc1" + Z])
                dve(lambda e, v1=v1: e.tensor_tensor(out=v1, in0=v1, in1=sc1[:, gsl, :], op=ALU.add), r=["sc1" + Z, R], w=[R])
                if (k // 4) % 2 == 1:
                    yield
            yield from cmac(2, 0, 1, nk // 4)
            yield from cmac(1, 0, 0, nk // 2)

        P.barrier()
        h0_fn()
        run_streams([p2_task(g) for g in range(32)], 4)
        run_streams([p2_task(g) for g in range(32, 64)], 4, extra=[scan_gen(0)])
        run_streams([p4_task(g) for g in range(32)], 4, extra=[scan_gen(1)])
        hout_fn()
        run_streams([p4_task(g) for g in range(32, 64)], 4)
        P.barrier()
        if s == 2:
            run_streams([p5_task(0, True)], 1)
        else:
            run_streams([p5_task(i) for i in range(16)], 4)

    def s5_layer(la, l, src, dst):
        stop_if = DBG["stop_if"]
        layer_ln(l)
        if la > 0:
            load_w(Wg, w_glu[la], "Wg", 1024)
            load_w(Wo, w_out_a[la], "Wo", 1024)
            dma(bglur[0:1, :], b_glu[la:la + 1, :], w=["bglur"], eng="pool")
        CUR["next_win"] = w_in_a[1] if la == 0 else w_in_b[0]
        push_scope()
        alloc_s5_seq()
        for s in range(3):
            load_gate(s)
            if s < 2:
                nk = 128
                sv = src[0][s].rearrange("(k i) d -> i k d", i=16)
                dv = dst[0][s].rearrange("(k i) d -> i k d", i=16)
                def h0_fn():
                    pool(lambda e: e.memset(S5["SH"][:, :, :, 0], 0.0), w=["SH0", "SH1"])
                def hout_fn(s=s):
                    for par in range(2):
                        for c in range(2):
                            dma(ssm_p[la, s].rearrange("(gh two) p c -> two c p gh", two=2)[par, c], S5["SH"][64 * par:64 * par + 64, :, c, 128],
                                r=["SH0", "SH1"], w=["ssm_out"], slow=True)
            else:
                nk = 4
                sv = src[1].rearrange("(k i) d -> i k d", i=16)
                dv = dst[1].rearrange("(k i) d -> i k d", i=16)
                def h0_fn():
                    for par in range(2):
                        for c in range(2):
                            dma(S5["SH"][64 * par:64 * par + 64, :, c, 0], st_in[la].rearrange("(gh two) p c -> two c p gh", two=2)[par, c],
                                w=["SH0", "SH1"], slow=True)
                def hout_fn():
                    for par in range(2):
                        for c in range(2):
                            dma(ssm_s[la].rearrange("(gh two) p c -> two c p gh", two=2)[par, c], S5["SH"][64 * par:64 * par + 64, :, c, 4],
                                r=["SH0", "SH1"], w=["ssm_out"], slow=True)
            s5_sequence(la, s, nk, lambda i, sv=sv: sv[i], lambda i, dv=dv: dv[i], h0_fn, hout_fn, src[1], dst[1])
        pop_scope()

    AT = {}

    def alloc_attn():
        AT["kf"] = T([128, 256]); AT["vf"] = T([128, 256]); AT["kb"] = T([128, 256], BF16)
        AT["kTs"] = [T([64, 4, 128], BF16) for _ in range(2)]
        AT["vbs"] = [T([128, 256], BF16) for _ in range(2)]
        AT["qb"] = T([128, D], BF16); AT["qT"] = T([64, 16, 128], BF16)
        AT["r1"] = T([128, 16, 32]); AT["r2"] = T([128, 16, 32])
        AT["sm"] = [T([128, 4, 256]) for _ in range(2)]; AT["eb"] = [T([128, 4, 256], BF16) for _ in range(2)]
        AT["eT"] = [T([128, 8, 128], BF16) for _ in range(2)]
        AT["st"] = [T([128, 20]) for _ in range(2)]; AT["rinv"] = T([128, 16])
        AT["xt"] = [T([128, D]) for _ in range(2)]; AT["sinkb"] = T([128, 16]); AT["nsinkb"] = T([128, 16])
        AT["ogb"] = T([128, D], BF16)
        AT["ckc"] = T([128, 256]); AT["ckb"] = T([128, 256], BF16)
        AT["xTp"] = T([128, 8, 128], BF16); AT["Wkv"] = T([128, 8, 512], BF16)

    def rope(src_ps, psname, nh, nt, j, out_ap, wname):
        sv = src_ps.rearrange("p (h d) -> p h d", d=64)
        ov = out_ap.rearrange("p (h d) -> p h d", d=64)
        cb = cosT[:nt, j:j + 1, :].broadcast_to([nt, nh, 32]); sb = sinT[:nt, j:j + 1, :].broadcast_to([nt, nh, 32])
        dve(lambda e: e.tensor_tensor(out=AT["r1"][:nt, :nh, :], in0=sv[:, :, 0:32], in1=cb, op=ALU.mult), r=[psname, "cosT"], w=["r1"])
        dve(lambda e: e.tensor_tensor(out=AT["r2"][:nt, :nh, :], in0=sv[:, :, 32:64], in1=sb, op=ALU.mult), r=[psname, "sinT"], w=["r2"])
        dve(lambda e: e.tensor_tensor(out=ov[:, :, 0:32], in0=AT["r1"][:nt, :nh, :], in1=AT["r2"][:nt, :nh, :], op=ALU.subtract), r=["r1", "r2"], w=[wname])
        dve(lambda e: e.tensor_tensor(out=AT["r1"][:nt, :nh, :], in0=sv[:, :, 32:64], in1=cb, op=ALU.mult), r=[psname, "cosT"], w=["r1"])
        dve(lambda e: e.tensor_tensor(out=AT["r2"][:nt, :nh, :], in0=sv[:, :, 0:32], in1=sb, op=ALU.mult), r=[psname, "sinT"], w=["r2"])
        dve(lambda e: e.tensor_tensor(out=ov[:, :, 32:64], in0=AT["r1"][:nt, :nh, :], in1=AT["r2"][:nt, :nh, :], op=ALU.add), r=["r1", "r2", wname], w=[wname])

    def kT_from(kb_t, rname, nt, slot):
        for h in range(4):
            pe(lambda e, h=h: e.transpose(out=psT[0:64, h * 128:h * 128 + nt], in_=kb_t[:nt, h * 64:(h + 1) * 64], identity=identb[:nt, :nt]),
               r=[rname, "identb"], w=["psT"], mark=(h == 3))
        dve(lambda e: e.tensor_copy(out=AT["kTs"][slot][:, :, :nt], in_=psT[0:64, 0:512].rearrange("p (h t) -> p h t", t=128)[:, :, :nt]),
            r=["psT"], w=["kT%d" % slot])

    def attn_pro(first_attn, s, nt, X, xn, cur, rope_j, t0, kv_out):
        act(lambda e: e.copy(out=xbt[:nt, :], in_=X[:nt, :]), r=[xn], w=["xbt"])
        yield
        for c in range(8):
            pe(lambda e, c=c: e.transpose(out=psT[:, c * 128:c * 128 + nt], in_=xbt[:nt, c * 128:(c + 1) * 128],
                                          identity=identb[:nt, :nt]), r=["xbt", "identb"], w=["psT"], mark=(c == 7))
        yield
        for c in range(8):
            act(lambda e, c=c, cp=CUR["condP"]: e.activation(out=hmT[:, c, :nt], in_=psT[:, c * 128:c * 128 + nt], func=AF.Identity,
                                                             bias=cp[:, c, s:s + 1], scale=cp[:, 8 + c, s:s + 1]),
                r=["psT"], w=["hmT"], mark=(c == 7))
        if first_attn:
            dve(lambda e: e.tensor_copy(out=AT["xTp"][:, :, :nt], in_=psT[:, :].rearrange("p (c t) -> p c t", t=128)[:, :, :nt]),
                r=["psT"], w=["xTp"])
        yield
        if first_attn:
            proj(psA[:, 1024:2048], AT["xTp"], "xTp", AT["Wkv"], "Wkv", nt, 512, ["psA_hi"])
            yield
            rope(psA[:nt, 1024:1280], "psA_hi", 4, nt, rope_j, AT["kf"][:nt, :], "kf")
            act(lambda e: e.copy(out=AT["vf"][:nt, :], in_=psA[:nt, 1280:1536]), r=["psA_hi"], w=["vf"])
            act(lambda e: e.copy(out=AT["kb"][:nt, :], in_=AT["kf"][:nt, :]), r=["kf"], w=["kb"])
            dve(lambda e: e.tensor_copy(out=AT["vbs"][cur][:nt, :], in_=AT["vf"][:nt, :]), r=["vf"], w=["vb%d" % cur])
            yield
            kT_from(AT["kb"], "kb", nt, cur)
            dma(ktscr[s, :, :, t0:t0 + nt], AT["kTs"][cur][:, :, :nt], r=["kT%d" % cur], w=["ktscr"], eng="pool")
            dma(vscr[s, t0:t0 + nt, :], AT["vbs"][cur][:nt, :], r=["vb%d" % cur], w=["vscr"], eng="pool")
            if kv_out is not None:
                dma(kv_out[0], AT["kf"][:nt, :], r=["kf"], w=["kvout"], eng="pool")
                dma(kv_out[1], AT["vf"][:nt, :], r=["vf"], w=["kvout"], eng="pool")
            yield
        else:
            dma(AT["kTs"][cur][:, :, :nt], ktscr[s, :, :, t0:t0 + nt], w=["kT%d" % cur])
            dma(AT["vbs"][cur][:nt, :], vscr[s, t0:t0 + nt, :], w=["vb%d" % cur])
        proj(psA, hmT, "hmT", AT["Win"], AT["Winn"], nt, 1024, ["psA_lo"])
        yield
        proj(psA[:, 1024:2048], hmT, "hmT", AT["Win"][:, :, 1024:2048], AT["Winn"], nt, 1024, ["psA_hi"])
        yield
        rope(psA[:nt, 0:1024], "psA_lo", 16, nt, rope_j, AT["qb"][:nt, :], "qb")
        act(lambda e: e.activation(out=szt[:nt, :], in_=psA[:nt, 1024:2048], func=AF.Silu), r=["psA_hi"], w=["szt"])
        yield
        for half, (pst, pname) in enumerate(((psT, "psT"), (psQ, "psQ"))):
            for hh in range(8):
                h = half * 8 + hh
                pe(lambda e, h=h, hh=hh, pst=pst: e.transpose(out=pst[0:64, hh * 128:hh * 128 + nt], in_=AT["qb"][:nt, h * 64:(h + 1) * 64],
                                                              identity=identb[:nt, :nt]),
                   r=["qb", "identb"], w=[pname], mark=(hh == 7))
            yield
            dve(lambda e, half=half, pst=pst: e.tensor_copy(out=AT["qT"][:, half * 8:half * 8 + 8, :nt],
                                                            in_=pst[0:64, :].rearrange("p (h t) -> p h t", t=128)[:, :, :nt]),
                r=[pname], w=["qT"])
        yield

    def attn_epi(s, nt, X, xn, dst_ap):
        dve(lambda e: e.tensor_tensor(out=t1[:nt, :].rearrange("p (h d) -> p h d", d=64), in0=psA[:nt, 0:1024].rearrange("p (h d) -> p h d", d=64),
                                      in1=AT["rinv"][:nt, :].unsqueeze(2).broadcast_to([nt, 16, 64]), op=ALU.mult), r=["psA_lo", "rinv"], w=["t1"])
        dve(lambda e: e.tensor_tensor(out=AT["ogb"][:nt, :], in0=t1[:nt, :], in1=szt[:nt, :], op=ALU.mult), r=["t1", "szt"], w=["ogb"])
        yield
        for c in range(8):
            pe(lambda e, c=c: e.transpose(out=psQ[:, c * 128:c * 128 + nt], in_=AT["ogb"][:nt, c * 128:(c + 1) * 128],
                                          identity=identb[:nt, :nt]), r=["ogb", "identb"], w=["psQ"], mark=(c == 7))
        yield
        dve(lambda e: e.tensor_copy(out=yT[:, :, :nt], in_=psQ[:, :].rearrange("p (c t) -> p c t", t=128)[:, :, :nt]), r=["psQ"], w=["yT"])
        yield
        proj(psB, yT, "yT", AT["Wo"], AT["Won"], nt, 1024, ["psB"])
        yield
        resid_ln(psB[:nt, 0:1024], "psB", 0, s, nt, dst_ap, "dst", X=X, xn=xn)
        yield

    def attn_heads(nt, prev, cur, mask, mname):
        nkeys = 128 + nt

        def hg_task(hg):
            def gen(sl):
                kvh = hg
                ps_s = psB if sl == 0 else psA[:, 1024:2048]
                psn = "psB" if sl == 0 else "psA_hi"
                ps_e = psT if sl == 0 else psQ
                pen = "psT" if sl == 0 else "psQ"
                sm = AT["sm"][sl]; eb = AT["eb"][sl]; eT = AT["eT"][sl]; st = AT["st"][sl]
                S = "_%d" % sl
                has_mask = mask is not None
                if has_mask:
                    pairs = ((mLa, mRa), (mLb, mRb)) if mname == "maskG" else ((mLa, mRa), (mL1, mR0))
                    for half in range(2):
                        for pi, (ml, mr) in enumerate(pairs):
                            pe(lambda e, half=half, ml=ml, mr=mr, pi=pi: e.matmul(ps_s[:nt, half * 512:(half + 1) * 512], lhsT=ml[0:1, :nt], rhs=mr[0:1, :],
                                                                                   start=(pi == 0), stop=False, skip_group_check=True),
                               r=["mk"], w=[psn], mark=False)
                for hh in range(4):
                    h = 4 * hg + hh
                    pe(lambda e, h=h, hh=hh: e.matmul(ps_s[:nt, hh * 256:hh * 256 + 128], lhsT=AT["qT"][:, h, :nt], rhs=AT["kTs"][prev][:, kvh, :],
                                                      start=(not has_mask), stop=(not has_mask), skip_group_check=True), r=["qT", "kT%d" % prev], w=[psn], mark=False)
                    pe(lambda e, h=h, hh=hh: e.matmul(ps_s[:nt, hh * 256 + 128:hh * 256 + 128 + nt], lhsT=AT["qT"][:, h, :nt],
                                                      rhs=AT["kTs"][cur][:, kvh, :nt], start=(not has_mask), stop=True, skip_group_check=True),
                       r=["qT", "kT%d" % cur], w=[psn], mark=(hh == 3))
                yield
                psv = ps_s[:nt, 0:1024].rearrange("p (h k) -> p h k", k=256)[:, :, :nkeys]
                dve(lambda e: e.reduce_max(out=st[:nt, 0:4], in_=psv, axis=AX.X), r=[psn], w=["st" + S])
                dve(lambda e: e.scalar_tensor_tensor(out=st[:nt, 4:8], in0=st[:nt, 0:4], scalar=-0.125, in1=AT["nsinkb"][:nt, 4 * hg:4 * hg + 4],
                                                     op0=ALU.mult, op1=ALU.min), r=["st" + S, "sinkb"], w=["st" + S])
                yield
                for hh in range(4):
                    act(lambda e, hh=hh: e.activation(out=eb[:nt, hh, :nkeys], in_=ps_s[:nt, hh * 256:hh * 256 + nkeys], func=AF.Exp, scale=0.125,
                                                      bias=st[:nt, 4 + hh:5 + hh], accum_out=st[:nt, 8 + hh:9 + hh]),
                        r=[psn, "st" + S], w=["eb" + S, "stb" + S], mark=(hh == 3))
                dve(lambda e: e.tensor_tensor(out=st[:nt, 12:16], in0=AT["sinkb"][:nt, 4 * hg:4 * hg + 4], in1=st[:nt, 4:8], op=ALU.add),
                    r=["sinkb", "st" + S], w=["stc" + S])
                act(lambda e: e.activation(out=st[:nt, 12:16], in_=st[:nt, 12:16], func=AF.Exp), r=["stc" + S], w=["stc" + S])
                dve(lambda e: e.tensor_tensor(out=st[:nt, 16:20], in0=st[:nt, 8:12], in1=st[:nt, 12:16], op=ALU.add),
                    r=["stb" + S, "stc" + S], w=["std" + S])
                dve(lambda e: e.reciprocal(out=AT["rinv"][:nt, 4 * hg:4 * hg + 4], in_=st[:nt, 16:20]), r=["std" + S], w=["rinv"])
                yield
                for hh in range(4):
                    pe(lambda e, hh=hh: e.transpose(out=ps_e[:, (2 * hh) * 128:(2 * hh) * 128 + nt], in_=eb[:nt, hh, 0:128], identity=identb[:nt, :nt]),
                       r=["eb" + S, "identb"], w=[pen], mark=False)
                    pe(lambda e, hh=hh: e.transpose(out=ps_e[:nt, (2 * hh + 1) * 128:(2 * hh + 1) * 128 + nt], in_=eb[:nt, hh, 128:128 + nt],
                                                    identity=identb[:nt, :nt]),
                       r=["eb" + S, "identb"], w=[pen], mark=(hh == 3))
                yield
                act(lambda e: e.copy(out=eT[:, :, :nt], in_=ps_e[:, 0:1024].rearrange("p (b t) -> p b t", t=128)[:, :, :nt]), r=[pen], w=["eT" + S])
                yield
                for hh in range(4):
                    h = 4 * hg + hh
                    pe(lambda e, h=h, hh=hh: e.matmul(psA[:nt, h * 64:(h + 1) * 64], lhsT=eT[:, 2 * hh, :nt], rhs=AT["vbs"][prev][:, kvh * 64:(kvh + 1) * 64],
                                                      start=True, stop=False, skip_group_check=True), r=["eT" + S, "vb%d" % prev], w=["psA_lo"], mark=False)
                    pe(lambda e, h=h, hh=hh: e.matmul(psA[:nt, h * 64:(h + 1) * 64], lhsT=eT[:nt, 2 * hh + 1, :nt],
                                                      rhs=AT["vbs"][cur][:nt, kvh * 64:(kvh + 1) * 64],
                                                      start=False, stop=True, skip_group_check=True), r=["eT" + S, "vb%d" % cur], w=["psA_lo"], mark=(hh == 3))
            return gen

        run_streams([hg_task(hg) for hg in range(4)], 2)

    def run_gen(g):
        for _ in g:
            pass

    def attn_layer(lb, l, src, dst):
        first = (lb == 0)
        layer_ln(l)
        if first:
            push_scope()
            alloc_attn()
            AT["WinB"] = T([128, 8, 2048], BF16)
            load_w(Wo, w_out_b[0], "Wo", 1024)
            load_w(AT["Wkv"], w_kv, "Wkv", 512)
            load_w(AT["WinB"], w_in_b[1], "WinB", 2048)
            AT["Win"] = Win; AT["Wo"] = Wo; AT["Winn"] = "Win"; AT["Won"] = "Wo"
        else:
            load_w(Wo, w_out_b[1], "Wo", 1024)
            AT["Win"] = AT["WinB"]; AT["Wo"] = Wo; AT["Winn"] = "WinB"; AT["Won"] = "Wo"
        dma(AT["sinkb"][:], sinks[lb].partition_broadcast(128), w=["sinkb"])
        dve(lambda e: e.tensor_scalar(out=AT["nsinkb"][:], in0=AT["sinkb"][:], scalar1=-1.0, scalar2=None, op0=ALU.mult), r=["sinkb"], w=["sinkb"])
        tiles = []
        for s in range(2):
            for j in range(16):
                cur = j % 2
                tiles.append(dict(s=s, j=j, nt=128, sap=src[0][s, 128 * j:128 * (j + 1), :], dap=dst[0][s, 128 * j:128 * (j + 1), :],
                                  cur=cur, prev=1 - cur, mask=(mask0 if j == 0 else maskG), mname=("mask0" if j == 0 else "maskG"),
                                  rope_j=j, t0=128 * j, kv_out=((ck_p[s], cv_p[s]) if (first and j == 15) else None)))
        tiles.append(dict(s=2, j=0, nt=64, sap=src[1][:, :], dap=dst[1][:, :], cur=0, prev=1, mask=None, mname=None, rope_j=16, t0=0,
                          kv_out=((ck_s[:, :], cv_s[:, :]) if first else None)))
        XT = AT["xt"]

        def setup_seq(t):
            if t["s"] < 2:
                pool(lambda e: e.memset(AT["kTs"][1][:], 0.0), w=["kT1"])
                pool(lambda e: e.memset(AT["vbs"][1][:], 0.0), w=["vb1"])
            else:
                dma(AT["ckc"][:], ck_in, w=["ckc"])
                act(lambda e: e.copy(out=AT["ckb"][:], in_=AT["ckc"][:]), r=["ckc"], w=["ckb"])
                kT_from(AT["ckb"], "ckb", 128, 1)
                dma(AT["ckc"][:], cv_in, w=["ckc"])
                dve(lambda e: e.tensor_copy(out=AT["vbs"][1][:], in_=AT["ckc"][:]), r=["ckc"], w=["vb1"])

        def pro_of(idx):
            t = tiles[idx]
            return attn_pro(first, t["s"], t["nt"], XT[idx % 2], "axt%d" % (idx % 2), t["cur"], t["rope_j"], t["t0"], t["kv_out"])

        dma(XT[0][:128, :], tiles[0]["sap"], w=["axt0"])
        setup_seq(tiles[0])
        run_gen(pro_of(0))
        for idx, t in enumerate(tiles):
            xs = idx % 2
            if idx + 1 < len(tiles):
                nx = tiles[idx + 1]
                dma(XT[1 - xs][:nx["nt"], :], nx["sap"], w=["axt%d" % (1 - xs)])
            attn_heads(t["nt"], t["prev"], t["cur"], t["mask"], t["mname"])
            if t["j"] == 0:
                load_gate(t["s"])
            epi = attn_epi(t["s"], t["nt"], XT[xs], "axt%d" % xs, t["dap"])
            if idx + 1 < len(tiles):
                nx = tiles[idx + 1]
                if nx["j"] == 0:
                    setup_seq(nx)
                run_streams([], 1, extra=[epi, pro_of(idx + 1)])
            else:
                run_gen(epi)
        if not first:
            pop_scope()
        else:
            P.barrier()

    def dbg_dump(name, src_ap, shape, regions):
        o = nc.dram_tensor("dbg_" + name, list(shape), src_ap.dtype if hasattr(src_ap, "dtype") else F32, kind="ExternalOutput").ap()
        dma(o, src_ap, r=regions, w=["dbg_" + name], slow=True)

    def stop_if(tag, dumps):
        if DEBUG["stop"] == tag:
            P.barrier()
            for name, ap, shape in dumps():
                dbg_dump(name, ap, shape, [])
            raise _Stop()

    DBG["stop_if"] = stop_if
    try:
        load_w(Win, w_in_a[0], "Win", 2048)
        load_w(Wg, w_glu[0], "Wg", 1024)
        load_w(Wo, w_out_a[0], "Wo", 1024)
        dma(bglur[0:1, :], b_glu[0:1, :], w=["bglur"], eng="pool")
        push_scope()
        ada_alloc(); gen_alloc()

        def gen_both():
            yield from gen_run(0)
            yield from gen_run(1)
        run_streams([], 1, extra=[ada_all(), gen_both()])
        pop_scope()
        stop_if("S", lambda: [("condP0", condPs[0][:], [128, 24, 4]), ("condP3", condPs[3][:], [128, 24, 4]), ("gate_d", gate_d4, [4, 3, D]),
                              ("TXd", TXd2[0], [64, 128, 512]), ("Gd", Gd2[0], [64, 64, 512]),
                              ("A1", A1s[0][:], [128, 32, 2]), ("A2", A2s[0][:], [128, 32, 2]),
                              ("TXd1", TXd2[1], [64, 128, 512])])
        s5_layer(0, 0, (x_p, x_s), (xa_p, xa_s))
        stop_if("L0", lambda: [("xa_p", xa_p, [2, SEQ, D]), ("xa_s", xa_s, [DEC, D])])
        s5_layer(1, 1, (xa_p, xa_s), (xb_p, xb_s))
        stop_if("L1", lambda: [("xb_p", xb_p, [2, SEQ, D]), ("xb_s", xb_s, [DEC, D])])
        attn_layer(0, 2, (xb_p, xb_s), (xa_p, xa_s))
        stop_if("L2", lambda: [("xa_p", xa_p, [2, SEQ, D]), ("xa_s", xa_s, [DEC, D])])
        attn_layer(1, 3, (xa_p, xa_s), (y_p, y_s))
    except _Stop:
        pass
    P.barrier(["sp"])
    P.replay()
    P.close()
    return nc


_NC = None


def kernel(**inputs):
    global _NC
    f = lambda a: np.ascontiguousarray(np.asarray(a, dtype=np.float32))
    inp = {k: f(v) for k, v in inputs.items()}
    if _NC is None:
        _NC = build_nc()
    nc = _NC
    wnames = ["w_ada", "b_ada", "ln_g", "ln_b", "w_in_a", "ssm_a_re", "ssm_a_im", "ssm_b_re", "ssm_b_im", "ssm_c_re",
              "ssm_c_im", "ssm_d", "ssm_log_dt", "w_glu", "b_glu", "w_out_a", "w_kv", "w_in_b", "attn_sinks", "w_out_b"]
    in_maps = []
    for c in range(NCORES):
        m = {k: inp[k] for k in wnames}
        m["x_p"] = f(inp["x_prompt"][2 * c:2 * c + 2])
        m["x_s"] = f(inp["x_sample"][c])
        m["st_in"] = f(inp["state_ssm"][:, c])
        m["ck_in"] = f(inp["cache_k"][c].reshape(128, 256))
        m["cv_in"] = f(inp["cache_v"][c].reshape(128, 256))
        m["c_all"] = f(np.stack([inp["c_prompt"][2 * c], inp["c_prompt"][2 * c + 1], inp["c_sample"][c]]))
        in_maps.append(m)
    res = run_bass_kernel_spmd(nc, in_maps, core_ids=list(range(NCORES)))
    R = res.results
    y_prompt = np.concatenate([r["y_p"] for r in R], axis=0)
    y_sample = np.stack([r["y_s"] for r in R], axis=0)
    ssm_pp = np.concatenate([r["ssm_p"] for r in R], axis=1)
    ckp = np.concatenate([r["ck_p"] for r in R], axis=0).reshape(16, 128, 4, 64)
    cvp = np.concatenate([r["cv_p"] for r in R], axis=0).reshape(16, 128, 4, 64)
    ssm_ss = np.stack([r["ssm_s"] for r in R], axis=1)
    cks = np.stack([r["ck_s"] for r in R], axis=0).reshape(8, 64, 4, 64)
    cvs = np.stack([r["cv_s"] for r in R], axis=0).reshape(8, 64, 4, 64)
    return (y_prompt.astype(np.float32), y_sample.astype(np.float32), ssm_pp.astype(np.float32), ckp.astype(np.float32),
            cvp.astype(np.float32), ssm_ss.astype(np.float32), cks.astype(np.float32), cvs.astype(np.float32))
```

```python
import math
import numpy as np
import concourse.bass as bass
import concourse.mybir as mybir
from concourse.bass_utils import run_bass_kernel_spmd

F32 = mybir.dt.float32
BF16 = mybir.dt.bfloat16
AF = mybir.ActivationFunctionType
ALU = mybir.AluOpType
AX = mybir.AxisListType

D = 1024
SEQ = 2048
DEC = 64
NCORES = 8
ALPHA = (2.0 * 4) ** 0.25
EPS = 1e-5
MAGIC = 12582912.0
TWO_PI = 2.0 * math.pi


class Prog:
    ENGS = ("pe", "act", "dve", "pool", "sp")

    def __init__(self, nc):
        self.nc = nc
        self.ops = {e: [] for e in self.ENGS}
        self.cur = {}
        self.waited = {e: {} for e in self.ENGS}
        self.lastw = {}
        self.reads = {}
        self.pending = {e: ([], []) for e in self.ENGS}
        self.dma_pool = []
        self.dma_rr = 0
        self.swdge_pool = []
        self.swdge_rr = 0
        self.nsem = 0
        self.stack = []

    def new_sem(self):
        cm = self.nc.semaphore("s%d" % self.nsem)
        self.nsem += 1
        s = cm.__enter__()
        self.stack.append(cm)
        return s

    def setup(self, n_dma=32):
        for e in self.ENGS:
            self.cur[e] = [self.new_sem(), 0]
        for _ in range(n_dma):
            self.dma_pool.append([self.new_sem(), 0])
        for _ in range(24):
            self.swdge_pool.append([self.new_sem(), 0])

    def _need(self, eng, ev, waits):
        if ev is None:
            return
        sem, val, src = ev
        if src == eng and eng == "pe":
            return
        k = id(sem)
        if self.waited[eng].get(k, (None, 0))[1] >= val:
            return
        self.waited[eng][k] = (sem, val)
        waits.append((sem, val))

    @staticmethod
    def _best(waits):
        best = {}
        for sem, val in waits:
            k = id(sem)
            if k not in best or best[k][1] < val:
                best[k] = (sem, val)
        return list(best.values())

    def op(self, eng, fn, reads=(), writes=(), mark=True, dma=False):
        al = {"psA_lo": ("psA_q0", "psA_q1"), "psA_hi": ("psA_q2", "psA_q3"), "psB": ("psB0", "psB1")}
        reads = [x for r in reads for x in al.get(r, (r,))]
        writes = [x for r in writes for x in al.get(r, (r,))]
        writes = list(writes) + [r for r in reads if r.startswith("ps")]
        reads = [r for r in reads if not r.startswith("ps")]
        waits = []
        for r in reads:
            self._need(eng, self.lastw.get(r), waits)
        for w in writes:
            self._need(eng, self.lastw.get(w), waits)
            for ev in self.reads.get(w, ()):
                self._need(eng, ev, waits)
        pr, pw = self.pending[eng]
        if dma:
            if eng == "pool":
                slot = self.swdge_pool[self.swdge_rr % len(self.swdge_pool)]
                self.swdge_rr += 1
            else:
                slot = self.dma_pool[self.dma_rr % len(self.dma_pool)]
                self.dma_rr += 1
            if slot[1] > 0:
                self._need(eng, (slot[0], slot[1], "dma"), waits)
            slot[1] += 16
            ev = (slot[0], slot[1], "dma")
            self.ops[eng].append((fn, self._best(waits), (slot[0], 16)))
            rr, ww = list(reads), list(writes)
        elif mark:
            c = self.cur[eng]
            if c[1] >= 2000:
                c = self.cur[eng] = [self.new_sem(), 0]
            c[1] += 1
            ev = (c[0], c[1], eng)
            self.ops[eng].append((fn, self._best(waits), (c[0], 1)))
            rr, ww = list(reads) + pr, list(writes) + pw
            self.pending[eng] = ([], [])
        else:
            self.ops[eng].append((fn, self._best(waits), None))
            pr.extend(reads)
            pw.extend(writes)
            return None
        for r in rr:
            self.reads.setdefault(r, []).append(ev)
        for w in ww:
            self.lastw[w] = ev
            self.reads[w] = []
        return ev

    def barrier(self, engs=None):
        evs = list(self.lastw.values())
        for l in self.reads.values():
            evs.extend(l)
        for eng in (engs or self.ENGS):
            waits = []
            for ev in evs:
                if ev is not None:
                    sem, val, src = ev
                    k = id(sem)
                    if self.waited[eng].get(k, (None, 0))[1] >= val:
                        continue
                    self.waited[eng][k] = (sem, val)
                    waits.append((sem, val))
            self.ops[eng].append((None, self._best(waits), None))
        if engs is None:
            self.lastw = {}
            self.reads = {}

    def replay(self):
        nc = self.nc
        engmap = {"pe": "tensor", "act": "scalar", "dve": "vector", "pool": "gpsimd", "sp": "sync"}
        with nc.Block() as block:
            for e in self.ENGS:
                ops = self.ops[e]

                def body(engine, ops=ops):
                    for fn, waits, inc in ops:
                        for sem, val in waits:
                            engine.wait_ge(sem, val)
                        if fn is None:
                            continue
                        inst = fn(engine)
                        if inc is not None:
                            inst.then_inc(inc[0], inc[1])
                getattr(block, engmap[e])(body)

    def close(self):
        for cm in reversed(self.stack):
            cm.__exit__(None, None, None)


DEBUG = {"stop": None}


class _Stop(Exception):
    pass


def build_nc():
    nc = bass.Bass("TRN2", target_bir_lowering=False)

    def din(name, shape):
        return nc.dram_tensor(name, list(shape), F32, kind="ExternalInput").ap()

    def dout(name, shape):
        return nc.dram_tensor(name, list(shape), F32, kind="ExternalOutput").ap()

    def dscr(name, shape, dt=F32):
        return nc.dram_tensor(name, list(shape), dt).ap()

    x_p = din("x_p", [2, SEQ, D]); x_s = din("x_s", [DEC, D])
    st_in = din("st_in", [2, 64, 64, 2]); ck_in = din("ck_in", [128, 256]); cv_in = din("cv_in", [128, 256])
    c_all = din("c_all", [3, D])
    w_ada = din("w_ada", [4, D, 3 * D]); b_ada = din("b_ada", [4, 3 * D])
    ln_g = din("ln_g", [4, D]); ln_b = din("ln_b", [4, D])
    w_in_a = din("w_in_a", [2, D, 2 * D])
    a_re = din("ssm_a_re", [2, 64, 64]); a_im = din("ssm_a_im", [2, 64, 64])
    b_re = din("ssm_b_re", [2, 64, 64, 16]); b_im = din("ssm_b_im", [2, 64, 64, 16])
    c_re = din("ssm_c_re", [2, 64, 16, 64]); c_im = din("ssm_c_im", [2, 64, 16, 64])
    ssm_d = din("ssm_d", [2, D]); log_dt = din("ssm_log_dt", [2, 64])
    w_glu = din("w_glu", [2, D, D]); b_glu = din("b_glu", [2, D]); w_out_a = din("w_out_a", [2, D, D])
    w_kv = din("w_kv", [D, 512]); w_in_b = din("w_in_b", [2, D, 2 * D])
    sinks = din("attn_sinks", [2, 16]); w_out_b = din("w_out_b", [2, D, D])

    y_p = dout("y_p", [2, SEQ, D]); y_s = dout("y_s", [DEC, D])
    ssm_p = dout("ssm_p", [2, 2, 64, 64, 2]); ck_p = dout("ck_p", [2, 128, 256]); cv_p = dout("cv_p", [2, 128, 256])
    ssm_s = dout("ssm_s", [2, 64, 64, 2]); ck_s = dout("ck_s", [DEC, 256]); cv_s = dout("cv_s", [DEC, 256])

    xa_p = dscr("xa_p", [2, SEQ, D]); xa_s = dscr("xa_s", [DEC, D])
    xb_p = dscr("xb_p", [2, SEQ, D]); xb_s = dscr("xb_s", [DEC, D])
    zscr = dscr("zscr", [16, 128, D], BF16)
    ktscr = dscr("ktscr", [3, 64, 4, SEQ], BF16); vscr = dscr("vscr", [3, SEQ, 256], BF16)
    TXd2 = dscr("TXd", [2, 64, 128, 512], BF16); Gd2 = dscr("Gd", [2, 64, 64, 512], BF16)
    a16d = dscr("a16d", [2, 64, 64], F32)
    ucs_d = dscr("ucs_d", [DEC, D], BF16); gcs_d = dscr("gcs_d", [DEC, D], BF16)

    P = Prog(nc)
    P.setup()
    _cnt = [0]

    scopes = []

    def T(shape, dt=F32, name=None):
        _cnt[0] += 1
        nm = "t%d" % _cnt[0]
        if scopes:
            cm = nc.sbuf_tensor(nm, list(shape), dt)
            t = cm.__enter__()
            scopes[-1].append(cm)
            return t
        return nc.alloc_sbuf_tensor(nm, list(shape), dt)

    def push_scope():
        P.barrier()
        scopes.append([])

    def pop_scope():
        P.barrier()
        for cm in reversed(scopes.pop()):
            cm.__exit__(None, None, None)

    def dve(fn, r=(), w=(), mark=True): return P.op("dve", fn, r, w, mark)
    def act(fn, r=(), w=(), mark=True): return P.op("act", fn, r, w, mark)
    def pool(fn, r=(), w=(), mark=True): return P.op("pool", fn, r, w, mark)
    def pe(fn, r=(), w=(), mark=True): return P.op("pe", fn, r, w, mark)
    def dma(out, in_, r=(), w=(), eng="sp", slow=False):
        if slow:
            return P.op(eng, lambda e: e.dma_start(out=out, in_=in_, allow_slow_non_contiguous=True), r, w, dma=True)
        return P.op(eng, lambda e: e.dma_start(out=out, in_=in_), r, w, dma=True)

    psA = nc.alloc_psum_tensor("psA", [128, 2048], F32)
    psB = nc.alloc_psum_tensor("psB", [128, 1024], F32)
    psT = nc.alloc_psum_tensor("psT", [128, 1024], BF16)
    psQ = nc.alloc_psum_tensor("psQ", [128, 1024], BF16)

    ident = T([128, 128]); identb = T([128, 128], BF16)
    pool(lambda e: e.memset(ident[:], 0.0), w=["ident"])
    pool(lambda e: e.affine_select(out=ident[:], in_=ident[:], pattern=[[-1, 128]], compare_op=ALU.not_equal,
                                   fill=1.0, base=0, channel_multiplier=1), r=["ident"], w=["ident"])
    dve(lambda e: e.tensor_copy(out=identb[:], in_=ident[:]), r=["ident"], w=["identb"])
    dsel = T([128, 256])
    pool(lambda e: e.memset(dsel[:], 0.0), w=["dsel"])
    dve(lambda e: e.tensor_copy(out=dsel[:, 0:128], in_=ident[:]), r=["ident", "dsel"], w=["dsel"])
    tmask = T([128, 16, 16])
    pool(lambda e: e.memset(tmask[:], 1.0), w=["tmask"])
    pool(lambda e: e.affine_select(out=tmask[:], in_=tmask[:], pattern=[[16, 16], [0, 16]], compare_op=ALU.is_ge,
                                   fill=0.0, base=15, channel_multiplier=-1), r=["tmask"], w=["tmask"])
    mLa = T([1, 128], BF16); mLb = T([1, 128], BF16); mL1 = T([1, 128], BF16)
    mRa = T([1, 512], BF16); mRb = T([1, 512], BF16); mR0 = T([1, 512], BF16)
    pool(lambda e: e.memset(mLa[:], 0.0), w=["mk"]); pool(lambda e: e.memset(mLa[0:1, 0:64], 1.0), r=["mk"], w=["mk"])
    pool(lambda e: e.memset(mLb[:], 0.0), r=["mk"], w=["mk"]); pool(lambda e: e.memset(mLb[0:1, 64:128], 1.0), r=["mk"], w=["mk"])
    pool(lambda e: e.memset(mL1[:], 1.0), r=["mk"], w=["mk"])
    for hh2 in range(2):
        pool(lambda e, hh2=hh2: e.memset(mRa[0:1, hh2 * 256:hh2 * 256 + 192], 0.0), r=["mk"], w=["mk"])
        pool(lambda e, hh2=hh2: e.memset(mRa[0:1, hh2 * 256 + 192:hh2 * 256 + 256], -1e30), r=["mk"], w=["mk"])
        pool(lambda e, hh2=hh2: e.memset(mRb[0:1, hh2 * 256 + 64:hh2 * 256 + 256], 0.0), r=["mk"], w=["mk"])
        pool(lambda e, hh2=hh2: e.memset(mRb[0:1, hh2 * 256:hh2 * 256 + 64], -1e30), r=["mk"], w=["mk"])
        pool(lambda e, hh2=hh2: e.memset(mR0[0:1, hh2 * 256 + 128:hh2 * 256 + 256], 0.0), r=["mk"], w=["mk"])
        pool(lambda e, hh2=hh2: e.memset(mR0[0:1, hh2 * 256:hh2 * 256 + 128], -1e30), r=["mk"], w=["mk"])
    maskG = "maskG"; mask0 = "mask0"
    epsb = T([128, 1])
    pool(lambda e: e.memset(epsb[:], EPS), w=["epsb"])

    def range_sin(out, ang_over_2pi, shape_tmp, r, w, nparts, offs=0.0):
        t1 = shape_tmp[0]; t2 = shape_tmp[1]
        dve(lambda e: e.tensor_scalar(out=t1, in0=ang_over_2pi, scalar1=offs, scalar2=MAGIC, op0=ALU.add, op1=ALU.add),
            r=r, w=["rs_t1"])
        dve(lambda e: e.tensor_scalar(out=t1, in0=t1, scalar1=MAGIC, scalar2=None, op0=ALU.subtract), r=["rs_t1"], w=["rs_t1"])
        dve(lambda e: e.scalar_tensor_tensor(out=t2, in0=ang_over_2pi, scalar=offs, in1=t1, op0=ALU.add, op1=ALU.subtract),
            r=r + ["rs_t1"], w=["rs_t2"])
        act(lambda e: e.activation(out=out, in_=t2, func=AF.Sin, scale=TWO_PI), r=["rs_t2"], w=w)

    cosT = T([128, 17, 32]); sinT = T([128, 17, 32])
    push_scope()
    posf = T([128, 17]); invf = T([128, 32]); rang = T([128, 17, 32]); rt1 = T([128, 17, 32]); rt2 = T([128, 17, 32])
    pool(lambda e: e.iota(posf[:, 0:16], pattern=[[128, 16]], base=0, channel_multiplier=1,
                          allow_small_or_imprecise_dtypes=True), w=["posf"])
    pool(lambda e: e.iota(posf[:, 16:17], pattern=[[0, 1]], base=1024, channel_multiplier=1,
                          allow_small_or_imprecise_dtypes=True), r=["posf"], w=["posf"])
    for jj in range(32):
        pool(lambda e, jj=jj: e.memset(invf[:, jj:jj + 1], float(np.float32(np.power(np.float32(10000.0), np.float32(-jj / 32.0))))),
             r=["invf"], w=["invf"])
    dve(lambda e: e.tensor_tensor(out=rang[:], in0=posf[:, :].unsqueeze(2).broadcast_to([128, 17, 32]),
                                  in1=invf[:, :].unsqueeze(1).broadcast_to([128, 17, 32]), op=ALU.mult),
        r=["posf", "invf"], w=["rang"])
    dve(lambda e: e.tensor_scalar(out=rang[:], in0=rang[:], scalar1=1.0 / TWO_PI, scalar2=None, op0=ALU.mult), r=["rang"], w=["rang"])
    range_sin(sinT[:], rang[:], (rt1[:], rt2[:]), ["rang"], ["sinT"], 128, 0.0)
    range_sin(cosT[:], rang[:], (rt1[:], rt2[:]), ["rang"], ["cosT"], 128, 0.25)
    pop_scope()

    lngb = T([128, D]); lnbb = T([128, D])
    condPs = [T([128, 24, 4]) for _ in range(4)]; gateb = T([128, D])
    gate_d4 = dscr("gate_d", [4, 3, D])
    CUR = {}

    def load_gate(s):
        dma(gateb[:], gate_d4[CUR["l"], s].partition_broadcast(128), r=["gate_d"], w=["gateb"])

    ADA = {}

    def ada_alloc():
        ADA["cT"] = T([128, 3, 8]); ADA["cTf"] = T([128, 8, 4])
        ADA["wts"] = [T([128, 1536]) for _ in range(3)]
        ADA["bbT"] = T([128, 24]); ADA["zt"] = T([1, 128])

    def ada_all():
        cT = ADA["cT"]; cTf = ADA["cTf"]; wts = ADA["wts"]; bbT = ADA["bbT"]; zt = ADA["zt"]
        pso = psA[:, 1024:1120]
        pool(lambda e: e.memset(zt[:], 0.0), w=["zt"])
        for s in range(3):
            dma(cT[:, s, :], c_all[s].rearrange("(c p) -> p c", p=128), w=["cT"], slow=True)
        act(lambda e: e.activation(out=cT[:], in_=cT[:], func=AF.Silu), r=["cT"], w=["cT"])
        dve(lambda e: e.tensor_copy(out=cTf[:, :, 0:3], in_=cT[:, :, :].rearrange("p s c -> p c s")), r=["cT"], w=["cTf"])
        chunks = [(l, c, hf) for l in range(4) for c in range(8) for hf in range(2)]

        def load(i):
            l, c, hf = chunks[i]
            dma(wts[i % 3][:, :], w_ada[l, c * 128:(c + 1) * 128, hf * 1536:(hf + 1) * 1536], w=["wada%d" % (i % 3)],
                eng=("sp" if i % 2 == 0 else "pool"))
        load(0); load(1)
        for i, (l, c, hf) in enumerate(chunks):
            condP = condPs[l]
            if i + 2 < len(chunks):
                load(i + 2)
            yield
            wt = wts[i % 3]; wn = "wada%d" % (i % 3)
            if c == 0 and hf == 0:
                dma(bbT[:], b_ada[l].rearrange("(c p) -> p c", p=128), w=["bbT"], slow=True)
                pe(lambda e: e.matmul(pso, lhsT=zt[0:1, 0:128], rhs=zt[0:1, 0:96], start=True, stop=False, skip_group_check=True),
                   r=["zt"], w=["psA_hi"], mark=False)
            for cc in range(12):
                ch = hf * 12 + cc
                pe(lambda e, c=c, ch=ch, cc=cc, wt=wt: e.matmul(pso[:, ch * 4:ch * 4 + 3], lhsT=wt[:, cc * 128:(cc + 1) * 128],
                                                                rhs=cTf[:, c, 0:3], start=False, stop=(c == 7), skip_group_check=True),
                   r=[wn, "cTf"], w=["psA_hi"], mark=(cc == 11))
            if c == 7 and hf == 1:
                dve(lambda e, condP=condP: e.tensor_tensor(out=condP[:, :, 0:3], in0=pso.rearrange("p (a b) -> p a b", b=4)[:, :, 0:3],
                                                           in1=bbT[:, :].unsqueeze(2).broadcast_to([128, 24, 3]), op=ALU.add),
                    r=["psA_hi", "bbT"], w=["adaP%d" % l])
                dve(lambda e, condP=condP: e.tensor_scalar(out=condP[:, 8:16, 0:3], in0=condP[:, 8:16, 0:3], scalar1=1.0, scalar2=None, op0=ALU.add),
                    r=["adaP%d" % l], w=["adaP%d" % l])
                for s in range(3):
                    dma(gate_d4[l, s].rearrange("(c p) -> p c", p=128), condP[:, 16:24, s], r=["adaP%d" % l], w=["gate_d"], slow=True, eng="pool")
        yield

    def layer_ln(l):
        CUR["l"] = l; CUR["condP"] = condPs[l]
        dma(lngb[:], ln_g[l].partition_broadcast(128), w=["lngb"])
        dma(lnbb[:], ln_b[l].partition_broadcast(128), w=["lnbb"])

    Win = T([128, 8, 2048], BF16); Wg = T([128, 8, D], BF16); Wo = T([128, 8, D], BF16)

    def load_w(dst, src, name, ncols):
        v = src.rearrange("(c p) n -> p c n", p=128)
        for c in range(8):
            dma(dst[:, c, :], v[:, c, :], w=[name], eng="pool")

    xt1 = T([128, D]); xt = [xt1, xt1]
    xbt = T([128, D], BF16); hmT = T([128, 8, 128], BF16)
    t2 = T([128, D]); t1 = T([128, D]); junk = t1
    s1 = T([128, 16])
    szt = T([128, D]); sg = szt; szb = T([128, D], BF16)
    ybt = T([128, D], BF16); yT = T([128, 8, 128], BF16)

    def run_streams(tasks, ns, extra=(), stagger=0):
        free = list(range(ns))
        active = [(g, None) for g in extra]
        it = iter(tasks)
        rnd = 0
        started = 0
        exhausted = False
        while True:
            while free and not exhausted:
                if started < ns and rnd < started * stagger:
                    break
                try:
                    f = next(it)
                except StopIteration:
                    exhausted = True
                    break
                sl = free.pop(0)
                active.append((f(sl), sl))
                started += 1
            if not active:
                if exhausted or not free:
                    break
                rnd += 1
                continue
            for item in list(active):
                try:
                    next(item[0])
                except StopIteration:
                    active.remove(item)
                    if item[1] is not None:
                        free.append(item[1])
            rnd += 1

    def transpose_to(dstT, src_b, nt, rname, wname):
        for c in range(8):
            pe(lambda e, c=c: e.transpose(out=psT[:, c * 128:c * 128 + nt], in_=src_b[:nt, c * 128:(c + 1) * 128],
                                          identity=identb[:nt, :nt]), r=[rname, "identb"], w=["psT"], mark=(c == 7))
        dve(lambda e: e.tensor_copy(out=dstT[:, :, :nt], in_=psT[:, :].rearrange("p (c t) -> p c t", t=128)[:, :, :nt]),
            r=["psT"], w=[wname])

    def load_mod(src_ap, nt, slot, s, want_plain, X=None, xn="xt0"):
        if X is None:
            X = xt[slot]
            dma(X[:nt, :], src_ap, w=[xn])
        act(lambda e: e.copy(out=xbt[:nt, :], in_=X[:nt, :]), r=[xn], w=["xbt"])
        for c in range(8):
            pe(lambda e, c=c: e.transpose(out=psT[:, c * 128:c * 128 + nt], in_=xbt[:nt, c * 128:(c + 1) * 128],
                                          identity=identb[:nt, :nt]), r=["xbt", "identb"], w=["psT"], mark=(c == 7))
        for c in range(8):
            act(lambda e, c=c, cp=CUR["condP"]: e.activation(out=hmT[:, c, :nt], in_=psT[:, c * 128:c * 128 + nt], func=AF.Identity,
                                                             bias=cp[:, c, s:s + 1], scale=cp[:, 8 + c, s:s + 1]),
                r=["psT"], w=["hmT"], mark=(c == 7))
        if want_plain:
            dve(lambda e: e.tensor_copy(out=AT["xTp"][:, :, :nt], in_=psT[:, :].rearrange("p (c t) -> p c t", t=128)[:, :, :nt]),
                r=["psT"], w=["xTp"])

    def proj(ps_out, lhsT_t, rname, W, wname, nt, ncol, psname, bias_row=None):
        for n in range(ncol // 512):
            if bias_row is not None:
                pe(lambda e, n=n: e.matmul(ps_out[:nt, n * 512:(n + 1) * 512], lhsT=onesb[0:1, :nt], rhs=bias_row[0:1, n * 512:(n + 1) * 512],
                                           start=True, stop=False), r=["onesb", "bglur"],
                   w=(list(psname) if isinstance(psname, (list, tuple)) else [psname]), mark=False)
            for c in range(8):
                pe(lambda e, n=n, c=c: e.matmul(ps_out[:nt, n * 512:(n + 1) * 512], lhsT=lhsT_t[:, c, :nt],
                                                rhs=W[:, c, n * 512:(n + 1) * 512], start=(c == 0 and bias_row is None), stop=(c == 7)),
                   r=[rname, wname], w=(list(psname) if isinstance(psname, (list, tuple)) else [psname]),
                   mark=(c == 7 and n == ncol // 512 - 1))

    def resid_ln(ps_o, psname, slot, s, nt, dst_ap, wregion, X=None, xn="xt0"):
        if X is None:
            X = xt[slot]
        XO = t1; xon = "t1"
        dve(lambda e: e.tensor_tensor(out=t1[:nt, :], in0=ps_o, in1=gateb[:nt, :], op=ALU.mult), r=[psname, "gateb"], w=["t1"])
        dve(lambda e: e.scalar_tensor_tensor(out=t2[:nt, :], in0=X[:nt, :], scalar=ALPHA, in1=t1[:nt, :], op0=ALU.mult, op1=ALU.add),
            r=[xn, "t1"], w=["t2"])
        dve(lambda e: e.bn_stats(out=s1[:nt, 0:6], in_=t2[:nt, 0:512]), r=["t2"], w=["s1a"])
        dve(lambda e: e.bn_stats(out=s1[:nt, 6:12], in_=t2[:nt, 512:1024]), r=["t2", "s1a"], w=["s1a"])
        dve(lambda e: e.bn_aggr(out=s1[:nt, 12:14], in_=s1[:nt, 0:12]), r=["s1a"], w=["s1b"])
        act(lambda e: e.activation(out=s1[:nt, 14:15], in_=s1[:nt, 13:14], func=AF.Sqrt, bias=epsb[:nt, 0:1]), r=["s1b", "epsb"], w=["s1c"])
        dve(lambda e: e.reciprocal(out=s1[:nt, 15:16], in_=s1[:nt, 14:15]), r=["s1c"], w=["s1d"])
        dve(lambda e: e.scalar_tensor_tensor(out=XO[:nt, :], in0=t2[:nt, :], scalar=s1[:nt, 12:13], in1=lngb[:nt, :], op0=ALU.subtract, op1=ALU.mult),
            r=["t2", "s1b", "lngb"], w=[xon])
        dve(lambda e: e.scalar_tensor_tensor(out=XO[:nt, :], in0=XO[:nt, :], scalar=s1[:nt, 15:16], in1=lnbb[:nt, :], op0=ALU.mult, op1=ALU.add),
            r=[xon, "s1d", "lnbb"], w=[xon])
        dma(dst_ap, XO[:nt, :], r=[xon], w=[wregion], eng="pool")

    A1s = [[T([128, 32, 2]) for _ in range(3)] for _ in range(2)]; A2s = [[T([128, 32, 2]) for _ in range(3)] for _ in range(2)]
    apw = T([128, 2, 32]); apt = T([128, 2, 32])
    GEN = {}
    bglur = T([1, D], BF16); onesb = T([1, 128], BF16)
    pool(lambda e: e.memset(onesb[:], 1.0), w=["onesb"])
    S5 = {}
    DBG = {}

    def alloc_s5_seq():
        S5["Uc"] = T([128, 64, 16, 16], BF16)
        S5["UTs"] = [T([128, 2, 128], BF16) for _ in range(4)]
        S5["TXs"] = [T([128, 512], BF16) for _ in range(4)]
        S5["Gs"] = [T([128, 512], BF16) for _ in range(4)]
        S5["SH"] = T([128, 32, 2, 130])
        S5["Hbs"] = [T([128, 2, 128], BF16) for _ in range(4)]
        S5["sc1"] = T([128, 32, 2]); S5["sc2"] = T([128, 32, 2])
        S5["gt"] = T([128, D], BF16)
        S5["h3b"] = T([128, D], BF16); S5["h4b"] = T([128, D], BF16)
        S5["s1"] = [T([128, 16]) for _ in range(4)]

    GB = 4

    def gen_alloc():
        GEN['are'] = T([64, 64])
        GEN['aim'] = T([64, 64])
        GEN['dtt'] = T([64, 64])
        GEN['rho'] = T([64, 64])
        GEN['th'] = T([64, 64])
        GEN['bre_t'] = T([64, 64, 16])
        GEN['bim_t'] = T([64, 64, 16])
        GEN['craw'] = T([128, 2, 8, 64])
        GEN['cre_t'] = T([64, 64, 16])
        GEN['cim_t'] = T([64, 64, 16])
        GEN['dcol'] = T([128, 64])
        GEN['ev'] = T([64, 40])
        GEN['a16'] = T([64, 2, 64])
        GEN['a16p'] = T([128, 2, 32])
        GEN['Pr'] = T([64, GB, 40])
        GEN['Pi'] = T([64, GB, 40])
        GEN['pa'] = T([64, GB, 40])
        GEN['pt1'] = T([64, GB, 40])
        GEN['pt2'] = T([64, GB, 40])
        GEN['pm'] = T([64, GB, 40])
        GEN['q1'] = T([64, GB])
        GEN['q2'] = T([64, GB])
        GEN['q3'] = T([64, GB])
        GEN['wr'] = T([64, GB])
        GEN['wi'] = T([64, GB])
        GEN['bbr'] = T([64, GB, 16])
        GEN['bbi'] = T([64, GB, 16])
        GEN['tb1'] = T([64, GB, 16])
        GEN['Fr'] = T([64, GB, 23, 16])
        GEN['Fi'] = T([64, GB, 23, 16])
        GEN['Ft'] = T([64, GB, 23, 16])
        GEN['Gr'] = T([64, GB, 17, 16])
        GEN['Gi'] = T([64, GB, 17, 16])
        GEN['Gt'] = T([64, GB, 17, 16])
        GEN['txs'] = [T([128, 512], BF16) for i in range(2)]
        GEN['gsb'] = [T([64, 2, 256], BF16) for i in range(2)]
        GEN['tbf'] = T([128, 256])

        for nm in ('en', 'er', 'ep', 'e2', 'et', 'wA', 'wB', 'wC', 'wD', 'wE', 'wrA', 'wiA'):
            GEN[nm] = T([64, 64])

    def gen_run(la):
        are = GEN['are']
        aim = GEN['aim']
        dtt = GEN['dtt']
        rho = GEN['rho']
        th = GEN['th']
        bre_t = GEN['bre_t']
        bim_t = GEN['bim_t']
        craw = GEN['craw']
        cre_t = GEN['cre_t']
        cim_t = GEN['cim_t']
        dcol = GEN['dcol']
        ev = GEN['ev']
        a16 = GEN['a16']
        a16p = GEN['a16p']
        Pr = GEN['Pr']
        Pi = GEN['Pi']
        pa = GEN['pa']
        pt1 = GEN['pt1']
        pt2 = GEN['pt2']
        pm = GEN['pm']
        q1 = GEN['q1']
        q2 = GEN['q2']
        q3 = GEN['q3']
        wr = GEN['wr']
        wi = GEN['wi']
        bbr = GEN['bbr']
        bbi = GEN['bbi']
        tb1 = GEN['tb1']
        Fr = GEN['Fr']
        Fi = GEN['Fi']
        Ft = GEN['Ft']
        Gr = GEN['Gr']
        Gi = GEN['Gi']
        Gt = GEN['Gt']
        txs = GEN['txs']
        gsb = GEN['gsb']
        tbf = GEN['tbf']
        TXd = TXd2[la]; Gd = Gd2[la]
        R = "g_"
        dma(are[:], a_re[la].rearrange("g p -> p g"), w=[R + "are"], slow=True)
        dma(aim[:], a_im[la].rearrange("g p -> p g"), w=[R + "aim"], slow=True)
        dma(dtt[:], log_dt[la].partition_broadcast(64), w=[R + "dtt"])
        dma(bre_t[:], b_re[la].rearrange("g p c -> p g c"), w=[R + "bre"])
        dma(bim_t[:], b_im[la].rearrange("g p c -> p g c"), w=[R + "bim"])
        dma(craw[:, 0, :, :], c_re[la].rearrange("(b g) c p -> (g c) b p", g=8), w=[R + "craw"])
        dma(craw[:, 1, :, :], c_im[la].rearrange("(b g) c p -> (g c) b p", g=8), w=[R + "craw"])
        for il in range(8):
            dma(dcol[16 * il:16 * il + 16, :], ssm_d[la].rearrange("(g c) -> c g", c=16), w=[R + "dcol"], slow=True)
        for ri, dst in ((0, cre_t), (1, cim_t)):
            for b in range(8):
                pe(lambda e, ri=ri, b=b: e.transpose(out=psA[0:64, b * 128:(b + 1) * 128], in_=craw[:, ri, b, :], identity=ident[:, :]),
                   r=[R + "craw", "ident"], w=["psA_lo"], mark=(b == 7))
            dve(lambda e, dst=dst: e.tensor_copy(out=dst[:, :, :], in_=psA[0:64, 0:1024].rearrange("p (g c) -> p g c", c=16)),
                r=["psA_lo"], w=[R + "c"])
        en = GEN['en']; er = GEN['er']; ep = GEN['ep']; e2 = GEN['e2']; et = GEN['et']
        dve(lambda e: e.tensor_scalar(out=en[:], in0=dtt[:], scalar1=1.0 / math.log(2.0), scalar2=MAGIC, op0=ALU.mult, op1=ALU.add),
            r=[R + "dtt"], w=[R + "en"])
        dve(lambda e: e.tensor_scalar(out=en[:], in0=en[:], scalar1=MAGIC, scalar2=None, op0=ALU.subtract), r=[R + "en"], w=[R + "en"])
        dve(lambda e: e.scalar_tensor_tensor(out=er[:], in0=en[:], scalar=-0.693359375, in1=dtt[:], op0=ALU.mult, op1=ALU.add),
            r=[R + "en", R + "dtt"], w=[R + "er"])
        dve(lambda e: e.scalar_tensor_tensor(out=er[:], in0=en[:], scalar=2.12194440e-4, in1=er[:], op0=ALU.mult, op1=ALU.add),
            r=[R + "en", R + "er"], w=[R + "er"])
        dve(lambda e: e.tensor_scalar(out=ep[:], in0=er[:], scalar1=1.0 / 5040.0, scalar2=None, op0=ALU.mult), r=[R + "er"], w=[R + "ep"])
        for ck in (1.0 / 720.0, 1.0 / 120.0, 1.0 / 24.0, 1.0 / 6.0, 0.5, 1.0):
            dve(lambda e, ck=ck: e.scalar_tensor_tensor(out=ep[:], in0=ep[:], scalar=ck, in1=er[:], op0=ALU.add, op1=ALU.mult),
                r=[R + "ep", R + "er"], w=[R + "ep"])
        dve(lambda e: e.tensor_scalar(out=ep[:], in0=ep[:], scalar1=1.0, scalar2=None, op0=ALU.add), r=[R + "ep"], w=[R + "ep"])
        pool(lambda e: e.memset(e2[:], 0.0), w=[R + "e2"])
        for kk in range(-16, 5):
            dve(lambda e, kk=kk: e.tensor_scalar(out=et[:], in0=en[:], scalar1=float(kk), scalar2=float(2.0 ** kk), op0=ALU.is_equal, op1=ALU.mult),
                r=[R + "en"], w=[R + "et"])
            dve(lambda e: e.tensor_tensor(out=e2[:], in0=e2[:], in1=et[:], op=ALU.add), r=[R + "e2", R + "et"], w=[R + "e2"])
        dve(lambda e: e.tensor_tensor(out=dtt[:], in0=ep[:], in1=e2[:], op=ALU.mult), r=[R + "ep", R + "e2"], w=[R + "dtt"])
        dve(lambda e: e.tensor_tensor(out=rho[:], in0=are[:], in1=dtt[:], op=ALU.mult), r=[R + "are", R + "dtt"], w=[R + "rho"])
        dve(lambda e: e.tensor_tensor(out=th[:], in0=aim[:], in1=dtt[:], op=ALU.mult), r=[R + "aim", R + "dtt"], w=[R + "th"])
        dve(lambda e: e.tensor_scalar(out=th[:], in0=th[:], scalar1=1.0 / TWO_PI, scalar2=None, op0=ALU.mult), r=[R + "th"], w=[R + "th"])
        pool(lambda e: e.iota(ev[:, 0:23], pattern=[[-1, 23]], base=15, channel_multiplier=0, allow_small_or_imprecise_dtypes=True), w=[R + "ev"])
        pool(lambda e: e.iota(ev[:, 23:40], pattern=[[1, 17]], base=0, channel_multiplier=0, allow_small_or_imprecise_dtypes=True), r=[R + "ev"], w=[R + "ev"])
        evb = ev[:, :].unsqueeze(1).broadcast_to([64, GB, 40])
        wA = GEN['wA']; wB = GEN['wB']; wC = GEN['wC']; wD = GEN['wD']; wE = GEN['wE']; wrA = GEN['wrA']; wiA = GEN['wiA']
        range_sin(wB[:], th[:], (wD[:], wE[:]), [R + "th"], [R + "wB"], 64, 0.0)
        range_sin(wA[:], th[:], (wD[:], wE[:]), [R + "th"], [R + "wA"], 64, 0.25)
        act(lambda e: e.activation(out=wC[:], in_=rho[:], func=AF.Exp), r=[R + "rho"], w=[R + "wC"])
        dve(lambda e: e.tensor_tensor(out=wA[:], in0=wA[:], in1=wC[:], op=ALU.mult), r=[R + "wA", R + "wC"], w=[R + "wA"])
        dve(lambda e: e.tensor_tensor(out=wB[:], in0=wB[:], in1=wC[:], op=ALU.mult), r=[R + "wB", R + "wC"], w=[R + "wB"])
        dve(lambda e: e.tensor_scalar(out=wA[:], in0=wA[:], scalar1=-1.0, scalar2=None, op0=ALU.add), r=[R + "wA"], w=[R + "wA"])
        dve(lambda e: e.tensor_tensor(out=wC[:], in0=are[:], in1=are[:], op=ALU.mult), r=[R + "are", R + "wC"], w=[R + "wC"])
        dve(lambda e: e.tensor_tensor(out=wD[:], in0=aim[:], in1=aim[:], op=ALU.mult), r=[R + "aim"], w=[R + "wD", "rs_t1"])
        dve(lambda e: e.tensor_tensor(out=wC[:], in0=wC[:], in1=wD[:], op=ALU.add), r=[R + "wC", R + "wD"], w=[R + "wC"])
        dve(lambda e: e.reciprocal(out=wC[:], in_=wC[:]), r=[R + "wC"], w=[R + "wC"])
        dve(lambda e: e.tensor_tensor(out=wrA[:], in0=wA[:], in1=are[:], op=ALU.mult), r=[R + "wA", R + "are"], w=[R + "wrA"])
        dve(lambda e: e.tensor_tensor(out=wD[:], in0=wB[:], in1=aim[:], op=ALU.mult), r=[R + "wB", R + "aim", R + "wD"], w=[R + "wD"])
        dve(lambda e: e.tensor_tensor(out=wrA[:], in0=wrA[:], in1=wD[:], op=ALU.add), r=[R + "wrA", R + "wD"], w=[R + "wrA"])
        dve(lambda e: e.tensor_tensor(out=wrA[:], in0=wrA[:], in1=wC[:], op=ALU.mult), r=[R + "wrA", R + "wC"], w=[R + "wrA"])
        dve(lambda e: e.tensor_tensor(out=wiA[:], in0=wB[:], in1=are[:], op=ALU.mult), r=[R + "wB", R + "are"], w=[R + "wiA"])
        dve(lambda e: e.tensor_tensor(out=wD[:], in0=wA[:], in1=aim[:], op=ALU.mult), r=[R + "wA", R + "aim", R + "wD"], w=[R + "wD"])
        dve(lambda e: e.tensor_tensor(out=wiA[:], in0=wiA[:], in1=wD[:], op=ALU.subtract), r=[R + "wiA", R + "wD"], w=[R + "wiA"])
        dve(lambda e: e.tensor_tensor(out=wiA[:], in0=wiA[:], in1=wC[:], op=ALU.mult), r=[R + "wiA", R + "wC"], w=[R + "wiA"])
        for gb in range(64 // GB):
            g0 = gb * GB
            gs = slice(g0, g0 + GB)
            dve(lambda e, gs=gs: e.tensor_tensor(out=pa[:], in0=th[:, gs].unsqueeze(2).broadcast_to([64, GB, 40]), in1=evb, op=ALU.mult),
                r=[R + "th", R + "ev"], w=[R + "pa"])
            dve(lambda e, gs=gs: e.tensor_tensor(out=pm[:], in0=rho[:, gs].unsqueeze(2).broadcast_to([64, GB, 40]), in1=evb, op=ALU.mult),
                r=[R + "rho", R + "ev"], w=[R + "pm"])
            act(lambda e: e.activation(out=pm[:], in_=pm[:], func=AF.Exp), r=[R + "pm"], w=[R + "pm"])
            range_sin(Pi[:], pa[:], (pt1[:], pt2[:]), [R + "pa"], [R + "Pi"], 64, 0.0)
            range_sin(Pr[:], pa[:], (pt1[:], pt2[:]), [R + "pa"], [R + "Pr"], 64, 0.25)
            dve(lambda e: e.tensor_tensor(out=Pr[:], in0=Pr[:], in1=pm[:], op=ALU.mult), r=[R + "Pr", R + "pm"], w=[R + "Pr"])
            dve(lambda e: e.tensor_tensor(out=Pi[:], in0=Pi[:], in1=pm[:], op=ALU.mult), r=[R + "Pi", R + "pm"], w=[R + "Pi"])
            wrb = wrA[:, gs].unsqueeze(2).broadcast_to([64, GB, 16]); wib = wiA[:, gs].unsqueeze(2).broadcast_to([64, GB, 16])
            brs = bre_t[:, gs, :]; bis = bim_t[:, gs, :]
            dve(lambda e, brs=brs, wrb=wrb: e.tensor_tensor(out=bbr[:], in0=brs, in1=wrb, op=ALU.mult), r=[R + "bre", R + "wrA"], w=[R + "bbr"])
            dve(lambda e, bis=bis, wib=wib: e.tensor_tensor(out=tb1[:], in0=bis, in1=wib, op=ALU.mult), r=[R + "bim", R + "wiA"], w=[R + "tb1"])
            dve(lambda e: e.tensor_tensor(out=bbr[:], in0=bbr[:], in1=tb1[:], op=ALU.subtract), r=[R + "bbr", R + "tb1"], w=[R + "bbr"])
            dve(lambda e, bis=bis, wrb=wrb: e.tensor_tensor(out=bbi[:], in0=bis, in1=wrb, op=ALU.mult), r=[R + "bim", R + "wrA"], w=[R + "bbi"])
            dve(lambda e, brs=brs, wib=wib: e.tensor_tensor(out=tb1[:], in0=brs, in1=wib, op=ALU.mult), r=[R + "bre", R + "wiA"], w=[R + "tb1"])
            dve(lambda e: e.tensor_tensor(out=bbi[:], in0=bbi[:], in1=tb1[:], op=ALU.add), r=[R + "bbi", R + "tb1"], w=[R + "bbi"])
            dve(lambda e, gs=gs: e.tensor_copy(out=a16[:, 0, gs], in_=Pr[:, :, 39]), r=[R + "Pr", R + "a16"], w=[R + "a16"])
            dve(lambda e, gs=gs: e.tensor_copy(out=a16[:, 1, gs], in_=Pi[:, :, 39]), r=[R + "Pi", R + "a16"], w=[R + "a16"])
            PrF = Pr[:, :, 0:23].unsqueeze(3).broadcast_to([64, GB, 23, 16])
            PiF = Pi[:, :, 0:23].unsqueeze(3).broadcast_to([64, GB, 23, 16])
            bbrB = bbr[:, :, :].unsqueeze(2).broadcast_to([64, GB, 23, 16])
            bbiB = bbi[:, :, :].unsqueeze(2).broadcast_to([64, GB, 23, 16])
            dve(lambda e: e.tensor_tensor(out=Fr[:], in0=PrF, in1=bbrB, op=ALU.mult), r=[R + "Pr", R + "bbr"], w=[R + "Fr"])
            dve(lambda e: e.tensor_tensor(out=Ft[:], in0=PiF, in1=bbiB, op=ALU.mult), r=[R + "Pi", R + "bbi"], w=[R + "Ft"])
            dve(lambda e: e.tensor_tensor(out=Fr[:], in0=Fr[:], in1=Ft[:], op=ALU.subtract), r=[R + "Fr", R + "Ft"], w=[R + "Fr"])
            dve(lambda e: e.tensor_tensor(out=Fi[:], in0=PrF, in1=bbiB, op=ALU.mult), r=[R + "Pr", R + "bbi"], w=[R + "Fi"])
            dve(lambda e: e.tensor_tensor(out=Ft[:], in0=PiF, in1=bbrB, op=ALU.mult), r=[R + "Pi", R + "bbr"], w=[R + "Ft"])
            dve(lambda e: e.tensor_tensor(out=Fi[:], in0=Fi[:], in1=Ft[:], op=ALU.add), r=[R + "Fi", R + "Ft"], w=[R + "Fi"])
            PrG = Pr[:, :, 23:40].unsqueeze(3).broadcast_to([64, GB, 17, 16])
            PiG = Pi[:, :, 23:40].unsqueeze(3).broadcast_to([64, GB, 17, 16])
            crB = cre_t[:, gs, :].unsqueeze(2).broadcast_to([64, GB, 17, 16])
            ciB = cim_t[:, gs, :].unsqueeze(2).broadcast_to([64, GB, 17, 16])
            dve(lambda e, b=crB: e.tensor_tensor(out=Gr[:], in0=PrG, in1=b, op=ALU.mult), r=[R + "Pr", R + "c"], w=[R + "Gr"])
            dve(lambda e, b=ciB: e.tensor_tensor(out=Gt[:], in0=PiG, in1=b, op=ALU.mult), r=[R + "Pi", R + "c"], w=[R + "Gt"])
            dve(lambda e: e.tensor_tensor(out=Gr[:], in0=Gr[:], in1=Gt[:], op=ALU.subtract), r=[R + "Gr", R + "Gt"], w=[R + "Gr"])
            dve(lambda e, b=crB: e.tensor_tensor(out=Gi[:], in0=PiG, in1=b, op=ALU.mult), r=[R + "Pi", R + "c"], w=[R + "Gi"])
            dve(lambda e, b=ciB: e.tensor_tensor(out=Gt[:], in0=PrG, in1=b, op=ALU.mult), r=[R + "Pr", R + "c"], w=[R + "Gt"])
            dve(lambda e: e.tensor_tensor(out=Gi[:], in0=Gi[:], in1=Gt[:], op=ALU.add), r=[R + "Gi", R + "Gt"], w=[R + "Gi"])
            dve(lambda e: e.tensor_scalar(out=Gi[:], in0=Gi[:], scalar1=-1.0, scalar2=None, op0=ALU.mult), r=[R + "Gi"], w=[R + "Gi"])
            for g8 in range(GB):
                g = g0 + g8
                sl = g % 2
                fl = lambda ap: ap.rearrange("p a b -> p (a b)")
                pe(lambda e, g8=g8: e.matmul(psB[:, 0:256], lhsT=fl(Fr[:, g8, 15:23, :]), rhs=fl(Gr[:, g8, 0:16, :]), start=True, stop=False),
                   r=[R + "Fr", R + "Gr"], w=["psB"], mark=False)
                pe(lambda e, g8=g8: e.matmul(psB[:, 0:256], lhsT=fl(Fi[:, g8, 15:23, :]), rhs=fl(Gi[:, g8, 0:16, :]), start=False, stop=True),
                   r=[R + "Fi", R + "Gi"], w=["psB"])
                dve(lambda e: e.tensor_tensor(out=tbf[:], in0=psB[:, 0:256], in1=tmask[:, :, :].rearrange("p a b -> p (a b)"), op=ALU.mult),
                    r=["psB", "tmask"], w=[R + "tbf"])
                dve(lambda e, g=g, sl=sl: e.scalar_tensor_tensor(out=txs[sl][:, 0:256], in0=dsel[:], scalar=dcol[:, g:g + 1], in1=tbf[:],
                                                                 op0=ALU.mult, op1=ALU.add),
                    r=["dsel", R + "dcol", R + "tbf"], w=[R + "txs%d" % sl])
                for h in range(2):
                    for c, Fc in ((0, Fr), (1, Fi)):
                        pe(lambda e, g8=g8, h=h, c=c, Fc=Fc: e.transpose(out=psA[:, (h * 2 + c) * 64:(h * 2 + c) * 64 + 64],
                                                                          in_=fl(Fc[:, g8, 8 * h:8 * h + 8, :]), identity=ident[0:64, 0:64]),
                           r=[R + "Fr", R + "Fi", "ident"], w=["psA_lo"], mark=(h == 1 and c == 1))
                act(lambda e, sl=sl: e.copy(out=txs[sl][:, 256:512], in_=psA[:, 0:256]), r=["psA_lo", R + "txs%d" % sl], w=[R + "txs%d" % sl])
                dma(TXd[g], txs[sl][:], r=[R + "txs%d" % sl], w=["TXd%d" % g])
                pool(lambda e, g8=g8, sl=sl: e.tensor_copy(out=gsb[sl][:, 0, :], in_=fl(Gr[:, g8, 1:17, :])), r=[R + "Gr"], w=[R + "gsb%d" % sl])
                pool(lambda e, g8=g8, sl=sl: e.tensor_copy(out=gsb[sl][:, 1, :], in_=fl(Gi[:, g8, 1:17, :])), r=[R + "Gi", R + "gsb%d" % sl], w=[R + "gsb%d" % sl])
                dma(Gd[g], gsb[sl][:, :, :], r=[R + "gsb%d" % sl], w=["Gd%d" % g])
            yield
        dma(a16d.rearrange("c p g -> p c g"), a16[:], r=[R + "a16"], w=["a16d"])
        for par in range(2):
            for c in range(2):
                dma(a16p[64 * par:64 * par + 64, c, :], a16d.rearrange("c p (gh two) -> two c p gh", two=2)[par, c],
                    r=["a16d"], w=[R + "a16p"], slow=True)
        for lev in range(3):
            A1 = A1s[la][lev]; A2 = A2s[la][lev]
            dve(lambda e, A1=A1: e.tensor_copy(out=A1[:, :, :], in_=a16p[:, 0, :].unsqueeze(2).broadcast_to([128, 32, 2])), r=[R + "a16p"], w=["A1"])
            dve(lambda e, A2=A2: e.tensor_copy(out=A2[:, :, 1], in_=a16p[:, 1, :]), r=[R + "a16p"], w=["A2"])
            dve(lambda e, A2=A2: e.tensor_scalar(out=A2[:, :, 0], in0=a16p[:, 1, :], scalar1=-1.0, scalar2=None, op0=ALU.mult), r=[R + "a16p", "A2"], w=["A2"])
            if lev < 2:
                dve(lambda e: e.tensor_tensor(out=apw[:, :, :], in0=a16p[:, :, :], in1=a16p[:, :, :], op=ALU.mult), r=[R + "a16p"], w=["apw"])
                dve(lambda e: e.tensor_tensor(out=apt[:, 1, :], in0=a16p[:, 0, :], in1=a16p[:, 1, :], op=ALU.mult), r=[R + "a16p"], w=["apt"])
                dve(lambda e: e.tensor_tensor(out=a16p[:, 0, :], in0=apw[:, 0, :], in1=apw[:, 1, :], op=ALU.subtract), r=["apw", R + "a16p"], w=[R + "a16p"])
                dve(lambda e: e.tensor_scalar(out=a16p[:, 1, :], in0=apt[:, 1, :], scalar1=2.0, scalar2=None, op0=ALU.mult), r=["apt", R + "a16p"], w=[R + "a16p"])
        yield

    def s5_sequence(la, s, nk, src_tile, dst_tile, h0_fn, hout_fn, x_src_s=None, x_dst_s=None):
        stop_if = DBG["stop_if"]
        nk0 = nk
        x_s = x_src_s
        Uc = S5["Uc"]; SH = S5["SH"]
        TXd = TXd2[la]; Gd = Gd2[la]
        SHf = SH[:, :, :, :].rearrange("p a b c -> p (a b c)")
        SHb = SHf[:, 4096:8192].bitcast(BF16)
        psBb = psB.bitcast(BF16)

        def hb(j):
            return SHb[:, 1024 * j:1024 * (j + 1)]
        SB = [dict(F1=xt1, F2=t1, H1=xbt, H2=hmT, H3=szb, H4=ybt, psT=psT, psTn="psT"),
              dict(F1=t2, F2=szt, H1=S5["gt"], H2=yT, H3=S5["h3b"], H4=S5["h4b"], psT=psQ, psTn="psQ"),
              dict(F1=SHf[:, 0:1024], F2=SHf[:, 1024:2048], H1=hb(0), H2=hb(1).rearrange("p (c t) -> p c t", t=128), H3=hb(2), H4=hb(3),
                   psT=psBb[:, 0:1024], psTn="psB0"),
              dict(F1=SHf[:, 2048:3072], F2=SHf[:, 3072:4096], H1=hb(4), H2=hb(5).rearrange("p (c t) -> p c t", t=128), H3=hb(6), H4=hb(7),
                   psT=psBb[:, 1024:2048], psTn="psB1")]
        for kk in range(4):
            SB[kk]["psX"] = psA[:, 512 * kk:512 * kk + 512]; SB[kk]["psXn"] = "psA_q%d" % kk; SB[kk]["s1"] = S5["s1"][kk]

        def proj_h(B, K, W, wname, n, hf, bias_row=None):
            if bias_row is not None:
                pe(lambda e: e.matmul(B["psX"][:n, 0:512], lhsT=onesb[0:1, :n], rhs=bias_row[0:1, hf * 512:(hf + 1) * 512], start=True, stop=False),
                   r=["onesb", "bglur"], w=[B["psXn"]], mark=False)
            for c in range(8):
                pe(lambda e, c=c: e.matmul(B["psX"][:n, 0:512], lhsT=B["H2"][:, c, :n], rhs=W[:, c, hf * 512:(hf + 1) * 512],
                                           start=(c == 0 and bias_row is None), stop=(c == 7)),
                   r=["H2" + K, wname], w=[B["psXn"]], mark=(c == 7))

        allU = ["Uc%d" % g for g in range(64)]

        def tr8(B, k, src, srcn, dst, dstn, modulate, n=None):
            n = nk if n is None else n
            for c in range(8):
                pe(lambda e, c=c: e.transpose(out=B["psT"][:, c * 128:c * 128 + n], in_=src[:n, c * 128:(c + 1) * 128], identity=identb[:n, :n]),
                   r=[srcn, "identb"], w=[B["psTn"]], mark=(c == 7))
            yield
            if modulate:
                for c in range(8):
                    act(lambda e, c=c, cp=CUR["condP"]: e.activation(out=dst[:, c, :n], in_=B["psT"][:, c * 128:c * 128 + n], func=AF.Identity,
                                                                     bias=cp[:, c, s:s + 1], scale=cp[:, 8 + c, s:s + 1]),
                        r=[B["psTn"]], w=[dstn], mark=(c == 7))
            else:
                act(lambda e: e.copy(out=dst[:, :, :n], in_=B["psT"][:, :].rearrange("p (c t) -> p c t", t=128)[:, :, :n]),
                    r=[B["psTn"]], w=[dstn])
            yield

        def p1_task(i, sample=False):
            def gen(k):
                B = SB[k]; K = "_%d" % k
                n = DEC if sample else nk
                dma(B["H1"][:n, :], x_s if sample else src_tile(i), w=["H1" + K], eng="pool")
                yield
                yield from tr8(B, k, B["H1"], "H1" + K, B["H2"], "H2" + K, True, n)
                for hf in range(2):
                    proj_h(B, K, Win, "Win", n, hf)
                    yield
                    if sample:
                        act(lambda e, hf=hf: e.copy(out=B["H4"][:n, hf * 512:(hf + 1) * 512], in_=B["psX"][:n, 0:512]), r=[B["psXn"]], w=["H4" + K])
                    else:
                        dve(lambda e, hf=hf: e.tensor_copy(out=Uc[:n, 32 * hf:32 * hf + 32, i, :], in_=B["psX"][:n, 0:512].rearrange("p (g c) -> p g c", c=16)),
                            r=[B["psXn"]], w=allU[32 * hf:32 * hf + 32])
                if sample:
                    dma(ucs_d, B["H4"][:n, :], r=["H4" + K], w=["ucs_d"])
                    uv = ucs_d.rearrange("(k i) (g c) -> i k g c", i=16, c=16)
                    for ii in range(16):
                        dma(Uc[:4, :, ii, :], uv[ii], r=["ucs_d"], w=allU, slow=True)
                for hf in range(2):
                    proj_h(B, K, Win[:, :, 1024:2048], "Win", n, hf)
                    yield
                    act(lambda e, hf=hf: e.activation(out=B["H3"][:n, hf * 512:(hf + 1) * 512], in_=B["psX"][:n, 0:512], func=AF.Silu),
                        r=[B["psXn"]], w=["H3" + K])
                dma(zscr[i, :n, :], B["H3"][:n, :], r=["H3" + K], w=["zscr%d" % i])
            return gen

        def pq(k):
            pst = psT if k < 2 else psQ
            return pst, ("psT" if k < 2 else "psQ"), (k % 2) * 512, psA[:, 512 * k:512 * k + 512], "psA_q%d" % k

        def p2_task(g):
            def gen(k):
                par = g % 2; gh = g // 2; sl = k
                pst, pstn, pc0, psx, psxn = pq(k)
                dma(S5["TXs"][sl][:, :], TXd[g], r=["TXd%d" % g], w=["TXs%d" % sl])
                yield
                for h in range(2):
                    pe(lambda e, h=h: e.transpose(out=pst[:, pc0 + h * 128:pc0 + h * 128 + nk],
                                                  in_=Uc[:nk, g, 8 * h:8 * h + 8, :].rearrange("p a b -> p (a b)"), identity=identb[:nk, :nk]),
                       r=["Uc%d" % g, "identb"], w=[pstn], mark=(h == 1))
                yield
                act(lambda e: e.copy(out=S5["UTs"][sl][:, :, :nk], in_=pst[:, pc0:pc0 + 256].rearrange("p (h k) -> p h k", k=128)[:, :, :nk]),
                    r=[pstn], w=["UTs%d" % sl])
                yield
                for c in range(2):
                    for h in range(2):
                        pe(lambda e, c=c, h=h: e.matmul(psx[64 * par:64 * par + 64, c * 128:c * 128 + nk],
                                                        lhsT=S5["TXs"][sl][:, 256 + (h * 2 + c) * 64:256 + (h * 2 + c) * 64 + 64],
                                                        rhs=S5["UTs"][sl][:, h, :nk], start=(h == 0), stop=(h == 1)),
                           r=["TXs%d" % sl, "UTs%d" % sl], w=[psxn], mark=(c == 1 and h == 1))
                yield
                dve(lambda e: e.tensor_copy(out=SH[64 * par:64 * par + 64, gh, :, 1:nk + 1],
                                            in_=psx[64 * par:64 * par + 64, 0:256].rearrange("p (c k) -> p c k", k=128)[:, :, :nk]),
                    r=[psxn], w=["SH%d" % (gh // 16)])
            return gen

        def p4_task(g):
            def gen(k):
                par = g % 2; gh = g // 2; sl = k
                pst, pstn, pc0, psx, psxn = pq(k)
                dma(S5["TXs"][sl][:, 0:256], TXd[g][:, 0:256], r=["TXd%d" % g], w=["TXs%d" % sl])
                dma(S5["Gs"][sl][64 * par:64 * par + 64, :], Gd[g], r=["Gd%d" % g], w=["Gs%d" % sl], eng="act")
                act(lambda e: e.copy(out=S5["Hbs"][k][64 * par:64 * par + 64, :, :nk], in_=SH[64 * par:64 * par + 64, gh, :, 0:nk]),
                    r=["SH%d" % (gh // 16)], w=["Hb%d" % k])
                yield
                for h in range(2):
                    pe(lambda e, h=h: e.transpose(out=pst[:, pc0 + h * 128:pc0 + h * 128 + nk],
                                                  in_=Uc[:nk, g, 8 * h:8 * h + 8, :].rearrange("p a b -> p (a b)"), identity=identb[:nk, :nk]),
                       r=["Uc%d" % g, "identb"], w=[pstn], mark=(h == 1))
                yield
                act(lambda e: e.copy(out=S5["UTs"][sl][:, :, :nk], in_=pst[:, pc0:pc0 + 256].rearrange("p (h k) -> p h k", k=128)[:, :, :nk]),
                    r=[pstn], w=["UTs%d" % sl])
                yield
                o = psx[:nk, 0:256]
                pe(lambda e: e.matmul(o, lhsT=S5["UTs"][sl][:, 0, :nk], rhs=S5["TXs"][sl][:, 0:256], start=True, stop=False,
                                      skip_group_check=True), r=["UTs%d" % sl, "TXs%d" % sl], w=[psxn], mark=False)
                pe(lambda e: e.matmul(o[:, 128:256], lhsT=S5["UTs"][sl][:, 1, :nk], rhs=S5["TXs"][sl][:, 0:128], start=False, stop=False,
                                      skip_group_check=True), r=["UTs%d" % sl, "TXs%d" % sl], w=[psxn], mark=False)
                for c in range(2):
                    pe(lambda e, c=c: e.matmul(o, lhsT=S5["Hbs"][k][64 * par:64 * par + 64, c, :nk],
                                               rhs=S5["Gs"][sl][64 * par:64 * par + 64, c * 256:(c + 1) * 256],
                                               start=False, stop=(c == 1), skip_group_check=True),
                       r=["Hb%d" % k, "Gs%d" % sl], w=[psxn], mark=(c == 1))
                yield
                act(lambda e: e.activation(out=Uc[:nk, g, :, :], in_=psx[:nk, 0:256].rearrange("p (i c) -> p i c", c=16), func=AF.Gelu),
                    r=[psxn], w=["Uc%d" % g])
            return gen

        def p5_task(i, sample=False):
            def gen(k):
                B = SB[k]; K = "_%d" % k
                nk = DEC if sample else nk0
                F1 = B["F1"]; F2 = B["F2"]; s1k = B["s1"]
                dma(F1[:nk, :], x_src_s if sample else src_tile(i), w=["F1" + K])
                dma(B["H3"][:nk, :], zscr[i, :nk, :], r=["zscr%d" % i], w=["H3" + K], eng="pool")
                if sample:
                    gv = gcs_d.rearrange("(k i) (g c) -> i k g c", i=16, c=16)
                    for ii in range(16):
                        dma(gv[ii], Uc[:4, :, ii, :], r=allU, w=["gcs_d"], slow=True)
                    dma(B["H1"][:nk, :], gcs_d, r=["gcs_d"], w=["H1" + K])
                else:
                    dve(lambda e: e.tensor_copy(out=B["H1"][:nk, :].rearrange("p (g c) -> p g c", c=16), in_=Uc[:nk, :, i, :]), r=allU, w=["H1" + K])
                yield
                yield from tr8(B, k, B["H1"], "H1" + K, B["H2"], "H2" + K, False, nk)
                for hf in range(2):
                    proj_h(B, K, Wg, "Wg", nk, hf, bias_row=bglur)
                    yield
                    act(lambda e, hf=hf: e.activation(out=F2[:nk, hf * 512:(hf + 1) * 512], in_=B["psX"][:nk, 0:512], func=AF.Sigmoid),
                        r=[B["psXn"]], w=["F2" + K])
                dve(lambda e: e.tensor_tensor(out=F2[:nk, :], in0=F2[:nk, :], in1=B["H1"][:nk, :], op=ALU.mult), r=["F2" + K, "H1" + K], w=["F2" + K])
                dve(lambda e: e.tensor_tensor(out=B["H4"][:nk, :], in0=F2[:nk, :], in1=B["H3"][:nk, :], op=ALU.mult), r=["F2" + K, "H3" + K], w=["H4" + K])
                yield
                yield from tr8(B, k, B["H4"], "H4" + K, B["H2"], "H2" + K, False, nk)
                nt = nk
                for hf in range(2):
                    proj_h(B, K, Wo, "Wo", nk, hf)
                    yield
                    dve(lambda e, hf=hf: e.tensor_tensor(out=F2[:nt, hf * 512:(hf + 1) * 512], in0=B["psX"][:nt, 0:512],
                                                         in1=gateb[:nt, hf * 512:(hf + 1) * 512], op=ALU.mult),
                        r=[B["psXn"], "gateb"], w=["F2" + K])
                dve(lambda e: e.scalar_tensor_tensor(out=F1[:nt, :], in0=F1[:nt, :], scalar=ALPHA, in1=F2[:nt, :], op0=ALU.mult, op1=ALU.add),
                    r=["F1" + K, "F2" + K], w=["F1" + K])
                dve(lambda e: e.bn_stats(out=s1k[:nt, 0:6], in_=F1[:nt, 0:512]), r=["F1" + K], w=["s1" + K])
                dve(lambda e: e.bn_stats(out=s1k[:nt, 6:12], in_=F1[:nt, 512:1024]), r=["F1" + K, "s1" + K], w=["s1" + K])
                dve(lambda e: e.bn_aggr(out=s1k[:nt, 12:14], in_=s1k[:nt, 0:12]), r=["s1" + K], w=["s1b" + K])
                yield
                act(lambda e: e.activation(out=s1k[:nt, 14:15], in_=s1k[:nt, 13:14], func=AF.Sqrt, bias=epsb[:nt, 0:1]), r=["s1b" + K, "epsb"], w=["s1c" + K])
                dve(lambda e: e.reciprocal(out=s1k[:nt, 15:16], in_=s1k[:nt, 14:15]), r=["s1c" + K], w=["s1d" + K])
                dve(lambda e: e.scalar_tensor_tensor(out=F2[:nt, :], in0=F1[:nt, :], scalar=s1k[:nt, 12:13], in1=lngb[:nt, :], op0=ALU.subtract, op1=ALU.mult),
                    r=["F1" + K, "s1b" + K, "lngb"], w=["F2" + K])
                dve(lambda e: e.scalar_tensor_tensor(out=F2[:nt, :], in0=F2[:nt, :], scalar=s1k[:nt, 15:16], in1=lnbb[:nt, :], op0=ALU.mult, op1=ALU.add),
                    r=["F2" + K, "s1d" + K, "lnbb"], w=["F2" + K])
                dma(x_dst_s if sample else dst_tile(i), F2[:nt, :], r=["F2" + K], w=["dst"], eng="pool")
            return gen

        if s == 2:
            run_streams([p1_task(0, True)], 1)
        else:
            run_streams([p1_task(i) for i in range(16)], 4)
        if s == 2:
            load_w(Win, CUR["next_win"], "Win", 2048)
        sc1 = S5["sc1"]; sc2 = S5["sc2"]

        def scan_gen(hf):
            gsl = slice(16 * hf, 16 * hf + 16)
            R = "SH%d" % hf; Z = "_%d" % hf
            TA = [xt1, t2][hf]; TB = [t1, szt][hf]
            TAn = ["F1_0", "F1_1"][hf]; TBn = ["F2_0", "F2_1"][hf]

            def cmac(dk, sk, lev, cnt):
                step = 2 << lev
                A1 = A1s[la][lev]; A2 = A2s[la][lev]
                j0 = 0
                while j0 < cnt:
                    m = min(32, cnt - j0)
                    d = SH[:, gsl, :, dk + j0 * step:dk + (j0 + m - 1) * step + 1:step]
                    sr = SH[:, gsl, :, sk + j0 * step:sk + (j0 + m - 1) * step + 1:step]
                    ta = TA[:, 0:16 * 2 * m].rearrange("p (g c m) -> p g c m", g=16, c=2)
                    tb = TB[:, 0:16 * 2 * m].rearrange("p (g c m) -> p g c m", g=16, c=2)
                    a1b = A1[:, gsl, :].unsqueeze(3).broadcast_to([128, 16, 2, m])
                    dve(lambda e, ta=ta, sr=sr, a1b=a1b: e.tensor_tensor(out=ta, in0=sr, in1=a1b, op=ALU.mult), r=[R, "A1"], w=[TAn])
                    pool(lambda e, tb=tb, sr=sr, A2=A2, m=m: e.tensor_tensor(out=tb[:, :, 0, :], in0=sr[:, :, 1, :],
                                                                             in1=A2[:, gsl, 0:1].broadcast_to([128, 16, m]), op=ALU.mult),
                         r=[R, "A2"], w=[TBn + "a"])
                    pool(lambda e, tb=tb, sr=sr, A2=A2, m=m: e.tensor_tensor(out=tb[:, :, 1, :], in0=sr[:, :, 0, :],
                                                                             in1=A2[:, gsl, 1:2].broadcast_to([128, 16, m]), op=ALU.mult),
                         r=[R, "A2"], w=[TBn + "b"])
                    dve(lambda e, ta=ta, tb=tb: e.tensor_tensor(out=ta, in0=ta, in1=tb, op=ALU.add), r=[TAn, TBn + "a", TBn + "b"], w=[TAn])
                    dve(lambda e, ta=ta, d=d: e.tensor_tensor(out=d, in0=d, in1=ta, op=ALU.add), r=[TAn, R], w=[R])
                    j0 += m
                    yield

            yield from cmac(2, 1, 0, nk // 2)
            yield from cmac(4, 2, 1, nk // 4)
            A1 = A1s[la][2]; A2 = A2s[la][2]
            for k in range(0, nk, 4):
                v0 = SH[:, gsl, :, k]; v1 = SH[:, gsl, :, k + 4]
                dve(lambda e, v0=v0: e.tensor_tensor(out=sc1[:, gsl, :], in0=A1[:, gsl, :], in1=v0, op=ALU.mult), r=[R, "A1"], w=["sc1" + Z])
                pool(lambda e, k=k: e.tensor_tensor(out=sc2[:, gsl, 0], in0=A2[:, gsl, 0], in1=SH[:, gsl, 1, k], op=ALU.mult), r=[R, "A2"], w=["sc2a" + Z])
                pool(lambda e, k=k: e.tensor_tensor(out=sc2[:, gsl, 1], in0=A2[:, gsl, 1], in1=SH[:, gsl, 0, k], op=ALU.mult), r=[R, "A2"], w=["sc2b" + Z])
                dve(lambda e: e.tensor_tensor(out=sc1[:, gsl, :], in0=sc1[:, gsl, :], in1=sc2[:, gsl, :], op=ALU.add),
                    r=["sc1" + Z, "sc2a" + Z, "sc2b" + Z], w=["sc1" + Z])
                dve(lambda e, v1=v1: e.tensor_tensor(out=v1, in0=v1, in1=sc1[:, gsl, :], op=ALU.add), r=["sc1" + Z, R], w=[R])
                if (k // 4) % 2 == 1:
                    yield
            yield from cmac(2, 0, 1, nk // 4)
            yield from cmac(1, 0, 0, nk // 2)

        P.barrier()
        h0_fn()
        run_streams([p2_task(g) for g in range(32)], 4)
        run_streams([p2_task(g) for g in range(32, 64)], 4, extra=[scan_gen(0)])
        run_streams([p4_task(g) for g in range(32)], 4, extra=[scan_gen(1)])
        hout_fn()
        run_streams([p4_task(g) for g in range(32, 64)], 4)
        P.barrier()
        if s == 2:
            run_streams([p5_task(0, True)], 1)
        else:
            run_streams([p5_task(i) for i in range(16)], 4)

    def s5_layer(la, l, src, dst):
        stop_if = DBG["stop_if"]
        layer_ln(l)
        if la > 0:
            load_w(Wg, w_glu[la], "Wg", 1024)
            load_w(Wo, w_out_a[la], "Wo", 1024)
            dma(bglur[0:1, :], b_glu[la:la + 1, :], w=["bglur"], eng="pool")
        CUR["next_win"] = w_in_a[1] if la == 0 else w_in_b[0]
        push_scope()
        alloc_s5_seq()
        for s in range(3):
            load_gate(s)
            if s < 2:
                nk = 128
                sv = src[0][s].rearrange("(k i) d -> i k d", i=16)
                dv = dst[0][s].rearrange("(k i) d -> i k d", i=16)
                def h0_fn():
                    pool(lambda e: e.memset(S5["SH"][:, :, :, 0], 0.0), w=["SH0", "SH1"])
                def hout_fn(s=s):
                    for par in range(2):
                        for c in range(2):
                            dma(ssm_p[la, s].rearrange("(gh two) p c -> two c p gh", two=2)[par, c], S5["SH"][64 * par:64 * par + 64, :, c, 128],
                                r=["SH0", "SH1"], w=["ssm_out"], slow=True)
            else:
                nk = 4
                sv = src[1].rearrange("(k i) d -> i k d", i=16)
                dv = dst[1].rearrange("(k i) d -> i k d", i=16)
                def h0_fn():
                    for par in range(2):
                        for c in range(2):
                            dma(S5["SH"][64 * par:64 * par + 64, :, c, 0], st_in[la].rearrange("(gh two) p c -> two c p gh", two=2)[par, c],
                                w=["SH0", "SH1"], slow=True)
                def hout_fn():
                    for par in range(2):
                        for c in range(2):
                            dma(ssm_s[la].rearrange("(gh two) p c -> two c p gh", two=2)[par, c], S5["SH"][64 * par:64 * par + 64, :, c, 4],
                                r=["SH0", "SH1"], w=["ssm_out"], slow=True)
            s5_sequence(la, s, nk, lambda i, sv=sv: sv[i], lambda i, dv=dv: dv[i], h0_fn, hout_fn, src[1], dst[1])
        pop_scope()

    AT = {}

    def alloc_attn():
        AT["kf"] = T([128, 256]); AT["vf"] = T([128, 256]); AT["kb"] = T([128, 256], BF16)
        AT["kTs"] = [T([64, 4, 128], BF16) for _ in range(2)]
        AT["vbs"] = [T([128, 256], BF16) for _ in range(2)]
        AT["qb"] = T([128, D], BF16); AT["qT"] = T([64, 16, 128], BF16)
        AT["r1"] = T([128, 16, 32]); AT["r2"] = T([128, 16, 32])
        AT["sm"] = [T([128, 4, 256]) for _ in range(2)]; AT["eb"] = [T([128, 4, 256], BF16) for _ in range(2)]
        AT["eT"] = [T([128, 8, 128], BF16) for _ in range(2)]
        AT["st"] = [T([128, 20]) for _ in range(2)]; AT["rinv"] = T([128, 16])
        AT["xt"] = [T([128, D]) for _ in range(2)]; AT["sinkb"] = T([128, 16]); AT["nsinkb"] = T([128, 16])
        AT["ogb"] = T([128, D], BF16)
        AT["ckc"] = T([128, 256]); AT["ckb"] = T([128, 256], BF16)
        AT["xTp"] = T([128, 8, 128], BF16); AT["Wkv"] = T([128, 8, 512], BF16)

    def rope(src_ps, psname, nh, nt, j, out_ap, wname):
        sv = src_ps.rearrange("p (h d) -> p h d", d=64)
        ov = out_ap.rearrange("p (h d) -> p h d", d=64)
        cb = cosT[:nt, j:j + 1, :].broadcast_to([nt, nh, 32]); sb = sinT[:nt, j:j + 1, :].broadcast_to([nt, nh, 32])
        dve(lambda e: e.tensor_tensor(out=AT["r1"][:nt, :nh, :], in0=sv[:, :, 0:32], in1=cb, op=ALU.mult), r=[psname, "cosT"], w=["r1"])
        dve(lambda e: e.tensor_tensor(out=AT["r2"][:nt, :nh, :], in0=sv[:, :, 32:64], in1=sb, op=ALU.mult), r=[psname, "sinT"], w=["r2"])
        dve(lambda e: e.tensor_tensor(out=ov[:, :, 0:32], in0=AT["r1"][:nt, :nh, :], in1=AT["r2"][:nt, :nh, :], op=ALU.subtract), r=["r1", "r2"], w=[wname])
        dve(lambda e: e.tensor_tensor(out=AT["r1"][:nt, :nh, :], in0=sv[:, :, 32:64], in1=cb, op=ALU.mult), r=[psname, "cosT"], w=["r1"])
        dve(lambda e: e.tensor_tensor(out=AT["r2"][:nt, :nh, :], in0=sv[:, :, 0:32], in1=sb, op=ALU.mult), r=[psname, "sinT"], w=["r2"])
        dve(lambda e: e.tensor_tensor(out=ov[:, :, 32:64], in0=AT["r1"][:nt, :nh, :], in1=AT["r2"][:nt, :nh, :], op=ALU.add), r=["r1", "r2", wname], w=[wname])

    def kT_from(kb_t, rname, nt, slot):
        for h in range(4):
            pe(lambda e, h=h: e.transpose(out=psT[0:64, h * 128:h * 128 + nt], in_=kb_t[:nt, h * 64:(h + 1) * 64], identity=identb[:nt, :nt]),
               r=[rname, "identb"], w=["psT"], mark=(h == 3))
        dve(lambda e: e.tensor_copy(out=AT["kTs"][slot][:, :, :nt], in_=psT[0:64, 0:512].rearrange("p (h t) -> p h t", t=128)[:, :, :nt]),
            r=["psT"], w=["kT%d" % slot])

    def attn_pro(first_attn, s, nt, X, xn, cur, rope_j, t0, kv_out):
        act(lambda e: e.copy(out=xbt[:nt, :], in_=X[:nt, :]), r=[xn], w=["xbt"])
        yield
        for c in range(8):
            pe(lambda e, c=c: e.transpose(out=psT[:, c * 128:c * 128 + nt], in_=xbt[:nt, c * 128:(c + 1) * 128],
                                          identity=identb[:nt, :nt]), r=["xbt", "identb"], w=["psT"], mark=(c == 7))
        yield
        for c in range(8):
            act(lambda e, c=c, cp=CUR["condP"]: e.activation(out=hmT[:, c, :nt], in_=psT[:, c * 128:c * 128 + nt], func=AF.Identity,
                                                             bias=cp[:, c, s:s + 1], scale=cp[:, 8 + c, s:s + 1]),
                r=["psT"], w=["hmT"], mark=(c == 7))
        if first_attn:
            dve(lambda e: e.tensor_copy(out=AT["xTp"][:, :, :nt], in_=psT[:, :].rearrange("p (c t) -> p c t", t=128)[:, :, :nt]),
                r=["psT"], w=["xTp"])
        yield
        if first_attn:
            proj(psA[:, 1024:2048], AT["xTp"], "xTp", AT["Wkv"], "Wkv", nt, 512, ["psA_hi"])
            yield
            rope(psA[:nt, 1024:1280], "psA_hi", 4, nt, rope_j, AT["kf"][:nt, :], "kf")
            act(lambda e: e.copy(out=AT["vf"][:nt, :], in_=psA[:nt, 1280:1536]), r=["psA_hi"], w=["vf"])
            act(lambda e: e.copy(out=AT["kb"][:nt, :], in_=AT["kf"][:nt, :]), r=["kf"], w=["kb"])
            dve(lambda e: e.tensor_copy(out=AT["vbs"][cur][:nt, :], in_=AT["vf"][:nt, :]), r=["vf"], w=["vb%d" % cur])
            yield
            kT_from(AT["kb"], "kb", nt, cur)
            dma(ktscr[s, :, :, t0:t0 + nt], AT["kTs"][cur][:, :, :nt], r=["kT%d" % cur], w=["ktscr"], eng="pool")
            dma(vscr[s, t0:t0 + nt, :], AT["vbs"][cur][:nt, :], r=["vb%d" % cur], w=["vscr"], eng="pool")
            if kv_out is not None:
                dma(kv_out[0], AT["kf"][:nt, :], r=["kf"], w=["kvout"], eng="pool")
                dma(kv_out[1], AT["vf"][:nt, :], r=["vf"], w=["kvout"], eng="pool")
            yield
        else:
            dma(AT["kTs"][cur][:, :, :nt], ktscr[s, :, :, t0:t0 + nt], w=["kT%d" % cur])
            dma(AT["vbs"][cur][:nt, :], vscr[s, t0:t0 + nt, :], w=["vb%d" % cur])
        proj(psA, hmT, "hmT", AT["Win"], AT["Winn"], nt, 1024, ["psA_lo"])
        yield
        proj(psA[:, 1024:2048], hmT, "hmT", AT["Win"][:, :, 1024:2048], AT["Winn"], nt, 1024, ["psA_hi"])
        yield
        rope(psA[:nt, 0:1024], "psA_lo", 16, nt, rope_j, AT["qb"][:nt, :], "qb")
        act(lambda e: e.activation(out=szt[:nt, :], in_=psA[:nt, 1024:2048], func=AF.Silu), r=["psA_hi"], w=["szt"])
        yield
        for half, (pst, pname) in enumerate(((psT, "psT"), (psQ, "psQ"))):
            for hh in range(8):
                h = half * 8 + hh
                pe(lambda e, h=h, hh=hh, pst=pst: e.transpose(out=pst[0:64, hh * 128:hh * 128 + nt], in_=AT["qb"][:nt, h * 64:(h + 1) * 64],
                                                              identity=identb[:nt, :nt]),
                   r=["qb", "identb"], w=[pname], mark=(hh == 7))
            yield
            dve(lambda e, half=half, pst=pst: e.tensor_copy(out=AT["qT"][:, half * 8:half * 8 + 8, :nt],
                                                            in_=pst[0:64, :].rearrange("p (h t) -> p h t", t=128)[:, :, :nt]),
                r=[pname], w=["qT"])
        yield

    def attn_epi(s, nt, X, xn, dst_ap):
        dve(lambda e: e.tensor_tensor(out=t1[:nt, :].rearrange("p (h d) -> p h d", d=64), in0=psA[:nt, 0:1024].rearrange("p (h d) -> p h d", d=64),
                                      in1=AT["rinv"][:nt, :].unsqueeze(2).broadcast_to([nt, 16, 64]), op=ALU.mult), r=["psA_lo", "rinv"], w=["t1"])
        dve(lambda e: e.tensor_tensor(out=AT["ogb"][:nt, :], in0=t1[:nt, :], in1=szt[:nt, :], op=ALU.mult), r=["t1", "szt"], w=["ogb"])
        yield
        for c in range(8):
            pe(lambda e, c=c: e.transpose(out=psQ[:, c * 128:c * 128 + nt], in_=AT["ogb"][:nt, c * 128:(c + 1) * 128],
                                          identity=identb[:nt, :nt]), r=["ogb", "identb"], w=["psQ"], mark=(c == 7))
        yield
        dve(lambda e: e.tensor_copy(out=yT[:, :, :nt], in_=psQ[:, :].rearrange("p (c t) -> p c t", t=128)[:, :, :nt]), r=["psQ"], w=["yT"])
        yield
        proj(psB, yT, "yT", AT["Wo"], AT["Won"], nt, 1024, ["psB"])
        yield
        resid_ln(psB[:nt, 0:1024], "psB", 0, s, nt, dst_ap, "dst", X=X, xn=xn)
        yield

    def attn_heads(nt, prev, cur, mask, mname):
        nkeys = 128 + nt

        def hg_task(hg):
            def gen(sl):
                kvh = hg
                ps_s = psB if sl == 0 else psA[:, 1024:2048]
                psn = "psB" if sl == 0 else "psA_hi"
                ps_e = psT if sl == 0 else psQ
                pen = "psT" if sl == 0 else "psQ"
                sm = AT["sm"][sl]; eb = AT["eb"][sl]; eT = AT["eT"][sl]; st = AT["st"][sl]
                S = "_%d" % sl
                has_mask = mask is not None
                if has_mask:
                    pairs = ((mLa, mRa), (mLb, mRb)) if mname == "maskG" else ((mLa, mRa), (mL1, mR0))
                    for half in range(2):
                        for pi, (ml, mr) in enumerate(pairs):
                            pe(lambda e, half=half, ml=ml, mr=mr, pi=pi: e.matmul(ps_s[:nt, half * 512:(half + 1) * 512], lhsT=ml[0:1, :nt], rhs=mr[0:1, :],
                                                                                   start=(pi == 0), stop=False, skip_group_check=True),
                               r=["mk"], w=[psn], mark=False)
                for hh in range(4):
                    h = 4 * hg + hh
                    pe(lambda e, h=h, hh=hh: e.matmul(ps_s[:nt, hh * 256:hh * 256 + 128], lhsT=AT["qT"][:, h, :nt], rhs=AT["kTs"][prev][:, kvh, :],
                                                      start=(not has_mask), stop=(not has_mask), skip_group_check=True), r=["qT", "kT%d" % prev], w=[psn], mark=False)
                    pe(lambda e, h=h, hh=hh: e.matmul(ps_s[:nt, hh * 256 + 128:hh * 256 + 128 + nt], lhsT=AT["qT"][:, h, :nt],
                                                      rhs=AT["kTs"][cur][:, kvh, :nt], start=(not has_mask), stop=True, skip_group_check=True),
                       r=["qT", "kT%d" % cur], w=[psn], mark=(hh == 3))
                yield
                psv = ps_s[:nt, 0:1024].rearrange("p (h k) -> p h k", k=256)[:, :, :nkeys]
                dve(lambda e: e.reduce_max(out=st[:nt, 0:4], in_=psv, axis=AX.X), r=[psn], w=["st" + S])
                dve(lambda e: e.scalar_tensor_tensor(out=st[:nt, 4:8], in0=st[:nt, 0:4], scalar=-0.125, in1=AT["nsinkb"][:nt, 4 * hg:4 * hg + 4],
                                                     op0=ALU.mult, op1=ALU.min), r=["st" + S, "sinkb"], w=["st" + S])
                yield
                for hh in range(4):
                    act(lambda e, hh=hh: e.activation(out=eb[:nt, hh, :nkeys], in_=ps_s[:nt, hh * 256:hh * 256 + nkeys], func=AF.Exp, scale=0.125,
                                                      bias=st[:nt, 4 + hh:5 + hh], accum_out=st[:nt, 8 + hh:9 + hh]),
                        r=[psn, "st" + S], w=["eb" + S, "stb" + S], mark=(hh == 3))
                dve(lambda e: e.tensor_tensor(out=st[:nt, 12:16], in0=AT["sinkb"][:nt, 4 * hg:4 * hg + 4], in1=st[:nt, 4:8], op=ALU.add),
                    r=["sinkb", "st" + S], w=["stc" + S])
                act(lambda e: e.activation(out=st[:nt, 12:16], in_=st[:nt, 12:16], func=AF.Exp), r=["stc" + S], w=["stc" + S])
                dve(lambda e: e.tensor_tensor(out=st[:nt, 16:20], in0=st[:nt, 8:12], in1=st[:nt, 12:16], op=ALU.add),
                    r=["stb" + S, "stc" + S], w=["std" + S])
                dve(lambda e: e.reciprocal(out=AT["rinv"][:nt, 4 * hg:4 * hg + 4], in_=st[:nt, 16:20]), r=["std" + S], w=["rinv"])
                yield
                for hh in range(4):
                    pe(lambda e, hh=hh: e.transpose(out=ps_e[:, (2 * hh) * 128:(2 * hh) * 128 + nt], in_=eb[:nt, hh, 0:128], identity=identb[:nt, :nt]),
                       r=["eb" + S, "identb"], w=[pen], mark=False)
                    pe(lambda e, hh=hh: e.transpose(out=ps_e[:nt, (2 * hh + 1) * 128:(2 * hh + 1) * 128 + nt], in_=eb[:nt, hh, 128:128 + nt],
                                                    identity=identb[:nt, :nt]),
                       r=["eb" + S, "identb"], w=[pen], mark=(hh == 3))
                yield
                dve(lambda e: e.tensor_copy(out=eT[:, :, :nt], in_=ps_e[:, 0:1024].rearrange("p (b t) -> p b t", t=128)[:, :, :nt]), r=[pen], w=["eT" + S])
                yield
                for hh in range(4):
                    h = 4 * hg + hh
                    pe(lambda e, h=h, hh=hh: e.matmul(psA[:nt, h * 64:(h + 1) * 64], lhsT=eT[:, 2 * hh, :nt], rhs=AT["vbs"][prev][:, kvh * 64:(kvh + 1) * 64],
                                                      start=True, stop=False, skip_group_check=True), r=["eT" + S, "vb%d" % prev], w=["psA_lo"], mark=False)
                    pe(lambda e, h=h, hh=hh: e.matmul(psA[:nt, h * 64:(h + 1) * 64], lhsT=eT[:nt, 2 * hh + 1, :nt],
                                                      rhs=AT["vbs"][cur][:nt, kvh * 64:(kvh + 1) * 64],
                                                      start=False, stop=True, skip_group_check=True), r=["eT" + S, "vb%d" % cur], w=["psA_lo"], mark=(hh == 3))
            return gen

        run_streams([hg_task(hg) for hg in range(4)], 2)

    def run_gen(g):
        for _ in g:
            pass

    def attn_layer(lb, l, src, dst):
        first = (lb == 0)
        layer_ln(l)
        if first:
            push_scope()
            alloc_attn()
            AT["WinB"] = T([128, 8, 2048], BF16)
            load_w(Wo, w_out_b[0], "Wo", 1024)
            load_w(AT["Wkv"], w_kv, "Wkv", 512)
            load_w(AT["WinB"], w_in_b[1], "WinB", 2048)
            AT["Win"] = Win; AT["Wo"] = Wo; AT["Winn"] = "Win"; AT["Won"] = "Wo"
        else:
            load_w(Wo, w_out_b[1], "Wo", 1024)
            AT["Win"] = AT["WinB"]; AT["Wo"] = Wo; AT["Winn"] = "WinB"; AT["Won"] = "Wo"
        dma(AT["sinkb"][:], sinks[lb].partition_broadcast(128), w=["sinkb"])
        dve(lambda e: e.tensor_scalar(out=AT["nsinkb"][:], in0=AT["sinkb"][:], scalar1=-1.0, scalar2=None, op0=ALU.mult), r=["sinkb"], w=["sinkb"])
        tiles = []
        for s in range(2):
            for j in range(16):
                cur = j % 2
                tiles.append(dict(s=s, j=j, nt=128, sap=src[0][s, 128 * j:128 * (j + 1), :], dap=dst[0][s, 128 * j:128 * (j + 1), :],
                                  cur=cur, prev=1 - cur, mask=(mask0 if j == 0 else maskG), mname=("mask0" if j == 0 else "maskG"),
                                  rope_j=j, t0=128 * j, kv_out=((ck_p[s], cv_p[s]) if (first and j == 15) else None)))
        tiles.append(dict(s=2, j=0, nt=64, sap=src[1][:, :], dap=dst[1][:, :], cur=0, prev=1, mask=None, mname=None, rope_j=16, t0=0,
                          kv_out=((ck_s[:, :], cv_s[:, :]) if first else None)))
        XT = AT["xt"]

        def setup_seq(t):
            if t["s"] < 2:
                pool(lambda e: e.memset(AT["kTs"][1][:], 0.0), w=["kT1"])
                pool(lambda e: e.memset(AT["vbs"][1][:], 0.0), w=["vb1"])
            else:
                dma(AT["ckc"][:], ck_in, w=["ckc"])
                act(lambda e: e.copy(out=AT["ckb"][:], in_=AT["ckc"][:]), r=["ckc"], w=["ckb"])
                kT_from(AT["ckb"], "ckb", 128, 1)
                dma(AT["ckc"][:], cv_in, w=["ckc"])
                dve(lambda e: e.tensor_copy(out=AT["vbs"][1][:], in_=AT["ckc"][:]), r=["ckc"], w=["vb1"])

        def pro_of(idx):
            t = tiles[idx]
            return attn_pro(first, t["s"], t["nt"], XT[idx % 2], "axt%d" % (idx % 2), t["cur"], t["rope_j"], t["t0"], t["kv_out"])

        dma(XT[0][:128, :], tiles[0]["sap"], w=["axt0"])
        setup_seq(tiles[0])
        run_gen(pro_of(0))
        for idx, t in enumerate(tiles):
            xs = idx % 2
            if idx + 1 < len(tiles):
                nx = tiles[idx + 1]
                dma(XT[1 - xs][:nx["nt"], :], nx["sap"], w=["axt%d" % (1 - xs)])
            attn_heads(t["nt"], t["prev"], t["cur"], t["mask"], t["mname"])
            if t["j"] == 0:
                load_gate(t["s"])
            epi = attn_epi(t["s"], t["nt"], XT[xs], "axt%d" % xs, t["dap"])
            if idx + 1 < len(tiles):
                nx = tiles[idx + 1]
                if nx["j"] == 0:
                    setup_seq(nx)
                run_streams([], 1, extra=[epi, pro_of(idx + 1)])
            else:
                run_gen(epi)
        if not first:
            pop_scope()
        else:
            P.barrier()

    def dbg_dump(name, src_ap, shape, regions):
        o = nc.dram_tensor("dbg_" + name, list(shape), src_ap.dtype if hasattr(src_ap, "dtype") else F32, kind="ExternalOutput").ap()
        dma(o, src_ap, r=regions, w=["dbg_" + name], slow=True)

    def stop_if(tag, dumps):
        if DEBUG["stop"] == tag:
            P.barrier()
            for name, ap, shape in dumps():
                dbg_dump(name, ap, shape, [])
            raise _Stop()

    DBG["stop_if"] = stop_if
    try:
        load_w(Win, w_in_a[0], "Win", 2048)
        load_w(Wg, w_glu[0], "Wg", 1024)
        load_w(Wo, w_out_a[0], "Wo", 1024)
        dma(bglur[0:1, :], b_glu[0:1, :], w=["bglur"], eng="pool")
        push_scope()
        ada_alloc(); gen_alloc()

        def gen_both():
            yield from gen_run(0)
            yield from gen_run(1)
        run_streams([], 1, extra=[ada_all(), gen_both()])
        pop_scope()
        stop_if("S", lambda: [("condP0", condPs[0][:], [128, 24, 4]), ("condP3", condPs[3][:], [128, 24, 4]), ("gate_d", gate_d4, [4, 3, D]),
                              ("TXd", TXd2[0], [64, 128, 512]), ("Gd", Gd2[0], [64, 64, 512]),
                              ("A1", A1s[0][:], [128, 32, 2]), ("A2", A2s[0][:], [128, 32, 2]),
                              ("TXd1", TXd2[1], [64, 128, 512])])
        s5_layer(0, 0, (x_p, x_s), (xa_p, xa_s))
        stop_if("L0", lambda: [("xa_p", xa_p, [2, SEQ, D]), ("xa_s", xa_s, [DEC, D])])
        s5_layer(1, 1, (xa_p, xa_s), (xb_p, xb_s))
        stop_if("L1", lambda: [("xb_p", xb_p, [2, SEQ, D]), ("xb_s", xb_s, [DEC, D])])
        attn_layer(0, 2, (xb_p, xb_s), (xa_p, xa_s))
        stop_if("L2", lambda: [("xa_p", xa_p, [2, SEQ, D]), ("xa_s", xa_s, [DEC, D])])
        attn_layer(1, 3, (xa_p, xa_s), (y_p, y_s))
    except _Stop:
        pass
    P.barrier(["sp"])
    P.replay()
    P.close()
    return nc


_NC = None


def kernel(**inputs):
    global _NC
    f = lambda a: np.ascontiguousarray(np.asarray(a, dtype=np.float32))
    inp = {k: f(v) for k, v in inputs.items()}
    if _NC is None:
        _NC = build_nc()
    nc = _NC
    wnames = ["w_ada", "b_ada", "ln_g", "ln_b", "w_in_a", "ssm_a_re", "ssm_a_im", "ssm_b_re", "ssm_b_im", "ssm_c_re",
              "ssm_c_im", "ssm_d", "ssm_log_dt", "w_glu", "b_glu", "w_out_a", "w_kv", "w_in_b", "attn_sinks", "w_out_b"]
    in_maps = []
    for c in range(NCORES):
        m = {k: inp[k] for k in wnames}
        m["x_p"] = f(inp["x_prompt"][2 * c:2 * c + 2])
        m["x_s"] = f(inp["x_sample"][c])
        m["st_in"] = f(inp["state_ssm"][:, c])
        m["ck_in"] = f(inp["cache_k"][c].reshape(128, 256))
        m["cv_in"] = f(inp["cache_v"][c].reshape(128, 256))
        m["c_all"] = f(np.stack([inp["c_prompt"][2 * c], inp["c_prompt"][2 * c + 1], inp["c_sample"][c]]))
        in_maps.append(m)
    res = run_bass_kernel_spmd(nc, in_maps, core_ids=list(range(NCORES)))
    R = res.results
    y_prompt = np.concatenate([r["y_p"] for r in R], axis=0)
    y_sample = np.stack([r["y_s"] for r in R], axis=0)
    ssm_pp = np.concatenate([r["ssm_p"] for r in R], axis=1)
    ckp = np.concatenate([r["ck_p"] for r in R], axis=0).reshape(16, 128, 4, 64)
    cvp = np.concatenate([r["cv_p"] for r in R], axis=0).reshape(16, 128, 4, 64)
    ssm_ss = np.stack([r["ssm_s"] for r in R], axis=1)
    cks = np.stack([r["ck_s"] for r in R], axis=0).reshape(8, 64, 4, 64)
    cvs = np.stack([r["cv_s"] for r in R], axis=0).reshape(8, 64, 4, 64)
    return (y_prompt.astype(np.float32), y_sample.astype(np.float32), ssm_pp.astype(np.float32), ckp.astype(np.float32),
            cvp.astype(np.float32), ssm_ss.astype(np.float32), cks.astype(np.float32), cvs.astype(np.float32))
```

```python
import math
import numpy as np
import concourse.bass as bass
import concourse.mybir as mybir
from concourse.bass_utils import run_bass_kernel_spmd

F32 = mybir.dt.float32
BF16 = mybir.dt.bfloat16
AF = mybir.ActivationFunctionType
ALU = mybir.AluOpType
AX = mybir.AxisListType

D = 1024
SEQ = 2048
DEC = 64
NCORES = 8
ALPHA = (2.0 * 4) ** 0.25
EPS = 1e-5
MAGIC = 12582912.0
TWO_PI = 2.0 * math.pi


class Prog:
    ENGS = ("pe", "act", "dve", "pool", "sp")

    def __init__(self, nc):
        self.nc = nc
        self.ops = {e: [] for e in self.ENGS}
        self.cur = {}
        self.waited = {e: {} for e in self.ENGS}
        self.lastw = {}
        self.reads = {}
        self.pending = {e: ([], []) for e in self.ENGS}
        self.dma_pool = []
        self.dma_rr = 0
        self.swdge_pool = []
        self.swdge_rr = 0
        self.nsem = 0
        self.stack = []

    def new_sem(self):
        cm = self.nc.semaphore("s%d" % self.nsem)
        self.nsem += 1
        s = cm.__enter__()
        self.stack.append(cm)
        return s

    def setup(self, n_dma=32):
        for e in self.ENGS:
            self.cur[e] = [self.new_sem(), 0]
        for _ in range(n_dma):
            self.dma_pool.append([self.new_sem(), 0])
        for _ in range(24):
            self.swdge_pool.append([self.new_sem(), 0])

    def _need(self, eng, ev, waits):
        if ev is None:
            return
        sem, val, src = ev
        if src == eng and eng == "pe":
            return
        k = id(sem)
        if self.waited[eng].get(k, (None, 0))[1] >= val:
            return
        self.waited[eng][k] = (sem, val)
        waits.append((sem, val))

    @staticmethod
    def _best(waits):
        best = {}
        for sem, val in waits:
            k = id(sem)
            if k not in best or best[k][1] < val:
                best[k] = (sem, val)
        return list(best.values())

    def op(self, eng, fn, reads=(), writes=(), mark=True, dma=False):
        al = {"psA_lo": ("psA_q0", "psA_q1"), "psA_hi": ("psA_q2", "psA_q3"), "psB": ("psB0", "psB1")}
        reads = [x for r in reads for x in al.get(r, (r,))]
        writes = [x for r in writes for x in al.get(r, (r,))]
        writes = list(writes) + [r for r in reads if r.startswith("ps")]
        reads = [r for r in reads if not r.startswith("ps")]
        waits = []
        for r in reads:
            self._need(eng, self.lastw.get(r), waits)
        for w in writes:
            self._need(eng, self.lastw.get(w), waits)
            for ev in self.reads.get(w, ()):
                self._need(eng, ev, waits)
        pr, pw = self.pending[eng]
        if dma:
            if eng == "pool":
                slot = self.swdge_pool[self.swdge_rr % len(self.swdge_pool)]
                self.swdge_rr += 1
            else:
                slot = self.dma_pool[self.dma_rr % len(self.dma_pool)]
                self.dma_rr += 1
            if slot[1] > 0:
                self._need(eng, (slot[0], slot[1], "dma"), waits)
            slot[1] += 16
            ev = (slot[0], slot[1], "dma")
            self.ops[eng].append((fn, self._best(waits), (slot[0], 16)))
            rr, ww = list(reads), list(writes)
        elif mark:
            c = self.cur[eng]
            if c[1] >= 2000:
                c = self.cur[eng] = [self.new_sem(), 0]
            c[1] += 1
            ev = (c[0], c[1], eng)
            self.ops[eng].append((fn, self._best(waits), (c[0], 1)))
            rr, ww = list(reads) + pr, list(writes) + pw
            self.pending[eng] = ([], [])
        else:
            self.ops[eng].append((fn, self._best(waits), None))
            pr.extend(reads)
            pw.extend(writes)
            return None
        for r in rr:
            self.reads.setdefault(r, []).append(ev)
        for w in ww:
            self.lastw[w] = ev
            self.reads[w] = []
        return ev

    def barrier(self, engs=None):
        evs = list(self.lastw.values())
        for l in self.reads.values():
            evs.extend(l)
        for eng in (engs or self.ENGS):
            waits = []
            for ev in evs:
                if ev is not None:
                    sem, val, src = ev
                    k = id(sem)
                    if self.waited[eng].get(k, (None, 0))[1] >= val:
                        continue
                    self.waited[eng][k] = (sem, val)
                    waits.append((sem, val))
            self.ops[eng].append((None, self._best(waits), None))
        if engs is None:
            self.lastw = {}
            self.reads = {}

    def replay(self):
        nc = self.nc
        engmap = {"pe": "tensor", "act": "scalar", "dve": "vector", "pool": "gpsimd", "sp": "sync"}
        with nc.Block() as block:
            for e in self.ENGS:
                ops = self.ops[e]

                def body(engine, ops=ops):
                    for fn, waits, inc in ops:
                        for sem, val in waits:
                            engine.wait_ge(sem, val)
                        if fn is None:
                            continue
                        inst = fn(engine)
                        if inc is not None:
                            inst.then_inc(inc[0], inc[1])
                getattr(block, engmap[e])(body)

    def close(self):
        for cm in reversed(self.stack):
            cm.__exit__(None, None, None)


DEBUG = {"stop": None}


class _Stop(Exception):
    pass


def build_nc():
    nc = bass.Bass("TRN2", target_bir_lowering=False)

    def din(name, shape):
        return nc.dram_tensor(name, list(shape), F32, kind="ExternalInput").ap()

    def dout(name, shape):
        return nc.dram_tensor(name, list(shape), F32, kind="ExternalOutput").ap()

    def dscr(name, shape, dt=F32):
        return nc.dram_tensor(name, list(shape), dt).ap()

    x_p = din("x_p", [2, SEQ, D]); x_s = din("x_s", [DEC, D])
    st_in = din("st_in", [2, 64, 64, 2]); ck_in = din("ck_in", [128, 256]); cv_in = din("cv_in", [128, 256])
    c_all = din("c_all", [3, D])
    w_ada = din("w_ada", [4, D, 3 * D]); b_ada = din("b_ada", [4, 3 * D])
    ln_g = din("ln_g", [4, D]); ln_b = din("ln_b", [4, D])
    w_in_a = din("w_in_a", [2, D, 2 * D])
    a_re = din("ssm_a_re", [2, 64, 64]); a_im = din("ssm_a_im", [2, 64, 64])
    b_re = din("ssm_b_re", [2, 64, 64, 16]); b_im = din("ssm_b_im", [2, 64, 64, 16])
    c_re = din("ssm_c_re", [2, 64, 16, 64]); c_im = din("ssm_c_im", [2, 64, 16, 64])
    ssm_d = din("ssm_d", [2, D]); log_dt = din("ssm_log_dt", [2, 64])
    w_glu = din("w_glu", [2, D, D]); b_glu = din("b_glu", [2, D]); w_out_a = din("w_out_a", [2, D, D])
    w_kv = din("w_kv", [D, 512]); w_in_b = din("w_in_b", [2, D, 2 * D])
    sinks = din("attn_sinks", [2, 16]); w_out_b = din("w_out_b", [2, D, D])

    y_p = dout("y_p", [2, SEQ, D]); y_s = dout("y_s", [DEC, D])
    ssm_p = dout("ssm_p", [2, 2, 64, 64, 2]); ck_p = dout("ck_p", [2, 128, 256]); cv_p = dout("cv_p", [2, 128, 256])
    ssm_s = dout("ssm_s", [2, 64, 64, 2]); ck_s = dout("ck_s", [DEC, 256]); cv_s = dout("cv_s", [DEC, 256])

    xa_p = dscr("xa_p", [2, SEQ, D]); xa_s = dscr("xa_s", [DEC, D])
    xb_p = dscr("xb_p", [2, SEQ, D]); xb_s = dscr("xb_s", [DEC, D])
    zscr = dscr("zscr", [16, 128, D], BF16)
    ktscr = dscr("ktscr", [3, 64, 4, SEQ], BF16); vscr = dscr("vscr", [3, SEQ, 256], BF16)
    TXd2 = dscr("TXd", [2, 64, 128, 512], BF16); Gd2 = dscr("Gd", [2, 64, 64, 512], BF16)
    a16d = dscr("a16d", [2, 64, 64], F32)
    ucs_d = dscr("ucs_d", [DEC, D], BF16); gcs_d = dscr("gcs_d", [DEC, D], BF16)

    P = Prog(nc)
    P.setup()
    _cnt = [0]

    scopes = []

    def T(shape, dt=F32, name=None):
        _cnt[0] += 1
        nm = "t%d" % _cnt[0]
        if scopes:
            cm = nc.sbuf_tensor(nm, list(shape), dt)
            t = cm.__enter__()
            scopes[-1].append(cm)
            return t
        return nc.alloc_sbuf_tensor(nm, list(shape), dt)

    def push_scope():
        P.barrier()
        scopes.append([])

    def pop_scope():
        P.barrier()
        for cm in reversed(scopes.pop()):
            cm.__exit__(None, None, None)

    def dve(fn, r=(), w=(), mark=True): return P.op("dve", fn, r, w, mark)
    def act(fn, r=(), w=(), mark=True): return P.op("act", fn, r, w, mark)
    def pool(fn, r=(), w=(), mark=True): return P.op("pool", fn, r, w, mark)
    def pe(fn, r=(), w=(), mark=True): return P.op("pe", fn, r, w, mark)
    def dma(out, in_, r=(), w=(), eng="sp", slow=False):
        if slow:
            return P.op(eng, lambda e: e.dma_start(out=out, in_=in_, allow_slow_non_contiguous=True), r, w, dma=True)
        return P.op(eng, lambda e: e.dma_start(out=out, in_=in_), r, w, dma=True)

    psA = nc.alloc_psum_tensor("psA", [128, 2048], F32)
    psB = nc.alloc_psum_tensor("psB", [128, 1024], F32)
    psT = nc.alloc_psum_tensor("psT", [128, 1024], BF16)
    psQ = nc.alloc_psum_tensor("psQ", [128, 1024], BF16)

    ident = T([128, 128]); identb = T([128, 128], BF16)
    pool(lambda e: e.memset(ident[:], 0.0), w=["ident"])
    pool(lambda e: e.affine_select(out=ident[:], in_=ident[:], pattern=[[-1, 128]], compare_op=ALU.not_equal,
                                   fill=1.0, base=0, channel_multiplier=1), r=["ident"], w=["ident"])
    dve(lambda e: e.tensor_copy(out=identb[:], in_=ident[:]), r=["ident"], w=["identb"])
    dsel = T([128, 256])
    pool(lambda e: e.memset(dsel[:], 0.0), w=["dsel"])
    dve(lambda e: e.tensor_copy(out=dsel[:, 0:128], in_=ident[:]), r=["ident", "dsel"], w=["dsel"])
    tmask = T([128, 16, 16])
    pool(lambda e: e.memset(tmask[:], 1.0), w=["tmask"])
    pool(lambda e: e.affine_select(out=tmask[:], in_=tmask[:], pattern=[[16, 16], [0, 16]], compare_op=ALU.is_ge,
                                   fill=0.0, base=15, channel_multiplier=-1), r=["tmask"], w=["tmask"])
    mLa = T([1, 128], BF16); mLb = T([1, 128], BF16); mL1 = T([1, 128], BF16)
    mRa = T([1, 512], BF16); mRb = T([1, 512], BF16); mR0 = T([1, 512], BF16)
    pool(lambda e: e.memset(mLa[:], 0.0), w=["mk"]); pool(lambda e: e.memset(mLa[0:1, 0:64], 1.0), r=["mk"], w=["mk"])
    pool(lambda e: e.memset(mLb[:], 0.0), r=["mk"], w=["mk"]); pool(lambda e: e.memset(mLb[0:1, 64:128], 1.0), r=["mk"], w=["mk"])
    pool(lambda e: e.memset(mL1[:], 1.0), r=["mk"], w=["mk"])
    for hh2 in range(2):
        pool(lambda e, hh2=hh2: e.memset(mRa[0:1, hh2 * 256:hh2 * 256 + 192], 0.0), r=["mk"], w=["mk"])
        pool(lambda e, hh2=hh2: e.memset(mRa[0:1, hh2 * 256 + 192:hh2 * 256 + 256], -1e30), r=["mk"], w=["mk"])
        pool(lambda e, hh2=hh2: e.memset(mRb[0:1, hh2 * 256 + 64:hh2 * 256 + 256], 0.0), r=["mk"], w=["mk"])
        pool(lambda e, hh2=hh2: e.memset(mRb[0:1, hh2 * 256:hh2 * 256 + 64], -1e30), r=["mk"], w=["mk"])
        pool(lambda e, hh2=hh2: e.memset(mR0[0:1, hh2 * 256 + 128:hh2 * 256 + 256], 0.0), r=["mk"], w=["mk"])
        pool(lambda e, hh2=hh2: e.memset(mR0[0:1, hh2 * 256:hh2 * 256 + 128], -1e30), r=["mk"], w=["mk"])
    maskG = "maskG"; mask0 = "mask0"
    epsb = T([128, 1])
    pool(lambda e: e.memset(epsb[:], EPS), w=["epsb"])

    def range_sin(out, ang_over_2pi, shape_tmp, r, w, nparts, offs=0.0):
        t1 = shape_tmp[0]; t2 = shape_tmp[1]
        dve(lambda e: e.tensor_scalar(out=t1, in0=ang_over_2pi, scalar1=offs, scalar2=MAGIC, op0=ALU.add, op1=ALU.add),
            r=r, w=["rs_t1"])
        dve(lambda e: e.tensor_scalar(out=t1, in0=t1, scalar1=MAGIC, scalar2=None, op0=ALU.subtract), r=["rs_t1"], w=["rs_t1"])
        dve(lambda e: e.scalar_tensor_tensor(out=t2, in0=ang_over_2pi, scalar=offs, in1=t1, op0=ALU.add, op1=ALU.subtract),
            r=r + ["rs_t1"], w=["rs_t2"])
        act(lambda e: e.activation(out=out, in_=t2, func=AF.Sin, scale=TWO_PI), r=["rs_t2"], w=w)

    cosT = T([128, 17, 32]); sinT = T([128, 17, 32])
    push_scope()
    posf = T([128, 17]); invf = T([128, 32]); rang = T([128, 17, 32]); rt1 = T([128, 17, 32]); rt2 = T([128, 17, 32])
    pool(lambda e: e.iota(posf[:, 0:16], pattern=[[128, 16]], base=0, channel_multiplier=1,
                          allow_small_or_imprecise_dtypes=True), w=["posf"])
    pool(lambda e: e.iota(posf[:, 16:17], pattern=[[0, 1]], base=1024, channel_multiplier=1,
                          allow_small_or_imprecise_dtypes=True), r=["posf"], w=["posf"])
    for jj in range(32):
        pool(lambda e, jj=jj: e.memset(invf[:, jj:jj + 1], float(np.float32(np.power(np.float32(10000.0), np.float32(-jj / 32.0))))),
             r=["invf"], w=["invf"])
    dve(lambda e: e.tensor_tensor(out=rang[:], in0=posf[:, :].unsqueeze(2).broadcast_to([128, 17, 32]),
                                  in1=invf[:, :].unsqueeze(1).broadcast_to([128, 17, 32]), op=ALU.mult),
        r=["posf", "invf"], w=["rang"])
    dve(lambda e: e.tensor_scalar(out=rang[:], in0=rang[:], scalar1=1.0 / TWO_PI, scalar2=None, op0=ALU.mult), r=["rang"], w=["rang"])
    range_sin(sinT[:], rang[:], (rt1[:], rt2[:]), ["rang"], ["sinT"], 128, 0.0)
    range_sin(cosT[:], rang[:], (rt1[:], rt2[:]), ["rang"], ["cosT"], 128, 0.25)
    pop_scope()

    lngb = T([128, D]); lnbb = T([128, D])
    condPs = [T([128, 24, 4]) for _ in range(4)]; gateb = T([128, D])
    gate_d4 = dscr("gate_d", [4, 3, D])
    CUR = {}

    def load_gate(s):
        dma(gateb[:], gate_d4[CUR["l"], s].partition_broadcast(128), r=["gate_d"], w=["gateb"])

    ADA = {}

    def ada_alloc():
        ADA["cT"] = T([128, 3, 8]); ADA["cTf"] = T([128, 8, 4])
        ADA["wts"] = [T([128, 1536]) for _ in range(3)]
        ADA["bbT"] = T([128, 24]); ADA["zt"] = T([1, 128])

    def ada_all():
        cT = ADA["cT"]; cTf = ADA["cTf"]; wts = ADA["wts"]; bbT = ADA["bbT"]; zt = ADA["zt"]
        pso = psA[:, 1024:1120]
        pool(lambda e: e.memset(zt[:], 0.0), w=["zt"])
        for s in range(3):
            dma(cT[:, s, :], c_all[s].rearrange("(c p) -> p c", p=128), w=["cT"], slow=True)
        act(lambda e: e.activation(out=cT[:], in_=cT[:], func=AF.Silu), r=["cT"], w=["cT"])
        dve(lambda e: e.tensor_copy(out=cTf[:, :, 0:3], in_=cT[:, :, :].rearrange("p s c -> p c s")), r=["cT"], w=["cTf"])
        chunks = [(l, c, hf) for l in range(4) for c in range(8) for hf in range(2)]

        def load(i):
            l, c, hf = chunks[i]
            dma(wts[i % 3][:, :], w_ada[l, c * 128:(c + 1) * 128, hf * 1536:(hf + 1) * 1536], w=["wada%d" % (i % 3)],
                eng=("sp" if i % 2 == 0 else "pool"))
        load(0); load(1)
        for i, (l, c, hf) in enumerate(chunks):
            condP = condPs[l]
            if i + 2 < len(chunks):
                load(i + 2)
            yield
            wt = wts[i % 3]; wn = "wada%d" % (i % 3)
            if c == 0 and hf == 0:
                dma(bbT[:], b_ada[l].rearrange("(c p) -> p c", p=128), w=["bbT"], slow=True)
                pe(lambda e: e.matmul(pso, lhsT=zt[0:1, 0:128], rhs=zt[0:1, 0:96], start=True, stop=False, skip_group_check=True),
                   r=["zt"], w=["psA_hi"], mark=False)
            for cc in range(12):
                ch = hf * 12 + cc
                pe(lambda e, c=c, ch=ch, cc=cc, wt=wt: e.matmul(pso[:, ch * 4:ch * 4 + 3], lhsT=wt[:, cc * 128:(cc + 1) * 128],
                                                                rhs=cTf[:, c, 0:3], start=False, stop=(c == 7), skip_group_check=True),
                   r=[wn, "cTf"], w=["psA_hi"], mark=(cc == 11))
            if c == 7 and hf == 1:
                dve(lambda e, condP=condP: e.tensor_tensor(out=condP[:, :, 0:3], in0=pso.rearrange("p (a b) -> p a b", b=4)[:, :, 0:3],
                                                           in1=bbT[:, :].unsqueeze(2).broadcast_to([128, 24, 3]), op=ALU.add),
                    r=["psA_hi", "bbT"], w=["adaP%d" % l])
                dve(lambda e, condP=condP: e.tensor_scalar(out=condP[:, 8:16, 0:3], in0=condP[:, 8:16, 0:3], scalar1=1.0, scalar2=None, op0=ALU.add),
                    r=["adaP%d" % l], w=["adaP%d" % l])
                for s in range(3):
                    dma(gate_d4[l, s].rearrange("(c p) -> p c", p=128), condP[:, 16:24, s], r=["adaP%d" % l], w=["gate_d"], slow=True, eng="pool")
        yield

    def layer_ln(l):
        CUR["l"] = l; CUR["condP"] = condPs[l]
        dma(lngb[:], ln_g[l].partition_broadcast(128), w=["lngb"])
        dma(lnbb[:], ln_b[l].partition_broadcast(128), w=["lnbb"])

    Win = T([128, 8, 2048], BF16); Wg = T([128, 8, D], BF16); Wo = T([128, 8, D], BF16)

    def load_w(dst, src, name, ncols):
        v = src.rearrange("(c p) n -> p c n", p=128)
        for c in range(8):
            dma(dst[:, c, :], v[:, c, :], w=[name], eng="pool")

    xt1 = T([128, D]); xt = [xt1, xt1]
    xbt = T([128, D], BF16); hmT = T([128, 8, 128], BF16)
    t2 = T([128, D]); t1 = T([128, D]); junk = t1
    s1 = T([128, 16])
    szt = T([128, D]); sg = szt; szb = T([128, D], BF16)
    ybt = T([128, D], BF16); yT = T([128, 8, 128], BF16)

    def run_streams(tasks, ns, extra=(), stagger=0):
        free = list(range(ns))
        active = [(g, None) for g in extra]
        it = iter(tasks)
        rnd = 0
        started = 0
        exhausted = False
        while True:
            while free and not exhausted:
                if started < ns and rnd < started * stagger:
                    break
                try:
                    f = next(it)
                except StopIteration:
                    exhausted = True
                    break
                sl = free.pop(0)
                active.append((f(sl), sl))
                started += 1
            if not active:
                if exhausted or not free:
                    break
                rnd += 1
                continue
            for item in list(active):
                try:
                    next(item[0])
                except StopIteration:
                    active.remove(item)
                    if item[1] is not None:
                        free.append(item[1])
            rnd += 1

    def transpose_to(dstT, src_b, nt, rname, wname):
        for c in range(8):
            pe(lambda e, c=c: e.transpose(out=psT[:, c * 128:c * 128 + nt], in_=src_b[:nt, c * 128:(c + 1) * 128],
                                          identity=identb[:nt, :nt]), r=[rname, "identb"], w=["psT"], mark=(c == 7))
        dve(lambda e: e.tensor_copy(out=dstT[:, :, :nt], in_=psT[:, :].rearrange("p (c t) -> p c t", t=128)[:, :, :nt]),
            r=["psT"], w=[wname])

    def load_mod(src_ap, nt, slot, s, want_plain, X=None, xn="xt0"):
        if X is None:
            X = xt[slot]
            dma(X[:nt, :], src_ap, w=[xn])
        act(lambda e: e.copy(out=xbt[:nt, :], in_=X[:nt, :]), r=[xn], w=["xbt"])
        for c in range(8):
            pe(lambda e, c=c: e.transpose(out=psT[:, c * 128:c * 128 + nt], in_=xbt[:nt, c * 128:(c + 1) * 128],
                                          identity=identb[:nt, :nt]), r=["xbt", "identb"], w=["psT"], mark=(c == 7))
        for c in range(8):
            act(lambda e, c=c, cp=CUR["condP"]: e.activation(out=hmT[:, c, :nt], in_=psT[:, c * 128:c * 128 + nt], func=AF.Identity,
                                                             bias=cp[:, c, s:s + 1], scale=cp[:, 8 + c, s:s + 1]),
                r=["psT"], w=["hmT"], mark=(c == 7))
        if want_plain:
            dve(lambda e: e.tensor_copy(out=AT["xTp"][:, :, :nt], in_=psT[:, :].rearrange("p (c t) -> p c t", t=128)[:, :, :nt]),
                r=["psT"], w=["xTp"])

    def proj(ps_out, lhsT_t, rname, W, wname, nt, ncol, psname, bias_row=None):
        for n in range(ncol // 512):
            if bias_row is not None:
                pe(lambda e, n=n: e.matmul(ps_out[:nt, n * 512:(n + 1) * 512], lhsT=onesb[0:1, :nt], rhs=bias_row[0:1, n * 512:(n + 1) * 512],
                                           start=True, stop=False), r=["onesb", "bglur"],
                   w=(list(psname) if isinstance(psname, (list, tuple)) else [psname]), mark=False)
            for c in range(8):
                pe(lambda e, n=n, c=c: e.matmul(ps_out[:nt, n * 512:(n + 1) * 512], lhsT=lhsT_t[:, c, :nt],
                                                rhs=W[:, c, n * 512:(n + 1) * 512], start=(c == 0 and bias_row is None), stop=(c == 7)),
                   r=[rname, wname], w=(list(psname) if isinstance(psname, (list, tuple)) else [psname]),
                   mark=(c == 7 and n == ncol // 512 - 1))

    def resid_ln(ps_o, psname, slot, s, nt, dst_ap, wregion, X=None, xn="xt0"):
        if X is None:
            X = xt[slot]
        XO = t1; xon = "t1"
        dve(lambda e: e.tensor_tensor(out=t1[:nt, :], in0=ps_o, in1=gateb[:nt, :], op=ALU.mult), r=[psname, "gateb"], w=["t1"])
        dve(lambda e: e.scalar_tensor_tensor(out=t2[:nt, :], in0=X[:nt, :], scalar=ALPHA, in1=t1[:nt, :], op0=ALU.mult, op1=ALU.add),
            r=[xn, "t1"], w=["t2"])
        dve(lambda e: e.bn_stats(out=s1[:nt, 0:6], in_=t2[:nt, 0:512]), r=["t2"], w=["s1a"])
        dve(lambda e: e.bn_stats(out=s1[:nt, 6:12], in_=t2[:nt, 512:1024]), r=["t2", "s1a"], w=["s1a"])
        dve(lambda e: e.bn_aggr(out=s1[:nt, 12:14], in_=s1[:nt, 0:12]), r=["s1a"], w=["s1b"])
        act(lambda e: e.activation(out=s1[:nt, 14:15], in_=s1[:nt, 13:14], func=AF.Sqrt, bias=epsb[:nt, 0:1]), r=["s1b", "epsb"], w=["s1c"])
        dve(lambda e: e.reciprocal(out=s1[:nt, 15:16], in_=s1[:nt, 14:15]), r=["s1c"], w=["s1d"])
        dve(lambda e: e.scalar_tensor_tensor(out=XO[:nt, :], in0=t2[:nt, :], scalar=s1[:nt, 12:13], in1=lngb[:nt, :], op0=ALU.subtract, op1=ALU.mult),
            r=["t2", "s1b", "lngb"], w=[xon])
        dve(lambda e: e.scalar_tensor_tensor(out=XO[:nt, :], in0=XO[:nt, :], scalar=s1[:nt, 15:16], in1=lnbb[:nt, :], op0=ALU.mult, op1=ALU.add),
            r=[xon, "s1d", "lnbb"], w=[xon])
        dma(dst_ap, XO[:nt, :], r=[xon], w=[wregion], eng="pool")

    A1s = [[T([128, 32, 2]) for _ in range(3)] for _ in range(2)]; A2s = [[T([128, 32, 2]) for _ in range(3)] for _ in range(2)]
    apw = T([128, 2, 32]); apt = T([128, 2, 32])
    GEN = {}
    bglur = T([1, D], BF16); onesb = T([1, 128], BF16)
    pool(lambda e: e.memset(onesb[:], 1.0), w=["onesb"])
    S5 = {}
    DBG = {}

    def alloc_s5_seq():
        S5["Uc"] = T([128, 64, 16, 16], BF16)
        S5["UTs"] = [T([128, 2, 128], BF16) for _ in range(4)]
        S5["TXs"] = [T([128, 512], BF16) for _ in range(4)]
        S5["Gs"] = [T([128, 512], BF16) for _ in range(4)]
        S5["SH"] = T([128, 32, 2, 130])
        S5["Hbs"] = [T([128, 2, 128], BF16) for _ in range(4)]
        S5["sc1"] = T([128, 32, 2]); S5["sc2"] = T([128, 32, 2])
        S5["gt"] = T([128, D], BF16)
        S5["h3b"] = T([128, D], BF16); S5["h4b"] = T([128, D], BF16)
        S5["s1"] = [T([128, 16]) for _ in range(4)]

    GB = 4

    def gen_alloc():
        GEN['are'] = T([64, 64])
        GEN['aim'] = T([64, 64])
        GEN['dtt'] = T([64, 64])
        GEN['rho'] = T([64, 64])
        GEN['th'] = T([64, 64])
        GEN['bre_t'] = T([64, 64, 16])
        GEN['bim_t'] = T([64, 64, 16])
        GEN['craw'] = T([128, 2, 8, 64])
        GEN['cre_t'] = T([64, 64, 16])
        GEN['cim_t'] = T([64, 64, 16])
        GEN['dcol'] = T([128, 64])
        GEN['ev'] = T([64, 40])
        GEN['a16'] = T([64, 2, 64])
        GEN['a16p'] = T([128, 2, 32])
        GEN['Pr'] = T([64, GB, 40])
        GEN['Pi'] = T([64, GB, 40])
        GEN['pa'] = T([64, GB, 40])
        GEN['pt1'] = T([64, GB, 40])
        GEN['pt2'] = T([64, GB, 40])
        GEN['pm'] = T([64, GB, 40])
        GEN['q1'] = T([64, GB])
        GEN['q2'] = T([64, GB])
        GEN['q3'] = T([64, GB])
        GEN['wr'] = T([64, GB])
        GEN['wi'] = T([64, GB])
        GEN['bbr'] = T([64, GB, 16])
        GEN['bbi'] = T([64, GB, 16])
        GEN['tb1'] = T([64, GB, 16])
        GEN['Fr'] = T([64, GB, 23, 16])
        GEN['Fi'] = T([64, GB, 23, 16])
        GEN['Ft'] = T([64, GB, 23, 16])
        GEN['Gr'] = T([64, GB, 17, 16])
        GEN['Gi'] = T([64, GB, 17, 16])
        GEN['Gt'] = T([64, GB, 17, 16])
        GEN['txs'] = [T([128, 512], BF16) for i in range(2)]
        GEN['gsb'] = [T([64, 2, 256], BF16) for i in range(2)]
        GEN['tbf'] = T([128, 256])

        for nm in ('en', 'er', 'ep', 'e2', 'et', 'wA', 'wB', 'wC', 'wD', 'wE', 'wrA', 'wiA'):
            GEN[nm] = T([64, 64])

    def gen_run(la):
        are = GEN['are']
        aim = GEN['aim']
        dtt = GEN['dtt']
        rho = GEN['rho']
        th = GEN['th']
        bre_t = GEN['bre_t']
        bim_t = GEN['bim_t']
        craw = GEN['craw']
        cre_t = GEN['cre_t']
        cim_t = GEN['cim_t']
        dcol = GEN['dcol']
        ev = GEN['ev']
        a16 = GEN['a16']
        a16p = GEN['a16p']
        Pr = GEN['Pr']
        Pi = GEN['Pi']
        pa = GEN['pa']
        pt1 = GEN['pt1']
        pt2 = GEN['pt2']
        pm = GEN['pm']
        q1 = GEN['q1']
        q2 = GEN['q2']
        q3 = GEN['q3']
        wr = GEN['wr']
        wi = GEN['wi']
        bbr = GEN['bbr']
        bbi = GEN['bbi']
        tb1 = GEN['tb1']
        Fr = GEN['Fr']
        Fi = GEN['Fi']
        Ft = GEN['Ft']
        Gr = GEN['Gr']
        Gi = GEN['Gi']
        Gt = GEN['Gt']
        txs = GEN['txs']
        gsb = GEN['gsb']
        tbf = GEN['tbf']
        TXd = TXd2[la]; Gd = Gd2[la]
        R = "g_"
        dma(are[:], a_re[la].rearrange("g p -> p g"), w=[R + "are"], slow=True)
        dma(aim[:], a_im[la].rearrange("g p -> p g"), w=[R + "aim"], slow=True)
        dma(dtt[:], log_dt[la].partition_broadcast(64), w=[R + "dtt"])
        dma(bre_t[:], b_re[la].rearrange("g p c -> p g c"), w=[R + "bre"])
        dma(bim_t[:], b_im[la].rearrange("g p c -> p g c"), w=[R + "bim"])
        dma(craw[:, 0, :, :], c_re[la].rearrange("(b g) c p -> (g c) b p", g=8), w=[R + "craw"])
        dma(craw[:, 1, :, :], c_im[la].rearrange("(b g) c p -> (g c) b p", g=8), w=[R + "craw"])
        for il in range(8):
            dma(dcol[16 * il:16 * il + 16, :], ssm_d[la].rearrange("(g c) -> c g", c=16), w=[R + "dcol"], slow=True)
        for ri, dst in ((0, cre_t), (1, cim_t)):
            for b in range(8):
                pe(lambda e, ri=ri, b=b: e.transpose(out=psA[0:64, b * 128:(b + 1) * 128], in_=craw[:, ri, b, :], identity=ident[:, :]),
                   r=[R + "craw", "ident"], w=["psA_lo"], mark=(b == 7))
            dve(lambda e, dst=dst: e.tensor_copy(out=dst[:, :, :], in_=psA[0:64, 0:1024].rearrange("p (g c) -> p g c", c=16)),
                r=["psA_lo"], w=[R + "c"])
        en = GEN['en']; er = GEN['er']; ep = GEN['ep']; e2 = GEN['e2']; et = GEN['et']
        dve(lambda e: e.tensor_scalar(out=en[:], in0=dtt[:], scalar1=1.0 / math.log(2.0), scalar2=MAGIC, op0=ALU.mult, op1=ALU.add),
            r=[R + "dtt"], w=[R + "en"])
        dve(lambda e: e.tensor_scalar(out=en[:], in0=en[:], scalar1=MAGIC, scalar2=None, op0=ALU.subtract), r=[R + "en"], w=[R + "en"])
        dve(lambda e: e.scalar_tensor_tensor(out=er[:], in0=en[:], scalar=-0.693359375, in1=dtt[:], op0=ALU.mult, op1=ALU.add),
            r=[R + "en", R + "dtt"], w=[R + "er"])
        dve(lambda e: e.scalar_tensor_tensor(out=er[:], in0=en[:], scalar=2.12194440e-4, in1=er[:], op0=ALU.mult, op1=ALU.add),
            r=[R + "en", R + "er"], w=[R + "er"])
        dve(lambda e: e.tensor_scalar(out=ep[:], in0=er[:], scalar1=1.0 / 5040.0, scalar2=None, op0=ALU.mult), r=[R + "er"], w=[R + "ep"])
        for ck in (1.0 / 720.0, 1.0 / 120.0, 1.0 / 24.0, 1.0 / 6.0, 0.5, 1.0):
            dve(lambda e, ck=ck: e.scalar_tensor_tensor(out=ep[:], in0=ep[:], scalar=ck, in1=er[:], op0=ALU.add, op1=ALU.mult),
                r=[R + "ep", R + "er"], w=[R + "ep"])
        dve(lambda e: e.tensor_scalar(out=ep[:], in0=ep[:], scalar1=1.0, scalar2=None, op0=ALU.add), r=[R + "ep"], w=[R + "ep"])
        pool(lambda e: e.memset(e2[:], 0.0), w=[R + "e2"])
        for kk in range(-16, 5):
            dve(lambda e, kk=kk: e.tensor_scalar(out=et[:], in0=en[:], scalar1=float(kk), scalar2=float(2.0 ** kk), op0=ALU.is_equal, op1=ALU.mult),
                r=[R + "en"], w=[R + "et"])
            dve(lambda e: e.tensor_tensor(out=e2[:], in0=e2[:], in1=et[:], op=ALU.add), r=[R + "e2", R + "et"], w=[R + "e2"])
        dve(lambda e: e.tensor_tensor(out=dtt[:], in0=ep[:], in1=e2[:], op=ALU.mult), r=[R + "ep", R + "e2"], w=[R + "dtt"])
        dve(lambda e: e.tensor_tensor(out=rho[:], in0=are[:], in1=dtt[:], op=ALU.mult), r=[R + "are", R + "dtt"], w=[R + "rho"])
        dve(lambda e: e.tensor_tensor(out=th[:], in0=aim[:], in1=dtt[:], op=ALU.mult), r=[R + "aim", R + "dtt"], w=[R + "th"])
        dve(lambda e: e.tensor_scalar(out=th[:], in0=th[:], scalar1=1.0 / TWO_PI, scalar2=None, op0=ALU.mult), r=[R + "th"], w=[R + "th"])
        pool(lambda e: e.iota(ev[:, 0:23], pattern=[[-1, 23]], base=15, channel_multiplier=0, allow_small_or_imprecise_dtypes=True), w=[R + "ev"])
        pool(lambda e: e.iota(ev[:, 23:40], pattern=[[1, 17]], base=0, channel_multiplier=0, allow_small_or_imprecise_dtypes=True), r=[R + "ev"], w=[R + "ev"])
        evb = ev[:, :].unsqueeze(1).broadcast_to([64, GB, 40])
        wA = GEN['wA']; wB = GEN['wB']; wC = GEN['wC']; wD = GEN['wD']; wE = GEN['wE']; wrA = GEN['wrA']; wiA = GEN['wiA']
        range_sin(wB[:], th[:], (wD[:], wE[:]), [R + "th"], [R + "wB"], 64, 0.0)
        range_sin(wA[:], th[:], (wD[:], wE[:]), [R + "th"], [R + "wA"], 64, 0.25)
        act(lambda e: e.activation(out=wC[:], in_=rho[:], func=AF.Exp), r=[R + "rho"], w=[R + "wC"])
        dve(lambda e: e.tensor_tensor(out=wA[:], in0=wA[:], in1=wC[:], op=ALU.mult), r=[R + "wA", R + "wC"], w=[R + "wA"])
        dve(lambda e: e.tensor_tensor(out=wB[:], in0=wB[:], in1=wC[:], op=ALU.mult), r=[R + "wB", R + "wC"], w=[R + "wB"])
        dve(lambda e: e.tensor_scalar(out=wA[:], in0=wA[:], scalar1=-1.0, scalar2=None, op0=ALU.add), r=[R + "wA"], w=[R + "wA"])
        dve(lambda e: e.tensor_tensor(out=wC[:], in0=are[:], in1=are[:], op=ALU.mult), r=[R + "are", R + "wC"], w=[R + "wC"])
        dve(lambda e: e.tensor_tensor(out=wD[:], in0=aim[:], in1=aim[:], op=ALU.mult), r=[R + "aim"], w=[R + "wD", "rs_t1"])
        dve(lambda e: e.tensor_tensor(out=wC[:], in0=wC[:], in1=wD[:], op=ALU.add), r=[R + "wC", R + "wD"], w=[R + "wC"])
        dve(lambda e: e.reciprocal(out=wC[:], in_=wC[:]), r=[R + "wC"], w=[R + "wC"])
        dve(lambda e: e.tensor_tensor(out=wrA[:], in0=wA[:], in1=are[:], op=ALU.mult), r=[R + "wA", R + "are"], w=[R + "wrA"])
        dve(lambda e: e.tensor_tensor(out=wD[:], in0=wB[:], in1=aim[:], op=ALU.mult), r=[R + "wB", R + "aim", R + "wD"], w=[R + "wD"])
        dve(lambda e: e.tensor_tensor(out=wrA[:], in0=wrA[:], in1=wD[:], op=ALU.add), r=[R + "wrA", R + "wD"], w=[R + "wrA"])
        dve(lambda e: e.tensor_tensor(out=wrA[:], in0=wrA[:], in1=wC[:], op=ALU.mult), r=[R + "wrA", R + "wC"], w=[R + "wrA"])
        dve(lambda e: e.tensor_tensor(out=wiA[:], in0=wB[:], in1=are[:], op=ALU.mult), r=[R + "wB", R + "are"], w=[R + "wiA"])
        dve(lambda e: e.tensor_tensor(out=wD[:], in0=wA[:], in1=aim[:], op=ALU.mult), r=[R + "wA", R + "aim", R + "wD"], w=[R + "wD"])
        dve(lambda e: e.tensor_tensor(out=wiA[:], in0=wiA[:], in1=wD[:], op=ALU.subtract), r=[R + "wiA", R + "wD"], w=[R + "wiA"])
        dve(lambda e: e.tensor_tensor(out=wiA[:], in0=wiA[:], in1=wC[:], op=ALU.mult), r=[R + "wiA", R + "wC"], w=[R + "wiA"])
        for gb in range(64 // GB):
            g0 = gb * GB
            gs = slice(g0, g0 + GB)
            dve(lambda e, gs=gs: e.tensor_tensor(out=pa[:], in0=th[:, gs].unsqueeze(2).broadcast_to([64, GB, 40]), in1=evb, op=ALU.mult),
                r=[R + "th", R + "ev"], w=[R + "pa"])
            dve(lambda e, gs=gs: e.tensor_tensor(out=pm[:], in0=rho[:, gs].unsqueeze(2).broadcast_to([64, GB, 40]), in1=evb, op=ALU.mult),
                r=[R + "rho", R + "ev"], w=[R + "pm"])
            act(lambda e: e.activation(out=pm[:], in_=pm[:], func=AF.Exp), r=[R + "pm"], w=[R + "pm"])
            range_sin(Pi[:], pa[:], (pt1[:], pt2[:]), [R + "pa"], [R + "Pi"], 64, 0.0)
            range_sin(Pr[:], pa[:], (pt1[:], pt2[:]), [R + "pa"], [R + "Pr"], 64, 0.25)
            dve(lambda e: e.tensor_tensor(out=Pr[:], in0=Pr[:], in1=pm[:], op=ALU.mult), r=[R + "Pr", R + "pm"], w=[R + "Pr"])
            dve(lambda e: e.tensor_tensor(out=Pi[:], in0=Pi[:], in1=pm[:], op=ALU.mult), r=[R + "Pi", R + "pm"], w=[R + "Pi"])
            wrb = wrA[:, gs].unsqueeze(2).broadcast_to([64, GB, 16]); wib = wiA[:, gs].unsqueeze(2).broadcast_to([64, GB, 16])
            brs = bre_t[:, gs, :]; bis = bim_t[:, gs, :]
            dve(lambda e, brs=brs, wrb=wrb: e.tensor_tensor(out=bbr[:], in0=brs, in1=wrb, op=ALU.mult), r=[R + "bre", R + "wrA"], w=[R + "bbr"])
            dve(lambda e, bis=bis, wib=wib: e.tensor_tensor(out=tb1[:], in0=bis, in1=wib, op=ALU.mult), r=[R + "bim", R + "wiA"], w=[R + "tb1"])
            dve(lambda e: e.tensor_tensor(out=bbr[:], in0=bbr[:], in1=tb1[:], op=ALU.subtract), r=[R + "bbr", R + "tb1"], w=[R + "bbr"])
            dve(lambda e, bis=bis, wrb=wrb: e.tensor_tensor(out=bbi[:], in0=bis, in1=wrb, op=ALU.mult), r=[R + "bim", R + "wrA"], w=[R + "bbi"])
            dve(lambda e, brs=brs, wib=wib: e.tensor_tensor(out=tb1[:], in0=brs, in1=wib, op=ALU.mult), r=[R + "bre", R + "wiA"], w=[R + "tb1"])
            dve(lambda e: e.tensor_tensor(out=bbi[:], in0=bbi[:], in1=tb1[:], op=ALU.add), r=[R + "bbi", R + "tb1"], w=[R + "bbi"])
            dve(lambda e, gs=gs: e.tensor_copy(out=a16[:, 0, gs], in_=Pr[:, :, 39]), r=[R + "Pr", R + "a16"], w=[R + "a16"])
            dve(lambda e, gs=gs: e.tensor_copy(out=a16[:, 1, gs], in_=Pi[:, :, 39]), r=[R + "Pi", R + "a16"], w=[R + "a16"])
            PrF = Pr[:, :, 0:23].unsqueeze(3).broadcast_to([64, GB, 23, 16])
            PiF = Pi[:, :, 0:23].unsqueeze(3).broadcast_to([64, GB, 23, 16])
            bbrB = bbr[:, :, :].unsqueeze(2).broadcast_to([64, GB, 23, 16])
            bbiB = bbi[:, :, :].unsqueeze(2).broadcast_to([64, GB, 23, 16])
            dve(lambda e: e.tensor_tensor(out=Fr[:], in0=PrF, in1=bbrB, op=ALU.mult), r=[R + "Pr", R + "bbr"], w=[R + "Fr"])
            dve(lambda e: e.tensor_tensor(out=Ft[:], in0=PiF, in1=bbiB, op=ALU.mult), r=[R + "Pi", R + "bbi"], w=[R + "Ft"])
            dve(lambda e: e.tensor_tensor(out=Fr[:], in0=Fr[:], in1=Ft[:], op=ALU.subtract), r=[R + "Fr", R + "Ft"], w=[R + "Fr"])
            dve(lambda e: e.tensor_tensor(out=Fi[:], in0=PrF, in1=bbiB, op=ALU.mult), r=[R + "Pr", R + "bbi"], w=[R + "Fi"])
            dve(lambda e: e.tensor_tensor(out=Ft[:], in0=PiF, in1=bbrB, op=ALU.mult), r=[R + "Pi", R + "bbr"], w=[R + "Ft"])
            dve(lambda e: e.tensor_tensor(out=Fi[:], in0=Fi[:], in1=Ft[:], op=ALU.add), r=[R + "Fi", R + "Ft"], w=[R + "Fi"])
            PrG = Pr[:, :, 23:40].unsqueeze(3).broadcast_to([64, GB, 17, 16])
            PiG = Pi[:, :, 23:40].unsqueeze(3).broadcast_to([64, GB, 17, 16])
            crB = cre_t[:, gs, :].unsqueeze(2).broadcast_to([64, GB, 17, 16])
            ciB = cim_t[:, gs, :].unsqueeze(2).broadcast_to([64, GB, 17, 16])
            dve(lambda e, b=crB: e.tensor_tensor(out=Gr[:], in0=PrG, in1=b, op=ALU.mult), r=[R + "Pr", R + "c"], w=[R + "Gr"])
            dve(lambda e, b=ciB: e.tensor_tensor(out=Gt[:], in0=PiG, in1=b, op=ALU.mult), r=[R + "Pi", R + "c"], w=[R + "Gt"])
            dve(lambda e: e.tensor_tensor(out=Gr[:], in0=Gr[:], in1=Gt[:], op=ALU.subtract), r=[R + "Gr", R + "Gt"], w=[R + "Gr"])
            dve(lambda e, b=crB: e.tensor_tensor(out=Gi[:], in0=PiG, in1=b, op=ALU.mult), r=[R + "Pi", R + "c"], w=[R + "Gi"])
            dve(lambda e, b=ciB: e.tensor_tensor(out=Gt[:], in0=PrG, in1=b, op=ALU.mult), r=[R + "Pr", R + "c"], w=[R + "Gt"])
            dve(lambda e: e.tensor_tensor(out=Gi[:], in0=Gi[:], in1=Gt[:], op=ALU.add), r=[R + "Gi", R + "Gt"], w=[R + "Gi"])
            dve(lambda e: e.tensor_scalar(out=Gi[:], in0=Gi[:], scalar1=-1.0, scalar2=None, op0=ALU.mult), r=[R + "Gi"], w=[R + "Gi"])
            for g8 in range(GB):
                g = g0 + g8
                sl = g % 2
                fl = lambda ap: ap.rearrange("p a b -> p (a b)")
                pe(lambda e, g8=g8: e.matmul(psB[:, 0:256], lhsT=fl(Fr[:, g8, 15:23, :]), rhs=fl(Gr[:, g8, 0:16, :]), start=True, stop=False),
                   r=[R + "Fr", R + "Gr"], w=["psB"], mark=False)
                pe(lambda e, g8=g8: e.matmul(psB[:, 0:256], lhsT=fl(Fi[:, g8, 15:23, :]), rhs=fl(Gi[:, g8, 0:16, :]), start=False, stop=True),
                   r=[R + "Fi", R + "Gi"], w=["psB"])
                dve(lambda e: e.tensor_tensor(out=tbf[:], in0=psB[:, 0:256], in1=tmask[:, :, :].rearrange("p a b -> p (a b)"), op=ALU.mult),
                    r=["psB", "tmask"], w=[R + "tbf"])
                dve(lambda e, g=g, sl=sl: e.scalar_tensor_tensor(out=txs[sl][:, 0:256], in0=dsel[:], scalar=dcol[:, g:g + 1], in1=tbf[:],
                                                                 op0=ALU.mult, op1=ALU.add),
                    r=["dsel", R + "dcol", R + "tbf"], w=[R + "txs%d" % sl])
                for h in range(2):
                    for c, Fc in ((0, Fr), (1, Fi)):
                        pe(lambda e, g8=g8, h=h, c=c, Fc=Fc: e.transpose(out=psA[:, (h * 2 + c) * 64:(h * 2 + c) * 64 + 64],
                                                                          in_=fl(Fc[:, g8, 8 * h:8 * h + 8, :]), identity=ident[0:64, 0:64]),
                           r=[R + "Fr", R + "Fi", "ident"], w=["psA_lo"], mark=(h == 1 and c == 1))
                act(lambda e, sl=sl: e.copy(out=txs[sl][:, 256:512], in_=psA[:, 0:256]), r=["psA_lo", R + "txs%d" % sl], w=[R + "txs%d" % sl])
                dma(TXd[g], txs[sl][:], r=[R + "txs%d" % sl], w=["TXd%d" % g])
                pool(lambda e, g8=g8, sl=sl: e.tensor_copy(out=gsb[sl][:, 0, :], in_=fl(Gr[:, g8, 1:17, :])), r=[R + "Gr"], w=[R + "gsb%d" % sl])
                pool(lambda e, g8=g8, sl=sl: e.tensor_copy(out=gsb[sl][:, 1, :], in_=fl(Gi[:, g8, 1:17, :])), r=[R + "Gi", R + "gsb%d" % sl], w=[R + "gsb%d" % sl])
                dma(Gd[g], gsb[sl][:, :, :], r=[R + "gsb%d" % sl], w=["Gd%d" % g])
            yield
        dma(a16d.rearrange("c p g -> p c g"), a16[:], r=[R + "a16"], w=["a16d"])
        for par in range(2):
            for c in range(2):
                dma(a16p[64 * par:64 * par + 64, c, :], a16d.rearrange("c p (gh two) -> two c p gh", two=2)[par, c],
                    r=["a16d"], w=[R + "a16p"], slow=True)
        for lev in range(3):
            A1 = A1s[la][lev]; A2 = A2s[la][lev]
            dve(lambda e, A1=A1: e.tensor_copy(out=A1[:, :, :], in_=a16p[:, 0, :].unsqueeze(2).broadcast_to([128, 32, 2])), r=[R + "a16p"], w=["A1"])
            dve(lambda e, A2=A2: e.tensor_copy(out=A2[:, :, 1], in_=a16p[:, 1, :]), r=[R + "a16p"], w=["A2"])
            dve(lambda e, A2=A2: e.tensor_scalar(out=A2[:, :, 0], in0=a16p[:, 1, :], scalar1=-1.0, scalar2=None, op0=ALU.mult), r=[R + "a16p", "A2"], w=["A2"])
            if lev < 2:
                dve(lambda e: e.tensor_tensor(out=apw[:, :, :], in0=a16p[:, :, :], in1=a16p[:, :, :], op=ALU.mult), r=[R + "a16p"], w=["apw"])
                dve(lambda e: e.tensor_tensor(out=apt[:, 1, :], in0=a16p[:, 0, :], in1=a16p[:, 1, :], op=ALU.mult), r=[R + "a16p"], w=["apt"])
                dve(lambda e: e.tensor_tensor(out=a16p[:, 0, :], in0=apw[:, 0, :], in1=apw[:, 1, :], op=ALU.subtract), r=["apw", R + "a16p"], w=[R + "a16p"])
                dve(lambda e: e.tensor_scalar(out=a16p[:, 1, :], in0=apt[:, 1, :], scalar1=2.0, scalar2=None, op0=ALU.mult), r=["apt", R + "a16p"], w=[R + "a16p"])
        yield

    def s5_sequence(la, s, nk, src_tile, dst_tile, h0_fn, hout_fn, x_src_s=None, x_dst_s=None):
        stop_if = DBG["stop_if"]
        nk0 = nk
        x_s = x_src_s
        Uc = S5["Uc"]; SH = S5["SH"]
        TXd = TXd2[la]; Gd = Gd2[la]
        SHf = SH[:, :, :, :].rearrange("p a b c -> p (a b c)")
        SHb = SHf[:, 4096:8192].bitcast(BF16)
        psBb = psB.bitcast(BF16)

        def hb(j):
            return SHb[:, 1024 * j:1024 * (j + 1)]
        SB = [dict(F1=xt1, F2=t1, H1=xbt, H2=hmT, H3=szb, H4=ybt, psT=psT, psTn="psT"),
              dict(F1=t2, F2=szt, H1=S5["gt"], H2=yT, H3=S5["h3b"], H4=S5["h4b"], psT=psQ, psTn="psQ"),
              dict(F1=SHf[:, 0:1024], F2=SHf[:, 1024:2048], H1=hb(0), H2=hb(1).rearrange("p (c t) -> p c t", t=128), H3=hb(2), H4=hb(3),
                   psT=psBb[:, 0:1024], psTn="psB0"),
              dict(F1=SHf[:, 2048:3072], F2=SHf[:, 3072:4096], H1=hb(4), H2=hb(5).rearrange("p (c t) -> p c t", t=128), H3=hb(6), H4=hb(7),
                   psT=psBb[:, 1024:2048], psTn="psB1")]
        for kk in range(4):
            SB[kk]["psX"] = psA[:, 512 * kk:512 * kk + 512]; SB[kk]["psXn"] = "psA_q%d" % kk; SB[kk]["s1"] = S5["s1"][kk]

        def proj_h(B, K, W, wname, n, hf, bias_row=None):
            if bias_row is not None:
                pe(lambda e: e.matmul(B["psX"][:n, 0:512], lhsT=onesb[0:1, :n], rhs=bias_row[0:1, hf * 512:(hf + 1) * 512], start=True, stop=False),
                   r=["onesb", "bglur"], w=[B["psXn"]], mark=False)
            for c in range(8):
                pe(lambda e, c=c: e.matmul(B["psX"][:n, 0:512], lhsT=B["H2"][:, c, :n], rhs=W[:, c, hf * 512:(hf + 1) * 512],
                                           start=(c == 0 and bias_row is None), stop=(c == 7)),
                   r=["H2" + K, "H2" + K + "e", "H2" + K + "o", wname], w=[B["psXn"]], mark=(c == 7))

        allU = ["Uc%d" % g for g in range(64)]

        def tr8(B, k, src, srcn, dst, dstn, modulate, n=None):
            n = nk if n is None else n
            for c in range(8):
                pe(lambda e, c=c: e.transpose(out=B["psT"][:, c * 128:c * 128 + n], in_=src[:n, c * 128:(c + 1) * 128], identity=identb[:n, :n]),
                   r=[srcn, "identb"], w=[B["psTn"]], mark=(c == 7))
            yield
            if modulate:
                for c in range(8):
                    if c < 4:
                        act(lambda e, c=c, cp=CUR["condP"]: e.activation(out=dst[:, c, :n], in_=B["psT"][:, c * 128:c * 128 + n], func=AF.Identity,
                                                                         bias=cp[:, c, s:s + 1], scale=cp[:, 8 + c, s:s + 1]),
                            r=[B["psTn"]], w=[dstn + "e"], mark=(c == 3))
                    else:
                        dve(lambda e, c=c, cp=CUR["condP"]: e.tensor_scalar(out=dst[:, c, :n], in0=B["psT"][:, c * 128:c * 128 + n],
                                                                            scalar1=cp[:, 8 + c, s:s + 1], scalar2=cp[:, c, s:s + 1],
                                                                            op0=ALU.mult, op1=ALU.add),
                            r=[B["psTn"]], w=[dstn + "o"], mark=(c == 7))
            else:
                act(lambda e: e.copy(out=dst[:, :, :n], in_=B["psT"][:, :].rearrange("p (c t) -> p c t", t=128)[:, :, :n]),
                    r=[B["psTn"]], w=[dstn])
            yield

        def p1_task(i, sample=False):
            def gen(k):
                B = SB[k]; K = "_%d" % k
                n = DEC if sample else nk
                dma(B["H1"][:n, :], x_s if sample else src_tile(i), w=["H1" + K], eng="pool")
                yield
                yield from tr8(B, k, B["H1"], "H1" + K, B["H2"], "H2" + K, True, n)
                for hf in range(2):
                    proj_h(B, K, Win, "Win", n, hf)
                    yield
                    if sample:
                        act(lambda e, hf=hf: e.copy(out=B["H4"][:n, hf * 512:(hf + 1) * 512], in_=B["psX"][:n, 0:512]), r=[B["psXn"]], w=["H4" + K])
                    else:
                        dve(lambda e, hf=hf: e.tensor_copy(out=Uc[:n, 32 * hf:32 * hf + 32, i, :], in_=B["psX"][:n, 0:512].rearrange("p (g c) -> p g c", c=16)),
                            r=[B["psXn"]], w=allU[32 * hf:32 * hf + 32])
                if sample:
                    dma(ucs_d, B["H4"][:n, :], r=["H4" + K], w=["ucs_d"])
                    uv = ucs_d.rearrange("(k i) (g c) -> i k g c", i=16, c=16)
                    for ii in range(16):
                        dma(Uc[:4, :, ii, :], uv[ii], r=["ucs_d"], w=allU, slow=True)
                for hf in range(2):
                    proj_h(B, K, Win[:, :, 1024:2048], "Win", n, hf)
                    yield
                    act(lambda e, hf=hf: e.activation(out=B["H3"][:n, hf * 512:(hf + 1) * 512], in_=B["psX"][:n, 0:512], func=AF.Silu),
                        r=[B["psXn"]], w=["H3" + K])
                dma(zscr[i, :n, :], B["H3"][:n, :], r=["H3" + K], w=["zscr%d" % i])
            return gen

        def pq(k):
            pst = psT if k < 2 else psQ
            return pst, ("psT" if k < 2 else "psQ"), (k % 2) * 512, psA[:, 512 * k:512 * k + 512], "psA_q%d" % k

        def p2_task(g):
            def gen(k):
                par = g % 2; gh = g // 2; sl = k
                pst, pstn, pc0, psx, psxn = pq(k)
                dma(S5["TXs"][sl][:, :], TXd[g], r=["TXd%d" % g], w=["TXs%d" % sl])
                yield
                for h in range(2):
                    pe(lambda e, h=h: e.transpose(out=pst[:, pc0 + h * 128:pc0 + h * 128 + nk],
                                                  in_=Uc[:nk, g, 8 * h:8 * h + 8, :].rearrange("p a b -> p (a b)"), identity=identb[:nk, :nk]),
                       r=["Uc%d" % g, "identb"], w=[pstn], mark=(h == 1))
                yield
                act(lambda e: e.copy(out=S5["UTs"][sl][:, :, :nk], in_=pst[:, pc0:pc0 + 256].rearrange("p (h k) -> p h k", k=128)[:, :, :nk]),
                    r=[pstn], w=["UTs%d" % sl])
                yield
                for c in range(2):
                    for h in range(2):
                        pe(lambda e, c=c, h=h: e.matmul(psx[64 * par:64 * par + 64, c * 128:c * 128 + nk],
                                                        lhsT=S5["TXs"][sl][:, 256 + (h * 2 + c) * 64:256 + (h * 2 + c) * 64 + 64],
                                                        rhs=S5["UTs"][sl][:, h, :nk], start=(h == 0), stop=(h == 1)),
                           r=["TXs%d" % sl, "UTs%d" % sl], w=[psxn], mark=(c == 1 and h == 1))
                yield
                dve(lambda e: e.tensor_copy(out=SH[64 * par:64 * par + 64, gh, :, 1:nk + 1],
                                            in_=psx[64 * par:64 * par + 64, 0:256].rearrange("p (c k) -> p c k", k=128)[:, :, :nk]),
                    r=[psxn], w=["SH%d" % (gh // 16)])
            return gen

        def p4_task(g):
            def gen(k):
                par = g % 2; gh = g // 2; sl = k
                pst, pstn, pc0, psx, psxn = pq(k)
                dma(S5["TXs"][sl][:, 0:256], TXd[g][:, 0:256], r=["TXd%d" % g], w=["TXs%d" % sl])
                dma(S5["Gs"][sl][64 * par:64 * par + 64, :], Gd[g], r=["Gd%d" % g], w=["Gs%d" % sl], eng="act")
                act(lambda e: e.copy(out=S5["Hbs"][k][64 * par:64 * par + 64, :, :nk], in_=SH[64 * par:64 * par + 64, gh, :, 0:nk]),
                    r=["SH%d" % (gh // 16)], w=["Hb%d" % k])
                yield
                for h in range(2):
                    pe(lambda e, h=h: e.transpose(out=pst[:, pc0 + h * 128:pc0 + h * 128 + nk],
                                                  in_=Uc[:nk, g, 8 * h:8 * h + 8, :].rearrange("p a b -> p (a b)"), identity=identb[:nk, :nk]),
                       r=["Uc%d" % g, "identb"], w=[pstn], mark=(h == 1))
                yield
                dve(lambda e: e.tensor_copy(out=S5["UTs"][sl][:, :, :nk], in_=pst[:, pc0:pc0 + 256].rearrange("p (h k) -> p h k", k=128)[:, :, :nk]),
                    r=[pstn], w=["UTs%d" % sl])
                yield
                o = psx[:nk, 0:256]
                pe(lambda e: e.matmul(o, lhsT=S5["UTs"][sl][:, 0, :nk], rhs=S5["TXs"][sl][:, 0:256], start=True, stop=False,
                                      skip_group_check=True), r=["UTs%d" % sl, "TXs%d" % sl], w=[psxn], mark=False)
                pe(lambda e: e.matmul(o[:, 128:256], lhsT=S5["UTs"][sl][:, 1, :nk], rhs=S5["TXs"][sl][:, 0:128], start=False, stop=False,
                                      skip_group_check=True), r=["UTs%d" % sl, "TXs%d" % sl], w=[psxn], mark=False)
                for c in range(2):
                    pe(lambda e, c=c: e.matmul(o, lhsT=S5["Hbs"][k][64 * par:64 * par + 64, c, :nk],
                                               rhs=S5["Gs"][sl][64 * par:64 * par + 64, c * 256:(c + 1) * 256],
                                               start=False, stop=(c == 1), skip_group_check=True),
                       r=["Hb%d" % k, "Gs%d" % sl], w=[psxn], mark=(c == 1))
                yield
                act(lambda e: e.activation(out=Uc[:nk, g, :, :], in_=psx[:nk, 0:256].rearrange("p (i c) -> p i c", c=16), func=AF.Gelu),
                    r=[psxn], w=["Uc%d" % g])
            return gen

        def p5_task(i, sample=False):
            def gen(k):
                B = SB[k]; K = "_%d" % k
                nk = DEC if sample else nk0
                F1 = B["F1"]; F2 = B["F2"]; s1k = B["s1"]
                dma(F1[:nk, :], x_src_s if sample else src_tile(i), w=["F1" + K])
                dma(B["H3"][:nk, :], zscr[i, :nk, :], r=["zscr%d" % i], w=["H3" + K], eng="pool")
                if sample:
                    gv = gcs_d.rearrange("(k i) (g c) -> i k g c", i=16, c=16)
                    for ii in range(16):
                        dma(gv[ii], Uc[:4, :, ii, :], r=allU, w=["gcs_d"], slow=True)
                    dma(B["H1"][:nk, :], gcs_d, r=["gcs_d"], w=["H1" + K])
                else:
                    act(lambda e: e.copy(out=B["H1"][:nk, :].rearrange("p (g c) -> p g c", c=16), in_=Uc[:nk, :, i, :]), r=allU, w=["H1" + K])
                yield
                yield from tr8(B, k, B["H1"], "H1" + K, B["H2"], "H2" + K, False, nk)
                for hf in range(2):
                    proj_h(B, K, Wg, "Wg", nk, hf, bias_row=bglur)
                    yield
                    act(lambda e, hf=hf: e.activation(out=F2[:nk, hf * 512:(hf + 1) * 512], in_=B["psX"][:nk, 0:512], func=AF.Sigmoid),
                        r=[B["psXn"]], w=["F2" + K])
                dve(lambda e: e.tensor_tensor(out=F2[:nk, :], in0=F2[:nk, :], in1=B["H1"][:nk, :], op=ALU.mult), r=["F2" + K, "H1" + K], w=["F2" + K])
                dve(lambda e: e.tensor_tensor(out=B["H4"][:nk, :], in0=F2[:nk, :], in1=B["H3"][:nk, :], op=ALU.mult), r=["F2" + K, "H3" + K], w=["H4" + K])
                yield
                yield from tr8(B, k, B["H4"], "H4" + K, B["H2"], "H2" + K, False, nk)
                nt = nk
                for hf in range(2):
                    proj_h(B, K, Wo, "Wo", nk, hf)
                    yield
                    dve(lambda e, hf=hf: e.tensor_tensor(out=F2[:nt, hf * 512:(hf + 1) * 512], in0=B["psX"][:nt, 0:512],
                                                         in1=gateb[:nt, hf * 512:(hf + 1) * 512], op=ALU.mult),
                        r=[B["psXn"], "gateb"], w=["F2" + K])
                dve(lambda e: e.scalar_tensor_tensor(out=F1[:nt, :], in0=F1[:nt, :], scalar=ALPHA, in1=F2[:nt, :], op0=ALU.mult, op1=ALU.add),
                    r=["F1" + K, "F2" + K], w=["F1" + K])
                dve(lambda e: e.bn_stats(out=s1k[:nt, 0:6], in_=F1[:nt, 0:512]), r=["F1" + K], w=["s1" + K])
                dve(lambda e: e.bn_stats(out=s1k[:nt, 6:12], in_=F1[:nt, 512:1024]), r=["F1" + K, "s1" + K], w=["s1" + K])
                dve(lambda e: e.bn_aggr(out=s1k[:nt, 12:14], in_=s1k[:nt, 0:12]), r=["s1" + K], w=["s1b" + K])
                yield
                act(lambda e: e.activation(out=s1k[:nt, 14:15], in_=s1k[:nt, 13:14], func=AF.Sqrt, bias=epsb[:nt, 0:1]), r=["s1b" + K, "epsb"], w=["s1c" + K])
                dve(lambda e: e.reciprocal(out=s1k[:nt, 15:16], in_=s1k[:nt, 14:15]), r=["s1c" + K], w=["s1d" + K])
                dve(lambda e: e.scalar_tensor_tensor(out=F2[:nt, :], in0=F1[:nt, :], scalar=s1k[:nt, 12:13], in1=lngb[:nt, :], op0=ALU.subtract, op1=ALU.mult),
                    r=["F1" + K, "s1b" + K, "lngb"], w=["F2" + K])
                dve(lambda e: e.scalar_tensor_tensor(out=F2[:nt, :], in0=F2[:nt, :], scalar=s1k[:nt, 15:16], in1=lnbb[:nt, :], op0=ALU.mult, op1=ALU.add),
                    r=["F2" + K, "s1d" + K, "lnbb"], w=["F2" + K])
                dma(x_dst_s if sample else dst_tile(i), F2[:nt, :], r=["F2" + K], w=["dst"], eng="pool")
            return gen

        if s == 2:
            run_streams([p1_task(0, True)], 1)
        else:
            run_streams([p1_task(i) for i in range(16)], 4)
        if s == 2:
            load_w(Win, CUR["next_win"], "Win", 2048)
        sc1 = S5["sc1"]; sc2 = S5["sc2"]

        def scan_gen(hf):
            gsl = slice(16 * hf, 16 * hf + 16)
            R = "SH%d" % hf; Z = "_%d" % hf
            TA = [xt1, t2][hf]; TB = [t1, szt][hf]
            TAn = ["F1_0", "F1_1"][hf]; TBn = ["F2_0", "F2_1"][hf]

            def cmac(dk, sk, lev, cnt):
                step = 2 << lev
                A1 = A1s[la][lev]; A2 = A2s[la][lev]
                j0 = 0
                while j0 < cnt:
                    m = min(32, cnt - j0)
                    d = SH[:, gsl, :, dk + j0 * step:dk + (j0 + m - 1) * step + 1:step]
                    sr = SH[:, gsl, :, sk + j0 * step:sk + (j0 + m - 1) * step + 1:step]
                    ta = TA[:, 0:16 * 2 * m].rearrange("p (g c m) -> p g c m", g=16, c=2)
                    tb = TB[:, 0:16 * 2 * m].rearrange("p (g c m) -> p g c m", g=16, c=2)
                    a1b = A1[:, gsl, :].unsqueeze(3).broadcast_to([128, 16, 2, m])
                    dve(lambda e, ta=ta, sr=sr, a1b=a1b: e.tensor_tensor(out=ta, in0=sr, in1=a1b, op=ALU.mult), r=[R, "A1"], w=[TAn])
                    pool(lambda e, tb=tb, sr=sr, A2=A2, m=m: e.tensor_tensor(out=tb[:, :, 0, :], in0=sr[:, :, 1, :],
                                                                             in1=A2[:, gsl, 0:1].broadcast_to([128, 16, m]), op=ALU.mult),
                         r=[R, "A2"], w=[TBn + "a"])
                    pool(lambda e, tb=tb, sr=sr, A2=A2, m=m: e.tensor_tensor(out=tb[:, :, 1, :], in0=sr[:, :, 0, :],
                                                                             in1=A2[:, gsl, 1:2].broadcast_to([128, 16, m]), op=ALU.mult),
                         r=[R, "A2"], w=[TBn + "b"])
                    dve(lambda e, ta=ta, tb=tb: e.tensor_tensor(out=ta, in0=ta, in1=tb, op=ALU.add), r=[TAn, TBn + "a", TBn + "b"], w=[TAn])
                    dve(lambda e, ta=ta, d=d: e.tensor_tensor(out=d, in0=d, in1=ta, op=ALU.add), r=[TAn, R], w=[R])
                    j0 += m
                    yield

            yield from cmac(2, 1, 0, nk // 2)
            yield from cmac(4, 2, 1, nk // 4)
            A1 = A1s[la][2]; A2 = A2s[la][2]
            for k in range(0, nk, 4):
                v0 = SH[:, gsl, :, k]; v1 = SH[:, gsl, :, k + 4]
                dve(lambda e, v0=v0: e.tensor_tensor(out=sc1[:, gsl, :], in0=A1[:, gsl, :], in1=v0, op=ALU.mult), r=[R, "A1"], w=["sc1" + Z])
                pool(lambda e, k=k: e.tensor_tensor(out=sc2[:, gsl, 0], in0=A2[:, gsl, 0], in1=SH[:, gsl, 1, k], op=ALU.mult), r=[R, "A2"], w=["sc2a" + Z])
                pool(lambda e, k=k: e.tensor_tensor(out=sc2[:, gsl, 1], in0=A2[:, gsl, 1], in1=SH[:, gsl, 0, k], op=ALU.mult), r=[R, "A2"], w=["sc2b" + Z])
                dve(lambda e: e.tensor_tensor(out=sc1[:, gsl, :], in0=sc1[:, gsl, :], in1=sc2[:, gsl, :], op=ALU.add),
                    r=["sc1" + Z, "sc2a" + Z, "sc2b" + Z], w=["sc1" + Z])
                dve(lambda e, v1=v1: e.tensor_tensor(out=v1, in0=v1, in1=sc1[:, gsl, :], op=ALU.add), r=["sc1" + Z, R], w=[R])
                if (k // 4) % 2 == 1:
                    yield
            yield from cmac(2, 0, 1, nk // 4)
            yield from cmac(1, 0, 0, nk // 2)

        P.barrier()
        h0_fn()
        run_streams([p2_task(g) for g in range(32)], 4)
        run_streams([p2_task(g) for g in range(32, 64)], 4, extra=[scan_gen(0)])
        run_streams([p4_task(g) for g in range(32)], 4, extra=[scan_gen(1)])
        hout_fn()
        run_streams([p4_task(g) for g in range(32, 64)], 4)
        P.barrier()
        if s == 2:
            run_streams([p5_task(0, True)], 1)
        else:
            run_streams([p5_task(i) for i in range(16)], 4)

    def s5_layer(la, l, src, dst):
        stop_if = DBG["stop_if"]
        layer_ln(l)
        if la > 0:
            load_w(Wg, w_glu[la], "Wg", 1024)
            load_w(Wo, w_out_a[la], "Wo", 1024)
            dma(bglur[0:1, :], b_glu[la:la + 1, :], w=["bglur"], eng="pool")
        CUR["next_win"] = w_in_a[1] if la == 0 else w_in_b[0]
        push_scope()
        alloc_s5_seq()
        for s in range(3):
            load_gate(s)
            if s < 2:
                nk = 128
                sv = src[0][s].rearrange("(k i) d -> i k d", i=16)
                dv = dst[0][s].rearrange("(k i) d -> i k d", i=16)
                def h0_fn():
                    pool(lambda e: e.memset(S5["SH"][:, :, :, 0], 0.0), w=["SH0", "SH1"])
                def hout_fn(s=s):
                    for par in range(2):
                        for c in range(2):
                            dma(ssm_p[la, s].rearrange("(gh two) p c -> two c p gh", two=2)[par, c], S5["SH"][64 * par:64 * par + 64, :, c, 128],
                                r=["SH0", "SH1"], w=["ssm_out"], slow=True)
            else:
                nk = 4
                sv = src[1].rearrange("(k i) d -> i k d", i=16)
                dv = dst[1].rearrange("(k i) d -> i k d", i=16)
                def h0_fn():
                    for par in range(2):
                        for c in range(2):
                            dma(S5["SH"][64 * par:64 * par + 64, :, c, 0], st_in[la].rearrange("(gh two) p c -> two c p gh", two=2)[par, c],
                                w=["SH0", "SH1"], slow=True)
                def hout_fn():
                    for par in range(2):
                        for c in range(2):
                            dma(ssm_s[la].rearrange("(gh two) p c -> two c p gh", two=2)[par, c], S5["SH"][64 * par:64 * par + 64, :, c, 4],
                                r=["SH0", "SH1"], w=["ssm_out"], slow=True)
            s5_sequence(la, s, nk, lambda i, sv=sv: sv[i], lambda i, dv=dv: dv[i], h0_fn, hout_fn, src[1], dst[1])
        pop_scope()

    AT = {}

    def alloc_attn():
        AT["kf"] = T([128, 256]); AT["vf"] = T([128, 256]); AT["kb"] = T([128, 256], BF16)
        AT["kTs"] = [T([64, 4, 128], BF16) for _ in range(2)]
        AT["vbs"] = [T([128, 256], BF16) for _ in range(2)]
        AT["qb"] = T([128, D], BF16); AT["qT"] = T([64, 16, 128], BF16)
        AT["r1"] = T([128, 16, 32]); AT["r2"] = T([128, 16, 32])
        AT["sm"] = [T([128, 4, 256]) for _ in range(2)]; AT["eb"] = [T([128, 4, 256], BF16) for _ in range(2)]
        AT["eT"] = [T([128, 8, 128], BF16) for _ in range(2)]
        AT["st"] = [T([128, 20]) for _ in range(2)]; AT["rinv"] = T([128, 16])
        AT["xt"] = [T([128, D]) for _ in range(2)]; AT["sinkb"] = T([128, 16]); AT["nsinkb"] = T([128, 16])
        AT["ogb"] = T([128, D], BF16)
        AT["ckc"] = T([128, 256]); AT["ckb"] = T([128, 256], BF16)
        AT["xTp"] = T([128, 8, 128], BF16); AT["Wkv"] = T([128, 8, 512], BF16)

    def rope(src_ps, psname, nh, nt, j, out_ap, wname):
        sv = src_ps.rearrange("p (h d) -> p h d", d=64)
        ov = out_ap.rearrange("p (h d) -> p h d", d=64)
        cb = cosT[:nt, j:j + 1, :].broadcast_to([nt, nh, 32]); sb = sinT[:nt, j:j + 1, :].broadcast_to([nt, nh, 32])
        dve(lambda e: e.tensor_tensor(out=AT["r1"][:nt, :nh, :], in0=sv[:, :, 0:32], in1=cb, op=ALU.mult), r=[psname, "cosT"], w=["r1"])
        dve(lambda e: e.tensor_tensor(out=AT["r2"][:nt, :nh, :], in0=sv[:, :, 32:64], in1=sb, op=ALU.mult), r=[psname, "sinT"], w=["r2"])
        dve(lambda e: e.tensor_tensor(out=ov[:, :, 0:32], in0=AT["r1"][:nt, :nh, :], in1=AT["r2"][:nt, :nh, :], op=ALU.subtract), r=["r1", "r2"], w=[wname])
        dve(lambda e: e.tensor_tensor(out=AT["r1"][:nt, :nh, :], in0=sv[:, :, 32:64], in1=cb, op=ALU.mult), r=[psname, "cosT"], w=["r1"])
        dve(lambda e: e.tensor_tensor(out=AT["r2"][:nt, :nh, :], in0=sv[:, :, 0:32], in1=sb, op=ALU.mult), r=[psname, "sinT"], w=["r2"])
        dve(lambda e: e.tensor_tensor(out=ov[:, :, 32:64], in0=AT["r1"][:nt, :nh, :], in1=AT["r2"][:nt, :nh, :], op=ALU.add), r=["r1", "r2", wname], w=[wname])

    def kT_from(kb_t, rname, nt, slot):
        for h in range(4):
            pe(lambda e, h=h: e.transpose(out=psT[0:64, h * 128:h * 128 + nt], in_=kb_t[:nt, h * 64:(h + 1) * 64], identity=identb[:nt, :nt]),
               r=[rname, "identb"], w=["psT"], mark=(h == 3))
        dve(lambda e: e.tensor_copy(out=AT["kTs"][slot][:, :, :nt], in_=psT[0:64, 0:512].rearrange("p (h t) -> p h t", t=128)[:, :, :nt]),
            r=["psT"], w=["kT%d" % slot])

    def attn_pro(first_attn, s, nt, X, xn, cur, rope_j, t0, kv_out):
        act(lambda e: e.copy(out=xbt[:nt, :], in_=X[:nt, :]), r=[xn], w=["xbt"])
        yield
        for c in range(8):
            pe(lambda e, c=c: e.transpose(out=psT[:, c * 128:c * 128 + nt], in_=xbt[:nt, c * 128:(c + 1) * 128],
                                          identity=identb[:nt, :nt]), r=["xbt", "identb"], w=["psT"], mark=(c == 7))
        yield
        for c in range(8):
            act(lambda e, c=c, cp=CUR["condP"]: e.activation(out=hmT[:, c, :nt], in_=psT[:, c * 128:c * 128 + nt], func=AF.Identity,
                                                             bias=cp[:, c, s:s + 1], scale=cp[:, 8 + c, s:s + 1]),
                r=["psT"], w=["hmT"], mark=(c == 7))
        if first_attn:
            dve(lambda e: e.tensor_copy(out=AT["xTp"][:, :, :nt], in_=psT[:, :].rearrange("p (c t) -> p c t", t=128)[:, :, :nt]),
                r=["psT"], w=["xTp"])
        yield
        if first_attn:
            proj(psA[:, 1024:2048], AT["xTp"], "xTp", AT["Wkv"], "Wkv", nt, 512, ["psA_hi"])
            yield
            rope(psA[:nt, 1024:1280], "psA_hi", 4, nt, rope_j, AT["kf"][:nt, :], "kf")
            act(lambda e: e.copy(out=AT["vf"][:nt, :], in_=psA[:nt, 1280:1536]), r=["psA_hi"], w=["vf"])
            act(lambda e: e.copy(out=AT["kb"][:nt, :], in_=AT["kf"][:nt, :]), r=["kf"], w=["kb"])
            dve(lambda e: e.tensor_copy(out=AT["vbs"][cur][:nt, :], in_=AT["vf"][:nt, :]), r=["vf"], w=["vb%d" % cur])
            yield
            kT_from(AT["kb"], "kb", nt, cur)
            dma(ktscr[s, :, :, t0:t0 + nt], AT["kTs"][cur][:, :, :nt], r=["kT%d" % cur], w=["ktscr"], eng="pool")
            dma(vscr[s, t0:t0 + nt, :], AT["vbs"][cur][:nt, :], r=["vb%d" % cur], w=["vscr"], eng="pool")
            if kv_out is not None:
                dma(kv_out[0], AT["kf"][:nt, :], r=["kf"], w=["kvout"], eng="pool")
                dma(kv_out[1], AT["vf"][:nt, :], r=["vf"], w=["kvout"], eng="pool")
            yield
        else:
            dma(AT["kTs"][cur][:, :, :nt], ktscr[s, :, :, t0:t0 + nt], w=["kT%d" % cur])
            dma(AT["vbs"][cur][:nt, :], vscr[s, t0:t0 + nt, :], w=["vb%d" % cur])
        proj(psA, hmT, "hmT", AT["Win"], AT["Winn"], nt, 1024, ["psA_lo"])
        yield
        proj(psA[:, 1024:2048], hmT, "hmT", AT["Win"][:, :, 1024:2048], AT["Winn"], nt, 1024, ["psA_hi"])
        yield
        rope(psA[:nt, 0:1024], "psA_lo", 16, nt, rope_j, AT["qb"][:nt, :], "qb")
        act(lambda e: e.activation(out=szt[:nt, :], in_=psA[:nt, 1024:2048], func=AF.Silu), r=["psA_hi"], w=["szt"])
        yield
        for half, (pst, pname) in enumerate(((psT, "psT"), (psQ, "psQ"))):
            for hh in range(8):
                h = half * 8 + hh
                pe(lambda e, h=h, hh=hh, pst=pst: e.transpose(out=pst[0:64, hh * 128:hh * 128 + nt], in_=AT["qb"][:nt, h * 64:(h + 1) * 64],
                                                              identity=identb[:nt, :nt]),
                   r=["qb", "identb"], w=[pname], mark=(hh == 7))
            yield
            dve(lambda e, half=half, pst=pst: e.tensor_copy(out=AT["qT"][:, half * 8:half * 8 + 8, :nt],
                                                            in_=pst[0:64, :].rearrange("p (h t) -> p h t", t=128)[:, :, :nt]),
                r=[pname], w=["qT"])
        yield

    def attn_epi(s, nt, X, xn, dst_ap):
        dve(lambda e: e.tensor_tensor(out=t1[:nt, :].rearrange("p (h d) -> p h d", d=64), in0=psA[:nt, 0:1024].rearrange("p (h d) -> p h d", d=64),
                                      in1=AT["rinv"][:nt, :].unsqueeze(2).broadcast_to([nt, 16, 64]), op=ALU.mult), r=["psA_lo", "rinv"], w=["t1"])
        dve(lambda e: e.tensor_tensor(out=AT["ogb"][:nt, :], in0=t1[:nt, :], in1=szt[:nt, :], op=ALU.mult), r=["t1", "szt"], w=["ogb"])
        yield
        for c in range(8):
            pe(lambda e, c=c: e.transpose(out=psQ[:, c * 128:c * 128 + nt], in_=AT["ogb"][:nt, c * 128:(c + 1) * 128],
                                          identity=identb[:nt, :nt]), r=["ogb", "identb"], w=["psQ"], mark=(c == 7))
        yield
        dve(lambda e: e.tensor_copy(out=yT[:, :, :nt], in_=psQ[:, :].rearrange("p (c t) -> p c t", t=128)[:, :, :nt]), r=["psQ"], w=["yT"])
        yield
        proj(psB, yT, "yT", AT["Wo"], AT["Won"], nt, 1024, ["psB"])
        yield
        resid_ln(psB[:nt, 0:1024], "psB", 0, s, nt, dst_ap, "dst", X=X, xn=xn)
        yield

    def attn_heads(nt, prev, cur, mask, mname):
        nkeys = 128 + nt

        def hg_task(hg):
            def gen(sl):
                kvh = hg
                ps_s = psB if sl == 0 else psA[:, 1024:2048]
                psn = "psB" if sl == 0 else "psA_hi"
                ps_e = psT if sl == 0 else psQ
                pen = "psT" if sl == 0 else "psQ"
                sm = AT["sm"][sl]; eb = AT["eb"][sl]; eT = AT["eT"][sl]; st = AT["st"][sl]
                S = "_%d" % sl
                has_mask = mask is not None
                if has_mask:
                    pairs = ((mLa, mRa), (mLb, mRb)) if mname == "maskG" else ((mLa, mRa), (mL1, mR0))
                    for half in range(2):
                        for pi, (ml, mr) in enumerate(pairs):
                            pe(lambda e, half=half, ml=ml, mr=mr, pi=pi: e.matmul(ps_s[:nt, half * 512:(half + 1) * 512], lhsT=ml[0:1, :nt], rhs=mr[0:1, :],
                                                                                   start=(pi == 0), stop=False, skip_group_check=True),
                               r=["mk"], w=[psn], mark=False)
                for hh in range(4):
                    h = 4 * hg + hh
                    pe(lambda e, h=h, hh=hh: e.matmul(ps_s[:nt, hh * 256:hh * 256 + 128], lhsT=AT["qT"][:, h, :nt], rhs=AT["kTs"][prev][:, kvh, :],
                                                      start=(not has_mask), stop=(not has_mask), skip_group_check=True), r=["qT", "kT%d" % prev], w=[psn], mark=False)
                    pe(lambda e, h=h, hh=hh: e.matmul(ps_s[:nt, hh * 256 + 128:hh * 256 + 128 + nt], lhsT=AT["qT"][:, h, :nt],
                                                      rhs=AT["kTs"][cur][:, kvh, :nt], start=(not has_mask), stop=True, skip_group_check=True),
                       r=["qT", "kT%d" % cur], w=[psn], mark=(hh == 3))
                yield
                psv = ps_s[:nt, 0:1024].rearrange("p (h k) -> p h k", k=256)[:, :, :nkeys]
                dve(lambda e: e.reduce_max(out=st[:nt, 0:4], in_=psv, axis=AX.X), r=[psn], w=["st" + S])
                dve(lambda e: e.scalar_tensor_tensor(out=st[:nt, 4:8], in0=st[:nt, 0:4], scalar=-0.125, in1=AT["nsinkb"][:nt, 4 * hg:4 * hg + 4],
                                                     op0=ALU.mult, op1=ALU.min), r=["st" + S, "sinkb"], w=["st" + S])
                yield
                for hh in range(4):
                    act(lambda e, hh=hh: e.activation(out=eb[:nt, hh, :nkeys], in_=ps_s[:nt, hh * 256:hh * 256 + nkeys], func=AF.Exp, scale=0.125,
                                                      bias=st[:nt, 4 + hh:5 + hh], accum_out=st[:nt, 8 + hh:9 + hh]),
                        r=[psn, "st" + S], w=["eb" + S, "stb" + S], mark=(hh == 3))
                dve(lambda e: e.tensor_tensor(out=st[:nt, 12:16], in0=AT["sinkb"][:nt, 4 * hg:4 * hg + 4], in1=st[:nt, 4:8], op=ALU.add),
                    r=["sinkb", "st" + S], w=["stc" + S])
                act(lambda e: e.activation(out=st[:nt, 12:16], in_=st[:nt, 12:16], func=AF.Exp), r=["stc" + S], w=["stc" + S])
                dve(lambda e: e.tensor_tensor(out=st[:nt, 16:20], in0=st[:nt, 8:12], in1=st[:nt, 12:16], op=ALU.add),
                    r=["stb" + S, "stc" + S], w=["std" + S])
                dve(lambda e: e.reciprocal(out=AT["rinv"][:nt, 4 * hg:4 * hg + 4], in_=st[:nt, 16:20]), r=["std" + S], w=["rinv"])
                yield
                for hh in range(4):
                    pe(lambda e, hh=hh: e.transpose(out=ps_e[:, (2 * hh) * 128:(2 * hh) * 128 + nt], in_=eb[:nt, hh, 0:128], identity=identb[:nt, :nt]),
                       r=["eb" + S, "identb"], w=[pen], mark=False)
                    pe(lambda e, hh=hh: e.transpose(out=ps_e[:nt, (2 * hh + 1) * 128:(2 * hh + 1) * 128 + nt], in_=eb[:nt, hh, 128:128 + nt],
                                                    identity=identb[:nt, :nt]),
                       r=["eb" + S, "identb"], w=[pen], mark=(hh == 3))
                yield
                dve(lambda e: e.tensor_copy(out=eT[:, :, :nt], in_=ps_e[:, 0:1024].rearrange("p (b t) -> p b t", t=128)[:, :, :nt]), r=[pen], w=["eT" + S])
                yield
                for hh in range(4):
                    h = 4 * hg + hh
                    pe(lambda e, h=h, hh=hh: e.matmul(psA[:nt, h * 64:(h + 1) * 64], lhsT=eT[:, 2 * hh, :nt], rhs=AT["vbs"][prev][:, kvh * 64:(kvh + 1) * 64],
                                                      start=True, stop=False, skip_group_check=True), r=["eT" + S, "vb%d" % prev], w=["psA_lo"], mark=False)
                    pe(lambda e, h=h, hh=hh: e.matmul(psA[:nt, h * 64:(h + 1) * 64], lhsT=eT[:nt, 2 * hh + 1, :nt],
                                                      rhs=AT["vbs"][cur][:nt, kvh * 64:(kvh + 1) * 64],
                                                      start=False, stop=True, skip_group_check=True), r=["eT" + S, "vb%d" % cur], w=["psA_lo"], mark=(hh == 3))
            return gen

        run_streams([hg_task(hg) for hg in range(4)], 2)

    def run_gen(g):
        for _ in g:
            pass

    def attn_layer(lb, l, src, dst):
        first = (lb == 0)
        layer_ln(l)
        if first:
            push_scope()
            alloc_attn()
            AT["WinB"] = T([128, 8, 2048], BF16)
            load_w(Wo, w_out_b[0], "Wo", 1024)
            load_w(AT["Wkv"], w_kv, "Wkv", 512)
            load_w(AT["WinB"], w_in_b[1], "WinB", 2048)
            AT["Win"] = Win; AT["Wo"] = Wo; AT["Winn"] = "Win"; AT["Won"] = "Wo"
        else:
            load_w(Wo, w_out_b[1], "Wo", 1024)
            AT["Win"] = AT["WinB"]; AT["Wo"] = Wo; AT["Winn"] = "WinB"; AT["Won"] = "Wo"
        dma(AT["sinkb"][:], sinks[lb].partition_broadcast(128), w=["sinkb"])
        dve(lambda e: e.tensor_scalar(out=AT["nsinkb"][:], in0=AT["sinkb"][:], scalar1=-1.0, scalar2=None, op0=ALU.mult), r=["sinkb"], w=["sinkb"])
        tiles = []
        for s in range(2):
            for j in range(16):
                cur = j % 2
                tiles.append(dict(s=s, j=j, nt=128, sap=src[0][s, 128 * j:128 * (j + 1), :], dap=dst[0][s, 128 * j:128 * (j + 1), :],
                                  cur=cur, prev=1 - cur, mask=(mask0 if j == 0 else maskG), mname=("mask0" if j == 0 else "maskG"),
                                  rope_j=j, t0=128 * j, kv_out=((ck_p[s], cv_p[s]) if (first and j == 15) else None)))
        tiles.append(dict(s=2, j=0, nt=64, sap=src[1][:, :], dap=dst[1][:, :], cur=0, prev=1, mask=None, mname=None, rope_j=16, t0=0,
                          kv_out=((ck_s[:, :], cv_s[:, :]) if first else None)))
        XT = AT["xt"]

        def setup_seq(t):
            if t["s"] < 2:
                pool(lambda e: e.memset(AT["kTs"][1][:], 0.0), w=["kT1"])
                pool(lambda e: e.memset(AT["vbs"][1][:], 0.0), w=["vb1"])
            else:
                dma(AT["ckc"][:], ck_in, w=["ckc"])
                act(lambda e: e.copy(out=AT["ckb"][:], in_=AT["ckc"][:]), r=["ckc"], w=["ckb"])
                kT_from(AT["ckb"], "ckb", 128, 1)
                dma(AT["ckc"][:], cv_in, w=["ckc"])
                dve(lambda e: e.tensor_copy(out=AT["vbs"][1][:], in_=AT["ckc"][:]), r=["ckc"], w=["vb1"])

        def pro_of(idx):
            t = tiles[idx]
            return attn_pro(first, t["s"], t["nt"], XT[idx % 2], "axt%d" % (idx % 2), t["cur"], t["rope_j"], t["t0"], t["kv_out"])

        dma(XT[0][:128, :], tiles[0]["sap"], w=["axt0"])
        setup_seq(tiles[0])
        run_gen(pro_of(0))
        for idx, t in enumerate(tiles):
            xs = idx % 2
            if idx + 1 < len(tiles):
                nx = tiles[idx + 1]
                dma(XT[1 - xs][:nx["nt"], :], nx["sap"], w=["axt%d" % (1 - xs)])
            attn_heads(t["nt"], t["prev"], t["cur"], t["mask"], t["mname"])
            if t["j"] == 0:
                load_gate(t["s"])
            epi = attn_epi(t["s"], t["nt"], XT[xs], "axt%d" % xs, t["dap"])
            if idx + 1 < len(tiles):
                nx = tiles[idx + 1]
                if nx["j"] == 0:
                    setup_seq(nx)
                run_streams([], 1, extra=[epi, pro_of(idx + 1)])
            else:
                run_gen(epi)
        if not first:
            pop_scope()
        else:
            P.barrier()

    def dbg_dump(name, src_ap, shape, regions):
        o = nc.dram_tensor("dbg_" + name, list(shape), src_ap.dtype if hasattr(src_ap, "dtype") else F32, kind="ExternalOutput").ap()
        dma(o, src_ap, r=regions, w=["dbg_" + name], slow=True)

    def stop_if(tag, dumps):
        if DEBUG["stop"] == tag:
            P.barrier()
            for name, ap, shape in dumps():
                dbg_dump(name, ap, shape, [])
            raise _Stop()

    DBG["stop_if"] = stop_if
    try:
        load_w(Win, w_in_a[0], "Win", 2048)
        load_w(Wg, w_glu[0], "Wg", 1024)
        load_w(Wo, w_out_a[0], "Wo", 1024)
        dma(bglur[0:1, :], b_glu[0:1, :], w=["bglur"], eng="pool")
        push_scope()
        ada_alloc(); gen_alloc()

        def gen_both():
            yield from gen_run(0)
            yield from gen_run(1)
        run_streams([], 1, extra=[ada_all(), gen_both()])
        pop_scope()
        stop_if("S", lambda: [("condP0", condPs[0][:], [128, 24, 4]), ("condP3", condPs[3][:], [128, 24, 4]), ("gate_d", gate_d4, [4, 3, D]),
                              ("TXd", TXd2[0], [64, 128, 512]), ("Gd", Gd2[0], [64, 64, 512]),
                              ("A1", A1s[0][:], [128, 32, 2]), ("A2", A2s[0][:], [128, 32, 2]),
                              ("TXd1", TXd2[1], [64, 128, 512])])
        s5_layer(0, 0, (x_p, x_s), (xa_p, xa_s))
        stop_if("L0", lambda: [("xa_p", xa_p, [2, SEQ, D]), ("xa_s", xa_s, [DEC, D])])
        s5_layer(1, 1, (xa_p, xa_s), (xb_p, xb_s))
        stop_if("L1", lambda: [("xb_p", xb_p, [2, SEQ, D]), ("xb_s", xb_s, [DEC, D])])
        attn_layer(0, 2, (xb_p, xb_s), (xa_p, xa_s))
        stop_if("L2", lambda: [("xa_p", xa_p, [2, SEQ, D]), ("xa_s", xa_s, [DEC, D])])
        attn_layer(1, 3, (xa_p, xa_s), (y_p, y_s))
    except _Stop:
        pass
    P.barrier(["sp"])
    P.replay()
    P.close()
    return nc


_NC = None


def kernel(**inputs):
    global _NC
    f = lambda a: np.ascontiguousarray(np.asarray(a, dtype=np.float32))
    inp = {k: f(v) for k, v in inputs.items()}
    if _NC is None:
        _NC = build_nc()
    nc = _NC
    wnames = ["w_ada", "b_ada", "ln_g", "ln_b", "w_in_a", "ssm_a_re", "ssm_a_im", "ssm_b_re", "ssm_b_im", "ssm_c_re",
              "ssm_c_im", "ssm_d", "ssm_log_dt", "w_glu", "b_glu", "w_out_a", "w_kv", "w_in_b", "attn_sinks", "w_out_b"]
    in_maps = []
    for c in range(NCORES):
        m = {k: inp[k] for k in wnames}
        m["x_p"] = f(inp["x_prompt"][2 * c:2 * c + 2])
        m["x_s"] = f(inp["x_sample"][c])
        m["st_in"] = f(inp["state_ssm"][:, c])
        m["ck_in"] = f(inp["cache_k"][c].reshape(128, 256))
        m["cv_in"] = f(inp["cache_v"][c].reshape(128, 256))
        m["c_all"] = f(np.stack([inp["c_prompt"][2 * c], inp["c_prompt"][2 * c + 1], inp["c_sample"][c]]))
        in_maps.append(m)
    res = run_bass_kernel_spmd(nc, in_maps, core_ids=list(range(NCORES)))
    R = res.results
    y_prompt = np.concatenate([r["y_p"] for r in R], axis=0)
    y_sample = np.stack([r["y_s"] for r in R], axis=0)
    ssm_pp = np.concatenate([r["ssm_p"] for r in R], axis=1)
    ckp = np.concatenate([r["ck_p"] for r in R], axis=0).reshape(16, 128, 4, 64)
    cvp = np.concatenate([r["cv_p"] for r in R], axis=0).reshape(16, 128, 4, 64)
    ssm_ss = np.stack([r["ssm_s"] for r in R], axis=1)
    cks = np.stack([r["ck_s"] for r in R], axis=0).reshape(8, 64, 4, 64)
    cvs = np.stack([r["cv_s"] for r in R], axis=0).reshape(8, 64, 4, 64)
    return (y_prompt.astype(np.float32), y_sample.astype(np.float32), ssm_pp.astype(np.float32), ckp.astype(np.float32),
            cvp.astype(np.float32), ssm_ss.astype(np.float32), cks.astype(np.float32), cvs.astype(np.float32))
```
